# Optimizing a Trainium2 kernel written in Bass

```python
import jax, jax.numpy as jnp
from jax import lax
import numpy as np

D_MODEL = 2048
BATCH = 4
SEQ = 2048
DEPTH = 4

HEAD_DIM = 128
GRID_W = 64
QBLK = 128
A_QBLK = 64
EPS = 1e-6
N_HEADS_TOTAL = D_MODEL // HEAD_DIM
A_HEADS = N_HEADS_TOTAL // 2
A_PATTERNS = ((128, 1), (512, 4), (2048, 16))
B_HEADS = N_HEADS_TOTAL - A_HEADS
NA_ROWS_MAX = 8
NA_COLS = 16
C_HEADS = N_HEADS_TOTAL // 2
C_KV_HEADS = C_HEADS // 4
C_WINDOW = 128
D_HEADS = N_HEADS_TOTAL - C_HEADS
D_KV_HEADS = D_HEADS // 4
ROPE_THETA = 10000.0

A_W = A_HEADS * HEAD_DIM
B_W = B_HEADS * HEAD_DIM
C_W = C_HEADS * HEAD_DIM
C_KV_W = C_KV_HEADS * HEAD_DIM
D_W = D_HEADS * HEAD_DIM
D_KV_W = D_KV_HEADS * HEAD_DIM
EVEN_SPLITS = (A_W, A_W, A_W, A_W, B_W, B_W, B_W, B_W)
ODD_SPLITS = (C_W, C_KV_W, C_KV_W, C_W, D_W, D_KV_W, D_KV_W, D_W)
EVEN_IN = sum(EVEN_SPLITS)
ODD_IN = sum(ODD_SPLITS)
EVEN_MIX = A_W + B_W
ODD_MIX = C_W + D_W

kernel_name = 'hybrid_dilated_neighbourhood_sink_axial_encoder'


def rms_norm(x, g):
    xf = x.astype(jnp.float32)
    y = xf * lax.rsqrt(jnp.mean(xf * xf, axis=-1, keepdims=True) + EPS)
    return (y * g.astype(jnp.float32)).astype(x.dtype)


def alibi_slopes(n):
    return 2.0 ** (-8.0 * jnp.arange(1, n + 1, dtype=jnp.float32) / n)


def _split(t, sizes):
    return jnp.split(t, np.cumsum(sizes)[:-1].tolist(), axis=-1)


def dilated_attention(q, k, v):
    b, s, h, hd = q.shape
    nb = s // A_QBLK
    slopes = alibi_slopes(h)
    scale = hd ** -0.5
    qb = q.reshape(b, nb, A_QBLK, h, hd).transpose(1, 0, 2, 3, 4)
    starts = jnp.arange(nb) * A_QBLK

    def one_block(args):
        q_blk, t0 = args
        t = t0 + jnp.arange(A_QBLK)
        outs, lses = [], []
        for window, dil in A_PATTERNS:
            n_side = (window // 2) // dil
            off = dil * jnp.arange(-n_side, n_side + 1)
            pos = t[:, None] + off[None, :]
            valid = (pos >= 0) & (pos < s)
            idx = jnp.clip(pos, 0, s - 1)
            k_g = k[:, idx]
            v_g = v[:, idx]
            logits = jnp.einsum('bqhd,bqkhd->bhqk', q_blk, k_g).astype(jnp.float32) * scale
            logits = logits - slopes[:, None, None] * jnp.abs(off).astype(jnp.float32)
            logits = jnp.where(valid[None, None], logits, -jnp.inf)
            m = jnp.max(logits, axis=-1, keepdims=True)
            p = jnp.exp(logits - m)
            den = jnp.sum(p, axis=-1, keepdims=True)
            o = jnp.einsum('bhqk,bqkhd->bhqd', p.astype(v.dtype), v_g).astype(jnp.float32) / den
            outs.append(o)
            lses.append(m + jnp.log(den))
        w = jax.nn.softmax(jnp.stack(lses), axis=0)
        o = jnp.sum(w * jnp.stack(outs), axis=0)
        return o.transpose(0, 2, 1, 3).astype(q.dtype)

    o = lax.map(one_block, (qb, starts))
    return o.transpose(1, 0, 2, 3, 4).reshape(b, s, h, hd)


def neighbourhood_attention(q, k, v, rpb):
    b, s, h, hd = q.shape
    rows = s // GRID_W
    kh = min(NA_ROWS_MAX, rows)
    kw = NA_COLS
    scale = hd ** -0.5
    qg = q.reshape(b, rows, GRID_W, h, hd).transpose(1, 0, 2, 3, 4)
    kg = k.reshape(b, rows, GRID_W, h, hd)
    vg = v.reshape(b, rows, GRID_W, h, hd)
    r = jnp.arange(rows)
    row_start = jnp.clip(r - kh // 2, 0, rows - kh)
    c = jnp.arange(GRID_W)
    col_idx = jnp.clip(c - kw // 2, 0, GRID_W - kw)[:, None] + jnp.arange(kw)[None, :]
    dcol = col_idx - c[:, None] + (NA_COLS - 1)

    def one_row(args):
        q_row, r0, rs = args
        k_rows = lax.dynamic_slice_in_dim(kg, rs, kh, axis=1)
        v_rows = lax.dynamic_slice_in_dim(vg, rs, kh, axis=1)
        k_n = k_rows[:, :, col_idx]
        v_n = v_rows[:, :, col_idx]
        drow = rs + jnp.arange(kh) - r0 + (NA_ROWS_MAX - 1)
        bias = rpb[:, drow[None, :, None], dcol[:, None, :]]
        logits = jnp.einsum('bchd,brckhd->bhcrk', q_row, k_n).astype(jnp.float32) * scale
        logits = logits + bias.astype(jnp.float32)[None]
        p = jax.nn.softmax(logits.reshape(b, h, GRID_W, kh * kw), axis=-1)
        p = p.reshape(b, h, GRID_W, kh, kw)
        o = jnp.einsum('bhcrk,brckhd->bchd', p.astype(v.dtype), v_n)
        return o.astype(q.dtype)

    o = lax.map(one_row, (qg, r, row_start))
    return o.transpose(1, 0, 2, 3, 4).reshape(b, s, h, hd)


def windowed_sink_attention(q, k, v, sinks):
    b, s, hq, hd = q.shape
    hkv = k.shape[2]
    g = hq // hkv
    nb = s // QBLK
    scale = hd ** -0.5
    slopes = alibi_slopes(hq).reshape(hkv, g)
    qb = q.reshape(b, nb, QBLK, hkv, g, hd)
    pad = ((0, 0), (QBLK, QBLK), (0, 0), (0, 0))
    kp = jnp.pad(k, pad).reshape(b, nb + 2, QBLK, hkv, hd)
    vp = jnp.pad(v, pad).reshape(b, nb + 2, QBLK, hkv, hd)
    kb = jnp.concatenate([kp[:, :-2], kp[:, 1:-1], kp[:, 2:]], axis=2)
    vb = jnp.concatenate([vp[:, :-2], vp[:, 1:-1], vp[:, 2:]], axis=2)
    qpos = jnp.arange(s).reshape(nb, QBLK)
    kpos = (jnp.arange(nb)[:, None] - 1) * QBLK + jnp.arange(3 * QBLK)[None, :]
    dist = jnp.abs(kpos[:, None, :] - qpos[:, :, None])
    valid = (dist <= C_WINDOW) & (kpos[:, None, :] >= 0) & (kpos[:, None, :] < s)
    logits = jnp.einsum('bnqhgd,bnshd->bhgnqs', qb, kb).astype(jnp.float32) * scale
    logits = logits - slopes[:, :, None, None, None] * dist.astype(jnp.float32)
    logits = jnp.where(valid, logits, -jnp.inf)
    sink = jnp.broadcast_to(sinks.astype(jnp.float32).reshape(hkv, g)[None, :, :, None, None, None],
                            logits.shape[:-1] + (1,))
    p = jax.nn.softmax(jnp.concatenate([logits, sink], axis=-1), axis=-1)[..., :-1]
    o = jnp.einsum('bhgnqs,bnshd->bnqhgd', p.astype(v.dtype), vb)
    return o.reshape(b, s, hq, hd).astype(q.dtype)


def _rope_1d(x, pos):
    hn = x.shape[-1] // 2
    inv = ROPE_THETA ** (-jnp.arange(hn, dtype=jnp.float32) / hn)
    ang = pos.astype(jnp.float32)[:, None] * inv[None, :]
    cos = jnp.cos(ang)[None, :, None, :]
    sin = jnp.sin(ang)[None, :, None, :]
    x1, x2 = x[..., :hn], x[..., hn:]
    return jnp.concatenate([x1 * cos - x2 * sin, x1 * sin + x2 * cos], axis=-1)


def axial_rope(x):
    s = x.shape[1]
    t = jnp.arange(s)
    half = x.shape[-1] // 2
    xf = x.astype(jnp.float32)
    y = jnp.concatenate([_rope_1d(xf[..., :half], t // GRID_W),
                         _rope_1d(xf[..., half:], t % GRID_W)], axis=-1)
    return y.astype(x.dtype)


def axial_rope_attention(q, k, v, gq, gk):
    b, s, hq, hd = q.shape
    hkv = k.shape[2]
    g = hq // hkv
    nb = s // QBLK
    scale = hd ** -0.5
    q = axial_rope(rms_norm(q, gq))
    k = axial_rope(rms_norm(k, gk))
    qb = q.reshape(b, nb, QBLK, hkv, g, hd).transpose(1, 0, 2, 3, 4, 5)

    def one_block(q_blk):
        logits = jnp.einsum('bqhgd,bshd->bhgqs', q_blk, k).astype(jnp.float32) * scale
        p = jax.nn.softmax(logits, axis=-1)
        return jnp.einsum('bhgqs,bshd->bqhgd', p.astype(v.dtype), v).astype(q.dtype)

    o = lax.map(one_block, qb)
    return o.transpose(1, 0, 2, 3, 4, 5).reshape(b, s, hq, hd)


def even_layer(y, w_in, w_out, rpb):
    b, s, _ = y.shape
    qa, ka, va, ga, qb, kb, vb, gb = _split(y @ w_in, EVEN_SPLITS)
    heads = lambda t, n: t.reshape(b, s, n, HEAD_DIM)
    oa = dilated_attention(heads(qa, A_HEADS), heads(ka, A_HEADS), heads(va, A_HEADS))
    ob = neighbourhood_attention(heads(qb, B_HEADS), heads(kb, B_HEADS), heads(vb, B_HEADS), rpb)
    mix = jnp.concatenate([oa.reshape(b, s, A_W) * jax.nn.silu(ga),
                           ob.reshape(b, s, B_W) * jax.nn.silu(gb)], axis=-1)
    return mix @ w_out


def odd_layer(y, w_in, w_out, sinks, gq, gk):
    b, s, _ = y.shape
    qc, kc, vc, gc, qd, kd, vd, gd = _split(y @ w_in, ODD_SPLITS)
    heads = lambda t, n: t.reshape(b, s, n, HEAD_DIM)
    oc = windowed_sink_attention(heads(qc, C_HEADS), heads(kc, C_KV_HEADS), heads(vc, C_KV_HEADS), sinks)
    od = axial_rope_attention(heads(qd, D_HEADS), heads(kd, D_KV_HEADS), heads(vd, D_KV_HEADS), gq, gk)
    mix = jnp.concatenate([oc.reshape(b, s, C_W) * jax.nn.silu(gc),
                           od.reshape(b, s, D_W) * jax.nn.silu(gd)], axis=-1)
    return mix @ w_out


def setup_inputs(seed: int = 0) -> dict:
    key = jax.random.key(seed)
    ks = jax.random.split(key, 12)
    n_even = (DEPTH + 1) // 2
    n_odd = DEPTH // 2
    nrm = jax.random.normal
    f32 = jnp.float32
    return {
        'x': nrm(ks[0], (BATCH, SEQ, D_MODEL), f32),
        'norm_g': 1.0 + 0.02 * nrm(ks[1], (DEPTH, D_MODEL), f32),
        'final_g': 1.0 + 0.02 * nrm(ks[2], (D_MODEL,), f32),
        'w_in_even': nrm(ks[3], (n_even, D_MODEL, EVEN_IN), f32) * D_MODEL ** -0.5,
        'w_out_even': nrm(ks[4], (n_even, EVEN_MIX, D_MODEL), f32) * EVEN_MIX ** -0.5,
        'rpb_b': 0.1 * nrm(ks[5], (n_even, B_HEADS, 2 * NA_ROWS_MAX - 1, 2 * NA_COLS - 1), f32),
        'w_in_odd': nrm(ks[6], (n_odd, D_MODEL, ODD_IN), f32) * D_MODEL ** -0.5,
        'w_out_odd': nrm(ks[7], (n_odd, ODD_MIX, D_MODEL), f32) * ODD_MIX ** -0.5,
        'sinks_c': 0.5 * nrm(ks[8], (n_odd, C_HEADS), f32),
        'qnorm_d': 1.0 + 0.02 * nrm(ks[9], (n_odd, HEAD_DIM), f32),
        'knorm_d': 1.0 + 0.02 * nrm(ks[10], (n_odd, HEAD_DIM), f32),
    }


def reference(x, norm_g, final_g, w_in_even, w_out_even, rpb_b, w_in_odd, w_out_odd,
              sinks_c, qnorm_d, knorm_d):
    h = x
    for layer in range(DEPTH):
        y = rms_norm(h, norm_g[layer])
        i = layer // 2
        if layer % 2 == 0:
            h = h + even_layer(y, w_in_even[i], w_out_even[i], rpb_b[i])
        else:
            h = h + odd_layer(y, w_in_odd[i], w_out_odd[i], sinks_c[i], qnorm_d[i], knorm_d[i])
    return rms_norm(h, final_g)
```

```python
import math
import os
DBG = os.environ.get('KDBG', '')
from contextlib import ExitStack

import numpy as np
import concourse.bass as bass
import concourse.mybir as mybir
from concourse.bass_utils import run_bass_kernel_spmd

F32 = mybir.dt.float32
BF16 = mybir.dt.bfloat16
AF = mybir.ActivationFunctionType
ALU = mybir.AluOpType
AX = mybir.AxisListType

D_MODEL = 2048
SEQ = 2048
BATCH = 4
DEPTH = 4
HD = 128
NCH = D_MODEL // 128
NTB = SEQ // 128
NTT = SEQ // 512
EPS = 1e-6
SCALE = HD ** -0.5
NEG = -30000.0

OFF_A, W_A = 1408, 2944
OFF_C, W_C = 512, 1152
REL_MIN, W_B = -10, 22 * 64
W_STRIP = W_A


def _alibi_slopes(n):
    return (2.0 ** (-8.0 * np.arange(1, n + 1, dtype=np.float32) / n)).astype(np.float32)


def _toeplitz_strip(f, off, width):
    k = np.arange(128)[:, None]
    j = np.arange(width)[None, :]
    return f(k - j + off)


def make_bias_a():
    slopes = _alibi_slopes(8)
    out = np.empty((8, 128, W_A), np.float32)
    for h in range(8):
        def f(d, h=h):
            ad = np.abs(d)
            mult = (ad <= 64).astype(np.float32)
            mult += ((d % 4 == 0) & (ad <= 256)).astype(np.float32)
            mult += ((d % 16 == 0) & (ad <= 1024)).astype(np.float32)
            val = -slopes[h] * ad.astype(np.float32) + np.log(np.maximum(mult, 1.0)).astype(np.float32)
            return np.where(mult > 0, val, NEG).astype(np.float32)
        out[h] = _toeplitz_strip(f, OFF_A, W_A)
    return out


def make_bias_c():
    slopes = _alibi_slopes(8)
    out = np.empty((8, 128, W_C), np.float32)
    for h in range(8):
        def f(d, h=h):
            ad = np.abs(d)
            val = -slopes[h] * ad.astype(np.float32)
            return np.where(ad <= 128, val, NEG).astype(np.float32)
        out[h] = _toeplitz_strip(f, OFF_C, W_C)
    return out


def make_strip_b(rpb):
    n = rpb.shape[0]
    p = np.arange(128)
    b = (p // 64)[:, None]
    cp = (p % 64)[:, None]
    col = np.arange(W_B)[None, :]
    rel = col // 64 + REL_MIN
    c = col % 64
    cs = np.clip(c - 8, 0, 48)
    colvalid = (cp >= cs) & (cp < cs + 16)
    dcol = np.clip(cp - c + 15, 0, 30)
    dr = b - rel + 7
    drc = np.clip(dr, 0, 14)
    rowvalid_i = ((b - rel) >= -4) & ((b - rel) <= 3)
    rowvalid_u = (dr >= 0) & (dr <= 14)
    out = np.empty((n, 8, 2, 128, W_B), np.float32)
    for i in range(n):
        for h in range(8):
            g = rpb[i, h][drc, dcol]
            out[i, h, 0] = np.where(colvalid & rowvalid_i, g, NEG)
            out[i, h, 1] = np.where(colvalid & rowvalid_u, g, NEG)
    return out


def make_rope():
    t = np.arange(SEQ)
    hn = 32
    inv = (10000.0 ** (-np.arange(hn, dtype=np.float32) / hn)).astype(np.float32)
    d = np.arange(128)
    pos = np.where(d[:, None] < 64, (t // 64)[None, :], (t % 64)[None, :]).astype(np.float32)
    ang = pos * inv[d % 32][:, None]
    return np.stack([np.cos(ang), np.sin(ang)]).astype(np.float32)


def make_rperm():
    r = np.zeros((128, 128), np.float32)
    for m in range(128):
        if (m % 64) < 32:
            r[m + 32, m] = -1.0
        else:
            r[m - 32, m] = 1.0
    return r


class Sem:
    def __init__(self, h, name):
        self.h = h
        self.name = name


class Buf:
    def __init__(self, name):
        self.name = name
        self.w = {}
        self.r = {}
        self.dsem = None
        self.dcount = 0


def _merge(dst, src):
    for s, v in src.items():
        if dst.get(s, 0) < v:
            dst[s] = v


class Builder:
    ENG = ("pe", "act", "dve", "pool", "sp")

    def __init__(self, nc, stack):
        self.nc = nc
        self.stack = stack
        self.q = {e: [] for e in self.ENG}
        self.esem = {}
        self.ecount = {}
        for e in ("pe", "act", "dve", "pool"):
            self.esem[e] = self.new_sem("es_" + e)
            self.ecount[e] = 0
        self.waited = {e: {} for e in self.ENG}
        self.dsems = {}

    def new_sem(self, name):
        return Sem(self.stack.enter_context(self.nc.semaphore(name)), name)

    def _emit_waits(self, eng, deps):
        wd = self.waited[eng]
        for s, v in deps.items():
            if wd.get(s, 0) < v:
                wd[s] = v
                self.q[eng].append(("wait", s.h, v))

    def _deps(self, reads, writes, accum):
        deps = {}
        for b in reads:
            _merge(deps, b.w)
        for b in writes:
            _merge(deps, b.r)
            _merge(deps, b.w)
        for b in accum:
            _merge(deps, b.r)
        return deps

    def _commit(self, tok, reads, writes, accum):
        for b in reads:
            _merge(b.r, tok)
        for b in writes:
            b.w = dict(tok)
            b.r = {}
        for b in accum:
            _merge(b.w, tok)
            b.r = {}

    def op(self, eng, fns, reads=(), writes=(), accum=()):
        if callable(fns):
            fns = [fns]
        deps = self._deps(reads, writes, accum)
        self._emit_waits(eng, deps)
        for f in fns[:-1]:
            self.q[eng].append(("ins", f, None, 0))
        s = self.esem[eng]
        self.ecount[eng] += 1
        self.q[eng].append(("ins", fns[-1], s.h, 1))
        tok = {s: self.ecount[eng]}
        self.waited[eng][s] = max(self.waited[eng].get(s, 0), 0)
        self._commit(tok, reads, writes, accum)

    def dma(self, eng, fn, sbuf, reads=(), writes=(), accum=()):
        if sbuf.name not in self.dsems:
            self.dsems[sbuf.name] = [self.new_sem("d_" + sbuf.name), 0]
        ent = self.dsems[sbuf.name]
        deps = self._deps(reads, writes, accum)
        self._emit_waits(eng, deps)
        ent[1] += 16
        self.q[eng].append(("ins", fn, ent[0].h, 16))
        tok = {ent[0]: ent[1]}
        self._commit(tok, reads, writes, accum)

    def barrier(self):
        toks = {self.esem[e]: self.ecount[e] for e in self.esem if self.ecount[e] > 0}
        for name, (sem, cnt) in self.dsems.items():
            if cnt > 0:
                toks[sem] = cnt
        for e in self.ENG:
            self._emit_waits(e, toks)

    def final_wait(self, eng, bufs):
        deps = {}
        for b in bufs:
            _merge(deps, b.w)
            _merge(deps, b.r)
        self._emit_waits(eng, deps)

    def flush(self, block):
        nc = self.nc

        def run(e, items):
            for it in items:
                if it[0] == "wait":
                    e.wait_ge(it[1], it[2])
                else:
                    ins = it[1](e)
                    if it[2] is not None:
                        ins.then_inc(it[2], it[3])

        q = self.q

        @block.tensor
        def _(e):
            run(e, q["pe"])

        @block.scalar
        def _(e):
            run(e, q["act"])

        @block.vector
        def _(e):
            run(e, q["dve"])

        @block.gpsimd
        def _(e):
            run(e, q["pool"])

        @block.sync
        def _(e):
            run(e, q["sp"])


def _segments(layer_is_even):
    segs = []
    if layer_is_even:
        for base, gh in ((0, 0), (4096, 8)):
            segs += [("q", gh + i) for i in range(8)]
            segs += [("k", gh + i) for i in range(8)]
            segs += [("v", gh + i) for i in range(8)]
            segs += [("g", gh + i) for i in range(8)]
    else:
        segs += [("q", i) for i in range(8)]
        segs += [("k", i) for i in range(2)]
        segs += [("v", i) for i in range(2)]
        segs += [("g", i) for i in range(8)]
        segs += [("qd", 8 + i) for i in range(8)]
        segs += [("kd", 2 + i) for i in range(2)]
        segs += [("v", 2 + i) for i in range(2)]
        segs += [("g", 8 + i) for i in range(8)]
    return segs


def _plan_a():
    plan = []
    for qt in range(NTT):
        l = []
        for kb in range(NTB):
            e = qt * 512 - kb * 128
            if -1535 <= e <= 1151:
                l.append((kb, [(0, 512, 0, e + OFF_A)]))
        plan.append(l)
    return plan


def _plan_c():
    plan = []
    for qt in range(NTT):
        l = []
        for kb in range(max(0, 4 * qt - 1), min(15, 4 * qt + 4) + 1):
            e = qt * 512 - kb * 128
            l.append((kb, [(0, 512, 0, e + OFF_C)]))
        plan.append(l)
    return plan


def _plan_b():
    plan = []
    for qt in range(NTT):
        l = []
        for j in range(max(0, 4 * qt - 2), min(15, 4 * qt + 5) + 1):
            c0 = (8 * qt - 2 * j - REL_MIN) * 64
            if qt == 0:
                which = 1 if (2 * j + 1) <= 7 else 0
                segs = [(0, 256, which, c0), (256, 512, 0, c0 + 256)]
            elif qt == NTT - 1:
                which = 1 if (2 * j) >= 24 else 0
                segs = [(0, 320, 0, c0), (320, 512, which, c0 + 320)]
            else:
                segs = [(0, 512, 0, c0)]
            l.append((j, segs))
        plan.append(l)
    return plan


def _plan_d():
    return [[(kb, None) for kb in range(NTB)] for _ in range(NTT)]


def build_program(n_layers=DEPTH, layers=None):
    nc = bass.Bass("TRN2", target_bir_lowering=False)

    def din(name, shape, dt=F32):
        return nc.dram_tensor(name, list(shape), dt, kind="ExternalInput").ap()

    x = din("x", [SEQ, D_MODEL])
    gam = din("gam", [DEPTH + 1, 128, D_MODEL])
    w_in_even = din("w_in_even", [2, D_MODEL, 8192])
    w_out_even = din("w_out_even", [2, D_MODEL, D_MODEL])
    w_in_odd = din("w_in_odd", [2, D_MODEL, 5120])
    w_out_odd = din("w_out_odd", [2, D_MODEL, D_MODEL])
    sinks = din("sinks", [2, 128, 128])
    qn = din("qn", [2, 128, 128])
    kn = din("kn", [2, 128, 128])
    bias_a = din("bias_a", [8, 128, W_A])
    bias_c = din("bias_c", [8, 128, W_C])
    strip_b = din("strip_b", [2, 8, 2, 128, W_B])
    rope = din("rope", [2, 128, SEQ])
    rperm_d = din("rperm", [128, 128])
    ident_d = din("ident", [128, 128])
    out = nc.dram_tensor("out", [SEQ, D_MODEL], F32, kind="ExternalOutput").ap()

    hbuf = nc.dram_tensor("hbuf", [SEQ, D_MODEL], F32).ap()
    qs = nc.dram_tensor("qs", [16, 128, SEQ], BF16).ap()
    ks = nc.dram_tensor("ks", [16, 128, SEQ], BF16).ap()
    gs = nc.dram_tensor("gs", [16, 128, SEQ], BF16).ap()
    vs = nc.dram_tensor("vs", [SEQ, 16 * 128], BF16).ap()
    NWG = 2 * (8 + 2) + 2 * (5 + 2)
    wbf = nc.dram_tensor("wbf", [NWG, D_MODEL, 1024], BF16).ap()

    with ExitStack() as stack:
        B = Builder(nc, stack)

        uid = [0]

        def uname(name):
            uid[0] += 1
            return f"{name}_u{uid[0]}"

        def sb(name, shape, dt):
            return stack.enter_context(nc.sbuf_tensor(uname(name), list(shape), dt))

        def ps(name, shape, dt):
            return stack.enter_context(nc.psum_tensor(uname(name), list(shape), dt))

        ybuf = sb("ybuf", [128, NCH, SEQ], BF16)
        ybufs = [Buf(f"ybuf{t}") for t in range(NTT)]
        NW = 2
        GW = 8
        wbuf = [sb(f"wbuf{i}", [128, NCH, GW * 128], BF16) for i in range(NW)]
        wbufs = [Buf(f"wbuf{i}") for i in range(NW)]
        ones_bf = sb("ones_bf", [128, 128], BF16)
        ident = sb("ident", [128, 128], BF16)
        rperm = sb("rperm_s", [128, 128], BF16)
        cbuf = Buf("consts")
        eps_t = sb("eps_t", [128, 1], F32)

        NPS = 8
        psf = [ps(f"psf{i}", [128, 512], F32) for i in range(NPS)]
        psfs = [Buf(f"psf{i}") for i in range(NPS)]
        pst = [psf[6].bitcast(BF16), psf[7].bitcast(BF16)]
        psts = [psfs[6], psfs[7]]

        hsrc_b = Buf("hres")
        qs_b = [Buf(f"qs{i}") for i in range(16)]
        ks_b = [Buf(f"ks{i}") for i in range(16)]
        gs_b = [Buf(f"gs{i}") for i in range(16)]
        vs_b = [Buf(f"vs{i}") for i in range(16)]
        out_b = Buf("out")

        B.op("dve", [lambda e: e.memset(ones_bf[:], 1.0),
                     lambda e: e.memset(eps_t[:], EPS)], writes=[cbuf])
        rp_b = Buf("rperm")
        B.dma("pool", lambda e: e.dma_start(out=rperm[:], in_=rperm_d), rp_b, writes=[rp_b])

        state = {"ps": 0}

        def next_ps():
            i = state["ps"] % NPS
            state["ps"] += 1
            return i

        def phase_scope():
            return ExitStack()

        wcb = {}
        wc_n = [0]

        def wcast(key, w, g):
            gi = wc_n[0]
            wc_n[0] += 1
            b = Buf(f"wc{gi}")
            wcb[key] = (gi, b)
            B.dma("pool", lambda e: e.dma_start(out=wbf[gi], in_=w[:, g * 1024:(g + 1) * 1024]), b, writes=[b])

        def wload(key, wi):
            gi, b = wcb[key]
            B.dma("sp", lambda e: e.dma_start(out=wbuf[wi][:], in_=wbf[gi].rearrange("(cc p) f -> p cc f", p=128)),
                  wbufs[wi], reads=[b], writes=[wbufs[wi]])

        def rmsnorm_phase(src, src_b, gidx, final):
            with phase_scope() as sc:
                def sbp(name, shape, dt):
                    return sc.enter_context(nc.sbuf_tensor(uname(name), list(shape), dt))
                NHB = 3
                hb = [sbp(f"hb{i}", [128, D_MODEL], F32) for i in range(NHB)]
                hbs = [Buf(f"hb{i}") for i in range(NHB)]
                sq = sbp("sq", [128, D_MODEL], F32)
                sqs = Buf("sq")
                gm = sbp("gm", [128, D_MODEL], F32)
                gms = Buf("gm")
                st = [sbp(f"st{i}", [128, 4], F32) for i in range(NHB)]
                sts = [Buf(f"st{i}") for i in range(NHB)]
                if final:
                    yb = [sbp(f"yo{i}", [128, D_MODEL], F32) for i in range(2)]
                else:
                    yb = [sbp(f"yb{i}", [128, D_MODEL], BF16) for i in range(2)]
                ybs = [Buf(f"yb{i}") for i in range(2)]
                B.dma("sp", lambda e: e.dma_start(out=gm[:], in_=gam[gidx]), gms, writes=[gms])
                def stage_a1(tb):
                    i = tb % NHB
                    B.dma("sp", lambda e, i=i, tb=tb: e.dma_start(out=hb[i][:], in_=src[tb * 128:(tb + 1) * 128, :]),
                          hbs[i], reads=[src_b], writes=[hbs[i]])
                    B.op("act", lambda e, i=i: e.activation(out=sq[:], in_=hb[i][:], func=AF.Square),
                         reads=[hbs[i]], writes=[sqs])
                    B.op("dve", lambda e, i=i: e.reduce_sum(out=st[i][:, 0:1], in_=sq[:], axis=AX.X),
                         reads=[sqs], writes=[sts[i]])

                def stage_a2(tb):
                    i = tb % NHB
                    j = tb % 2
                    B.op("act", lambda e, i=i: e.activation(out=st[i][:, 1:2], in_=st[i][:, 0:1], func=AF.Sqrt,
                                                            scale=1.0 / D_MODEL, bias=eps_t[:, 0:1]),
                         reads=[cbuf], writes=[sts[i]])
                    B.op("dve", lambda e, i=i: e.reciprocal(out=st[i][:, 2:3], in_=st[i][:, 1:2]), writes=[sts[i]])
                    B.op("dve", lambda e, i=i, j=j: e.scalar_tensor_tensor(out=yb[j][:], in0=hb[i][:], scalar=st[i][:, 2:3],
                                                                           in1=gm[:], op0=ALU.mult, op1=ALU.mult),
                         reads=[hbs[i], sts[i], gms], writes=[ybs[j]])

                def stage_b(tb):
                    i = tb % 2
                    if final:
                        B.dma("sp", lambda e, i=i, tb=tb: e.dma_start(out=out[tb * 128:(tb + 1) * 128, :], in_=yb[i][:]),
                              ybs[i], reads=[ybs[i]], accum=[out_b])
                    else:
                        tt = tb // 4
                        for c4 in range(4):
                            pi = (tb * 4 + c4) % 2
                            fns = []
                            for k in range(4):
                                cc = c4 * 4 + k
                                fns.append(lambda e, pi=pi, k=k, cc=cc, i=i: e.transpose(
                                    pst[pi][:, k * 128:(k + 1) * 128], yb[i][:, cc * 128:(cc + 1) * 128], ident[:]))
                            B.op("pe", fns, reads=[ybs[i], idb_], writes=[psts[pi]])
                            dst = ybuf[:, c4 * 4:(c4 + 1) * 4, tb * 128:(tb + 1) * 128]
                            srcp = pst[pi][:, 0:512].rearrange("p (k t) -> p k t", k=4)
                            B.op("act", lambda e, dst=dst, srcp=srcp: e.activation(out=dst, in_=srcp, func=AF.Copy),
                                 reads=[psts[pi]], accum=[ybufs[tt]])

                for tb in range(NTB + 1):
                    if tb < NTB:
                        stage_a1(tb)
                    if tb >= 1:
                        stage_b(tb - 1)
                    if tb < NTB:
                        stage_a2(tb)

        def inproj_phase(w, is_even, li, layer_id):
            segs = _segments(is_even)
            ngroups = len(segs) // GW
            with phase_scope() as sc:
                def sbp(name, shape, dt):
                    return sc.enter_context(nc.sbuf_tensor(uname(name), list(shape), dt))
                NO = 4
                ot = [sbp(f"ot{i}", [128, 512], BF16) for i in range(NO)]
                ots = [Buf(f"ot{i}") for i in range(NO)]
                oi = [0]
                if not is_even:
                    cs_t = sbp("cs_t", [128, 2, SEQ], F32)
                    cs_b = Buf("cs_t")
                    B.dma("sp", lambda e: e.dma_start(out=cs_t[:], in_=rope.rearrange("a p t -> p a t")), cs_b, writes=[cs_b])
                    gq = sbp("gq", [128, 2, 128], F32)
                    gq_b = Buf("gq")
                    B.dma("sp", lambda e: e.dma_start(out=gq[:, 0, :], in_=qn[li]), gq_b, accum=[gq_b])
                    B.dma("sp", lambda e: e.dma_start(out=gq[:, 1, :], in_=kn[li]), gq_b, accum=[gq_b])
                    tmp = [sbp(f"tmp{i}", [128, 512], F32) for i in range(5)]
                    tmps = [Buf(f"tmp{i}") for i in range(5)]
                    sqh = sbp("sqh", [128, 512], BF16)
                    xhi = sbp("xhi", [128, 512], BF16)
                    xlo = sbp("xlo", [128, 512], BF16)
                    xhib, xlob, sqhb = Buf("xhi"), Buf("xlo"), Buf("sqh")

                wload(("in", layer_id, 0), 0)
                for g in range(ngroups):
                    wi = g % NW
                    if g + 1 < ngroups:
                        wload(("in", layer_id, g + 1), (g + 1) % NW)
                    gsegs = segs[g * GW:(g + 1) * GW]
                    k = 0
                    while k < GW:
                        typ, idx = gsegs[k]
                        if 'noqd' in DBG and typ in ('qd', 'kd'):
                            typ = typ[0]
                        if typ == "v":
                            n = 1
                            while n < 4 and k + n < GW and gsegs[k + n][0] == "v" and gsegs[k + n][1] == idx + n:
                                n += 1
                            ncol = 128 * n
                            for tb in range(NTB):
                                p = next_ps()
                                fns = [lambda e, p=p, cc=cc, tb=tb, k=k, ncol=ncol, wi=wi: e.matmul(
                                    psf[p][:, 0:ncol], lhsT=ybuf[:, cc, tb * 128:(tb + 1) * 128],
                                    rhs=wbuf[wi][:, cc, k * 128:k * 128 + ncol], start=(cc == 0), stop=(cc == NCH - 1))
                                    for cc in range(NCH)]
                                B.op("pe", fns, reads=[ybufs[tb // 4], wbufs[wi]], writes=[psfs[p]])
                                o = oi[0] % NO
                                oi[0] += 1
                                B.op("dve", lambda e, o=o, p=p, ncol=ncol: e.tensor_copy(out=ot[o][:, 0:ncol], in_=psf[p][:, 0:ncol]),
                                     reads=[psfs[p]], writes=[ots[o]])
                                B.dma("sp", lambda e, o=o, tb=tb, idx=idx, ncol=ncol: e.dma_start(
                                    out=vs[tb * 128:(tb + 1) * 128, idx * 128:idx * 128 + ncol], in_=ot[o][:, 0:ncol]),
                                    ots[o], reads=[ots[o]], accum=[vs_b[idx + j] for j in range(n)])
                            k += n
                            continue
                        for tt in range(NTT):
                            p = next_ps()
                            fns = [lambda e, p=p, cc=cc, tt=tt, k=k, wi=wi: e.matmul(
                                psf[p][:], lhsT=wbuf[wi][:, cc, k * 128:(k + 1) * 128],
                                rhs=ybuf[:, cc, tt * 512:(tt + 1) * 512], start=(cc == 0), stop=(cc == NCH - 1))
                                for cc in range(NCH)]
                            B.op("pe", fns, reads=[ybufs[tt], wbufs[wi]], writes=[psfs[p]])
                            o = oi[0] % NO
                            oi[0] += 1
                            if typ == "q":
                                B.op("dve", lambda e, o=o, p=p: e.tensor_scalar(out=ot[o][:], in0=psf[p][:], scalar1=SCALE,
                                                                               scalar2=None, op0=ALU.mult),
                                     reads=[psfs[p]], writes=[ots[o]])
                                dst, dstb = qs, qs_b
                            elif typ == "k":
                                B.op("act", lambda e, o=o, p=p: e.activation(out=ot[o][:], in_=psf[p][:], func=AF.Copy),
                                     reads=[psfs[p]], writes=[ots[o]])
                                dst, dstb = ks, ks_b
                            elif typ == "g":
                                B.op("act", lambda e, o=o, p=p: e.activation(out=ot[o][:], in_=psf[p][:], func=AF.Silu),
                                     reads=[psfs[p]], writes=[ots[o]])
                                dst, dstb = gs, gs_b
                            else:
                                isq = typ == "qd"
                                gcol = 0 if isq else 1
                                osc = SCALE if isq else 1.0
                                xg, sqx, t1, t2, rs = tmp
                                xgb, sqxb, t1b, t2b, rsb = tmps
                                sqb = sqhb
                                B.op("dve", lambda e, p=p, gcol=gcol: e.tensor_scalar(
                                    out=xg[:], in0=psf[p][:], scalar1=gq[:, gcol, 0:1], scalar2=None, op0=ALU.mult),
                                    reads=[psfs[p], gq_b], writes=[xgb])
                                B.op("act", lambda e, p=p: e.activation(out=sqh[:], in_=psf[p][:], func=AF.Square),
                                     reads=[psfs[p], xgb], writes=[sqb])
                                B.op("dve", lambda e: e.tensor_copy(out=xhi[:], in_=xg[:]), reads=[xgb], writes=[xhib])
                                B.op("dve", lambda e: e.tensor_tensor(out=xlo[:], in0=xg[:], in1=xhi[:], op=ALU.subtract),
                                     reads=[xgb, xhib], writes=[xlob])
                                p2 = next_ps()
                                B.op("pe", lambda e, p2=p2: e.matmul(psf[p2][:], lhsT=ones_bf[:], rhs=sqh[:], start=True, stop=True),
                                     reads=[sqb, cbuf], writes=[psfs[p2]])
                                p3 = next_ps()
                                B.op("pe", [lambda e, p3=p3: e.matmul(psf[p3][:], lhsT=rperm[:], rhs=xhi[:], start=True, stop=False),
                                            lambda e, p3=p3: e.matmul(psf[p3][:], lhsT=rperm[:], rhs=xlo[:], start=False, stop=True)],
                                     reads=[xhib, xlob, rp_b], writes=[psfs[p3]])
                                B.op("dve", lambda e, p2=p2: e.tensor_scalar(out=t2[:], in0=psf[p2][:], scalar1=1.0 / HD,
                                                                             scalar2=EPS, op0=ALU.mult, op1=ALU.add),
                                     reads=[psfs[p2]], writes=[t2b])
                                B.op("act", lambda e: e.activation(out=rs[:], in_=t2[:], func=AF.Ln),
                                     reads=[t2b], writes=[rsb])
                                B.op("act", lambda e: e.activation(out=sqx[:], in_=rs[:], func=AF.Exp, scale=-0.5),
                                     reads=[rsb], writes=[sqxb])
                                B.op("dve", lambda e, tt=tt: e.tensor_tensor(out=t1[:], in0=xg[:], in1=cs_t[:, 0, tt * 512:(tt + 1) * 512],
                                                                            op=ALU.mult),
                                     reads=[xgb, cs_b], writes=[t1b])
                                B.op("dve", lambda e, tt=tt, p3=p3: e.tensor_tensor(out=t2[:], in0=psf[p3][:],
                                                                                   in1=cs_t[:, 1, tt * 512:(tt + 1) * 512], op=ALU.mult),
                                     reads=[psfs[p3], cs_b], writes=[t2b])
                                B.op("dve", lambda e: e.tensor_tensor(out=rs[:], in0=t1[:], in1=t2[:], op=ALU.add),
                                     reads=[t1b, t2b], writes=[rsb])
                                B.op("dve", lambda e, o=o, osc=osc: e.scalar_tensor_tensor(
                                    out=ot[o][:], in0=rs[:], scalar=osc, in1=sqx[:], op0=ALU.mult, op1=ALU.mult),
                                    reads=[rsb, sqxb], writes=[ots[o]])
                                dst, dstb = (qs, qs_b) if isq else (ks, ks_b)
                            B.dma("sp", lambda e, o=o, dst=dst, idx=idx, tt=tt: e.dma_start(
                                out=dst[idx, :, tt * 512:(tt + 1) * 512], in_=ot[o][:]),
                                ots[o], reads=[ots[o]], accum=[dstb[idx]])
                        k += 1

        def attention_phase(is_even, li):
            with phase_scope() as sc:
                def sbp(name, shape, dt):
                    return sc.enter_context(nc.sbuf_tensor(uname(name), list(shape), dt))
                qt_ = [sbp(f"qh{i}", [128, SEQ], BF16) for i in range(2)]
                qtb = [Buf(f"qh{i}") for i in range(2)]
                gt_ = [sbp(f"gh{i}", [128, SEQ], BF16) for i in range(2)]
                gtb = [Buf(f"gh{i}") for i in range(2)]
                kt_ = [sbp(f"kh{i}", [128, SEQ], BF16) for i in range(2)]
                ktb = [Buf(f"kh{i}") for i in range(2)]
                vt_ = [sbp(f"vh{i}", [128, NTB, 128], BF16) for i in range(2)]
                vtb = [Buf(f"vh{i}") for i in range(2)]
                WS = W_A if is_even else W_C
                sS1 = sbp("sS0", [128, WS], F32)
                sS = [sS1, sS1]
                sSb1 = Buf("sS0")
                sSb = [sSb1, sSb1]
                eS = [sbp(f"eS{i}", [128, WS], BF16) for i in range(2)]
                sAb = [Buf(f"eS{i}") for i in range(2)]
                sBb = sAb
                sA = [t[:].rearrange("p (a w) -> p a w", a=1) for t in eS]
                if is_even:
                    sBst = [t[:, 0:2 * W_B].rearrange("p (a w) -> p a w", a=2) for t in sS]
                    sB = [t[:, 0:2 * W_B].rearrange("p (a w) -> p a w", a=2) for t in eS]
                LOOK = 3
                NT = 4
                tt_ = [sbp(f"tT{i}", [128, 512], BF16) for i in range(NT)]
                ttb = [Buf(f"tT{i}") for i in range(NT)]
                NP = LOOK + 2
                pt_ = [sbp(f"pT{i}", [128, 512], BF16) for i in range(NP)]
                ptb = [Buf(f"pT{i}") for i in range(NP)]
                rd_ = [sbp(f"rd{i}", [128, 512], F32) for i in range(2)]
                rdb = [Buf(f"rd{i}") for i in range(2)]
                if not is_even:
                    of_ = [sbp(f"of{i}", [128, 512], F32) for i in range(2)]
                    ofb = [Buf(f"of{i}") for i in range(2)]
                gp_ = [sbp(f"gp{i}", [128, 512], F32) for i in range(2)]
                gpb = [Buf(f"gp{i}") for i in range(2)]
                lt1 = sbp("lt0", [128, 512], F32)
                lt_ = [lt1, lt1]
                ltb1 = Buf("lt0")
                ltb = [ltb1, ltb1]
                if not is_even:
                    sk0 = sbp("sk0", [128, 128], F32)
                    sk = sbp("sk", [128, 128], F32)
                    sk0b = Buf("sk0")
                    skb = Buf("sk")
                    B.dma("sp", lambda e: e.dma_start(out=sk0[:], in_=sinks[li]), sk0b, writes=[sk0b])
                    B.op("act", lambda e: e.activation(out=sk[:], in_=sk0[:], func=AF.Exp), reads=[sk0b], writes=[skb])

                S_IDX = [0, 1, 2, 3]
                O_IDX = [4, 5]
                D_IDX = [6, 7]

                tiles = []
                for gh in range(16):
                    if is_even:
                        kind = "A" if gh < 8 else "B"
                        kvh = gh
                    else:
                        kind = "C" if gh < 8 else "D"
                        kvh = gh // 4 if gh < 8 else 2 + (gh - 8) // 4
                    plan = {"A": _plan_a, "B": _plan_b, "C": _plan_c, "D": _plan_d}[kind]()
                    for qt in range(NTT):
                        l = plan[qt]
                        for n, (kb, segs) in enumerate(l):
                            tiles.append((gh, kind, kvh, qt, kb, segs, n == 0, n == len(l) - 1))

                hinfo = []
                ksl_, prev_kvh = -1, -1
                for gh in range(16):
                    if is_even:
                        kind = "A" if gh < 8 else "B"
                        kvh = gh
                    else:
                        kind = "C" if gh < 8 else "D"
                        kvh = gh // 4 if gh < 8 else 2 + (gh - 8) // 4
                    newkv = kvh != prev_kvh
                    if newkv:
                        ksl_ = (ksl_ + 1) % 2
                        prev_kvh = kvh
                    hinfo.append((kind, kvh, ksl_, newkv))
                hcount = [0] * 16
                for t in tiles:
                    hcount[t[0]] += 1

                def issue_loads(gh):
                    kind, kvh, ksl, newkv = hinfo[gh]
                    hs = gh % 2
                    B.dma("sp", lambda e: e.dma_start(out=qt_[hs][:], in_=qs[gh]), qtb[hs],
                          reads=[qs_b[gh]], writes=[qtb[hs]])
                    if newkv:
                        B.dma("sp", lambda e: e.dma_start(out=kt_[ksl][:], in_=ks[kvh]), ktb[ksl],
                              reads=[ks_b[kvh]], writes=[ktb[ksl]])
                        B.dma("sp", lambda e: e.dma_start(
                            out=vt_[ksl][:], in_=vs[:, kvh * 128:(kvh + 1) * 128].rearrange("(tb p) d -> p tb d", p=128)),
                            vtb[ksl], reads=[vs_b[kvh]], writes=[vtb[ksl]])
                    if kind in ("A", "C"):
                        src_ = bias_a if kind == "A" else bias_c
                        B.dma("sp", lambda e: e.dma_start(out=sS[hs][:], in_=src_[gh % 8]),
                              sSb[hs], writes=[sSb[hs]])
                    elif kind == "B":
                        B.dma("sp", lambda e: e.dma_start(
                            out=sBst[hs], in_=strip_b[li, gh - 8].rearrange("a p w -> p a w")),
                            sSb[hs], writes=[sSb[hs]])
                    B.dma("sp", lambda e: e.dma_start(out=gt_[hs][:], in_=gs[gh]), gtb[hs],
                          reads=[gs_b[gh]], writes=[gtb[hs]])

                def issue_strip_exp(gh):
                    kind = hinfo[gh][0]
                    hs = gh % 2
                    if kind in ("A", "C"):
                        B.op("act", lambda e: e.activation(out=eS[hs][:], in_=sS[hs][:], func=AF.Exp),
                             reads=[sSb[hs]], writes=[sAb[hs]])
                    elif kind == "B":
                        B.op("act", lambda e: e.activation(out=eS[hs][:, 0:2 * W_B], in_=sS[hs][:, 0:2 * W_B], func=AF.Exp),
                             reads=[sSb[hs]], writes=[sAb[hs]])

                pend = []
                deferred = []
                DEFER = 3
                ngen = [0]

                def issue_pv(item):
                    (gh, kind, kvh, qt, kb, segs, first, last, pslot, hslot, kslot, qn_) = item
                    oi_ = O_IDX[qn_ % 2]
                    di_ = D_IDX[qn_ % 2]
                    B.op("pe", [lambda e: e.matmul(psf[oi_][:], lhsT=vt_[kslot][:, kb, :], rhs=pt_[pslot][:],
                                                   start=first, stop=last),
                                lambda e: e.matmul(psf[di_][:], lhsT=ones_bf[:], rhs=pt_[pslot][:],
                                                   start=first, stop=last)],
                         reads=[vtb[kslot], ptb[pslot], cbuf],
                         writes=([psfs[oi_], psfs[di_]] if first else []),
                         accum=([] if first else [psfs[oi_], psfs[di_]]))
                    if last:
                        r = qn_ % 2

                        def epi1():
                            if kind == "C":
                                B.op("dve", lambda e: e.tensor_scalar(out=of_[r][:], in0=psf[di_][:], scalar1=sk[:, gh:gh + 1],
                                                                      scalar2=None, op0=ALU.add),
                                     reads=[psfs[di_], skb], writes=[ofb[r]])
                                B.op("act", lambda e: e.activation(out=lt_[r][:], in_=of_[r][:], func=AF.Ln),
                                     reads=[ofb[r]], writes=[ltb[r]])
                            else:
                                B.op("act", lambda e: e.activation(out=lt_[r][:], in_=psf[di_][:], func=AF.Ln),
                                     reads=[psfs[di_]], writes=[ltb[r]])
                            B.op("act", lambda e: e.activation(out=rd_[r][:], in_=lt_[r][:], func=AF.Exp, scale=-1.0),
                                 reads=[ltb[r]], writes=[rdb[r]])
                            B.op("pool", lambda e: e.tensor_tensor(out=gp_[r][:], in0=gt_[hslot][:, qt * 512:(qt + 1) * 512],
                                                                   in1=rd_[r][:], op=ALU.mult),
                                 reads=[gtb[hslot], rdb[r]], writes=[gpb[r]])

                        def epi2():
                            B.op("dve", lambda e: e.tensor_tensor(out=ybuf[:, gh, qt * 512:(qt + 1) * 512], in0=psf[oi_][:],
                                                                  in1=gp_[r][:], op=ALU.mult),
                                 reads=[psfs[oi_], gpb[r]], accum=[ybufs[qt]])
                        deferred.append((ngen[0] + 1, epi1))
                        deferred.append((ngen[0] + 4, epi2))
                        deferred.sort(key=lambda t: t[0])

                issue_loads(0)
                issue_strip_exp(0)
                qtn = -1
                tih = 0
                prev_gh = -1
                for ti, (gh, kind, kvh, qt, kb, segs, first, last) in enumerate(tiles):
                    if gh != prev_gh:
                        prev_gh = gh
                        tih = 0
                    if tih == LOOK + 6 and gh + 1 < 16:
                        issue_loads(gh + 1)
                    if tih == hcount[gh] - 1 and gh + 1 < 16:
                        issue_strip_exp(gh + 1)
                    tih += 1
                    if first:
                        qtn += 1
                    hs, ksl, qn_ = gh % 2, hinfo[gh][2], qtn
                    si = S_IDX[ti % 4]
                    B.op("pe", lambda e, si=si, ksl=ksl, kb=kb, hs=hs, qt=qt: e.matmul(
                        psf[si][:], lhsT=kt_[ksl][:, kb * 128:(kb + 1) * 128], rhs=qt_[hs][:, qt * 512:(qt + 1) * 512],
                        start=True, stop=True),
                        reads=[ktb[ksl], qtb[hs]], writes=[psfs[si]])
                    pslot = ti % NP
                    if segs is None:
                        B.op("act", lambda e, pslot=pslot, si=si: e.activation(out=pt_[pslot][:], in_=psf[si][:], func=AF.Exp),
                             reads=[psfs[si]], writes=[ptb[pslot]])
                    else:
                        tsl = ti % NT
                        B.op("act", lambda e, tsl=tsl, si=si: e.activation(out=tt_[tsl][:], in_=psf[si][:], func=AF.Exp),
                             reads=[psfs[si]], writes=[ttb[tsl]])
                        fns = []
                        for (c0, c1, which, s0) in segs:
                            if kind == "B":
                                bap = sB[hs][:, which, s0:s0 + (c1 - c0)]
                            else:
                                bap = sA[hs][:, 0, s0:s0 + (c1 - c0)]
                            fns.append(lambda e, tsl=tsl, pslot=pslot, c0=c0, c1=c1, bap=bap: e.tensor_tensor(
                                out=pt_[pslot][:, c0:c1], in0=tt_[tsl][:, c0:c1], in1=bap, op=ALU.mult))
                        B.op("dve", fns, reads=[ttb[tsl], sAb[hs]], writes=[ptb[pslot]])
                    ngen[0] += 1
                    pend.append((gh, kind, kvh, qt, kb, segs, first, last, pslot, hs, ksl, qn_))
                    if len(pend) > LOOK:
                        issue_pv(pend.pop(0))
                    while deferred and deferred[0][0] <= ngen[0]:
                        deferred.pop(0)[1]()
                while pend:
                    issue_pv(pend.pop(0))
                while deferred:
                    deferred.pop(0)[1]()

        def wout_load(layer_id, g2):
            wload(("out", layer_id, g2), g2 % NW)

        def outproj_phase(w, src, src_b):
            with phase_scope() as sc:
                def sbp(name, shape, dt):
                    return sc.enter_context(nc.sbuf_tensor(uname(name), list(shape), dt))
                NH = 3
                hi = [sbp(f"hi{i}", [128, 512], F32) for i in range(NH)]
                hib = [Buf(f"hi{i}") for i in range(NH)]
                ho = [sbp(f"ho{i}", [128, 512], F32) for i in range(NH)]
                hob = [Buf(f"ho{i}") for i in range(NH)]
                new_b = Buf("hres_new")
                n = 0
                for og in range(4):
                    wi = (og // 2) % NW
                    co = (og % 2) * 512
                    for tb in range(NTB):
                        s = n % NH
                        n += 1
                        B.dma("sp", lambda e, s=s, tb=tb, og=og: e.dma_start(
                            out=hi[s][:], in_=src[tb * 128:(tb + 1) * 128, og * 512:(og + 1) * 512]),
                            hib[s], reads=[src_b], writes=[hib[s]])
                        p = next_ps()
                        fns = [lambda e, p=p, fc=fc, tb=tb, wi=wi, co=co: e.matmul(
                            psf[p][:], lhsT=ybuf[:, fc, tb * 128:(tb + 1) * 128], rhs=wbuf[wi][:, fc, co:co + 512],
                            start=(fc == 0), stop=(fc == NCH - 1)) for fc in range(NCH)]
                        B.op("pe", fns, reads=[ybufs[tb // 4], wbufs[wi]], writes=[psfs[p]])
                        B.op("dve", lambda e, s=s, p=p: e.tensor_tensor(out=ho[s][:], in0=psf[p][:], in1=hi[s][:], op=ALU.add),
                             reads=[psfs[p], hib[s]], writes=[hob[s]])
                        B.dma("sp", lambda e, s=s, tb=tb, og=og: e.dma_start(
                            out=hbuf[tb * 128:(tb + 1) * 128, og * 512:(og + 1) * 512], in_=ho[s][:]),
                            hob[s], reads=[hob[s]], accum=[new_b])
                return new_b

        idb_ = Buf("ident")
        B.dma("pool", lambda e: e.dma_start(out=ident[:], in_=ident_d), idb_, writes=[idb_])
        B.barrier()

        src, src_b = x, Buf("x")
        layer_list = list(layers if layers is not None else range(n_layers))

        def wts(layer):
            ev = layer % 2 == 0
            return (w_in_even if ev else w_in_odd)[layer // 2], (w_out_even if ev else w_out_odd)[layer // 2], (8 if ev else 5)

        def cast_in(layer):
            wi_, _, ng = wts(layer)
            for g in range(ng):
                wcast(("in", layer, g), wi_, g)

        def cast_out(layer):
            _, wo_, _ = wts(layer)
            for g in range(2):
                wcast(("out", layer, g), wo_, g)

        cast_in(layer_list[0])
        cast_out(layer_list[0])
        if len(layer_list) > 1:
            cast_in(layer_list[1])
        for pos, layer in enumerate(layer_list):
            is_even = layer % 2 == 0
            li = layer // 2
            w_in = (w_in_even if is_even else w_in_odd)[li]
            w_out = (w_out_even if is_even else w_out_odd)[li]
            if pos >= 1:
                cast_out(layer)
                if pos + 1 < len(layer_list):
                    cast_in(layer_list[pos + 1])
            rmsnorm_phase(src, src_b, layer, final=False)
            B.barrier()
            inproj_phase(w_in, is_even, li, layer)
            B.barrier()
            for g2 in range(2):
                wout_load(layer, g2)
            if 'noattn' not in DBG:
                attention_phase(is_even, li)
            B.barrier()
            new_b = outproj_phase(w_out, src, src_b)
            B.barrier()
            src, src_b = hbuf, new_b
        rmsnorm_phase(src, src_b, DEPTH, final=True)
        B.final_wait("sp", [out_b])

        with nc.Block() as block:
            B.flush(block)
    return nc


_CACHE = {}


def _prep_inputs(x, norm_g, final_g, w_in_even, w_out_even, rpb_b, w_in_odd, w_out_odd,
                 sinks_c, qnorm_d, knorm_d):
    f = lambda a: np.ascontiguousarray(np.asarray(a, dtype=np.float32))
    gam = np.concatenate([f(norm_g), f(final_g)[None, :]], 0)
    gam = np.ascontiguousarray(np.broadcast_to(gam[:, None, :], (DEPTH + 1, 128, D_MODEL)))
    common = {
        "gam": gam,
        "w_in_even": f(w_in_even), "w_out_even": f(w_out_even),
        "w_in_odd": f(w_in_odd), "w_out_odd": f(w_out_odd),
        "sinks": np.ascontiguousarray(np.broadcast_to(np.tile(f(sinks_c), (1, 16))[:, None, :], (2, 128, 128))),
        "qn": np.ascontiguousarray(np.broadcast_to(f(qnorm_d)[:, :, None], (2, 128, 128))),
        "kn": np.ascontiguousarray(np.broadcast_to(f(knorm_d)[:, :, None], (2, 128, 128))),
        "bias_a": make_bias_a(), "bias_c": make_bias_c(),
        "strip_b": make_strip_b(f(rpb_b)),
        "rope": make_rope(), "rperm": make_rperm(), "ident": np.eye(128, dtype=np.float32),
    }
    return common


def kernel(x, norm_g, final_g, w_in_even, w_out_even, rpb_b, w_in_odd, w_out_odd,
           sinks_c, qnorm_d, knorm_d):
    x = np.asarray(x, dtype=np.float32)
    common = _prep_inputs(x, norm_g, final_g, w_in_even, w_out_even, rpb_b, w_in_odd, w_out_odd,
                          sinks_c, qnorm_d, knorm_d)
    if "nc" not in _CACHE:
        _CACHE["nc"] = build_program(DEPTH)
    nc = _CACHE["nc"]
    in_maps = [dict(common, x=np.ascontiguousarray(x[b])) for b in range(BATCH)]
    res = run_bass_kernel_spmd(nc, in_maps, core_ids=list(range(BATCH)))
    return np.stack([np.asarray(res.results[b]["out"], dtype=np.float32) for b in range(BATCH)], 0)
```

```python
import math
import os
DBG = os.environ.get('KDBG', '')
from contextlib import ExitStack

import numpy as np
import concourse.bass as bass
import concourse.mybir as mybir
from concourse.bass_utils import run_bass_kernel_spmd

F32 = mybir.dt.float32
BF16 = mybir.dt.bfloat16
AF = mybir.ActivationFunctionType
ALU = mybir.AluOpType
AX = mybir.AxisListType

D_MODEL = 2048
SEQ = 2048
BATCH = 4
DEPTH = 4
HD = 128
NCH = D_MODEL // 128
NTB = SEQ // 128
NTT = SEQ // 512
EPS = 1e-6
SCALE = HD ** -0.5
NEG = -30000.0

OFF_A, W_A = 1408, 2944
OFF_C, W_C = 512, 1152
REL_MIN, W_B = -10, 22 * 64
W_STRIP = W_A


def _alibi_slopes(n):
    return (2.0 ** (-8.0 * np.arange(1, n + 1, dtype=np.float32) / n)).astype(np.float32)


def _toeplitz_strip(f, off, width):
    k = np.arange(128)[:, None]
    j = np.arange(width)[None, :]
    return f(k - j + off)


def make_bias_a():
    slopes = _alibi_slopes(8)
    out = np.empty((8, 128, W_A), np.float32)
    for h in range(8):
        def f(d, h=h):
            ad = np.abs(d)
            mult = (ad <= 64).astype(np.float32)
            mult += ((d % 4 == 0) & (ad <= 256)).astype(np.float32)
            mult += ((d % 16 == 0) & (ad <= 1024)).astype(np.float32)
            val = -slopes[h] * ad.astype(np.float32) + np.log(np.maximum(mult, 1.0)).astype(np.float32)
            return np.where(mult > 0, val, NEG).astype(np.float32)
        out[h] = _toeplitz_strip(f, OFF_A, W_A)
    return out


def make_bias_c():
    slopes = _alibi_slopes(8)
    out = np.empty((8, 128, W_C), np.float32)
    for h in range(8):
        def f(d, h=h):
            ad = np.abs(d)
            val = -slopes[h] * ad.astype(np.float32)
            return np.where(ad <= 128, val, NEG).astype(np.float32)
        out[h] = _toeplitz_strip(f, OFF_C, W_C)
    return out


def make_strip_b(rpb):
    n = rpb.shape[0]
    p = np.arange(128)
    b = (p // 64)[:, None]
    cp = (p % 64)[:, None]
    col = np.arange(W_B)[None, :]
    rel = col // 64 + REL_MIN
    c = col % 64
    cs = np.clip(c - 8, 0, 48)
    colvalid = (cp >= cs) & (cp < cs + 16)
    dcol = np.clip(cp - c + 15, 0, 30)
    dr = b - rel + 7
    drc = np.clip(dr, 0, 14)
    rowvalid_i = ((b - rel) >= -4) & ((b - rel) <= 3)
    rowvalid_u = (dr >= 0) & (dr <= 14)
    out = np.empty((n, 8, 2, 128, W_B), np.float32)
    for i in range(n):
        for h in range(8):
            g = rpb[i, h][drc, dcol]
            out[i, h, 0] = np.where(colvalid & rowvalid_i, g, NEG)
            out[i, h, 1] = np.where(colvalid & rowvalid_u, g, NEG)
    return out


def make_rope():
    t = np.arange(SEQ)
    hn = 32
    inv = (10000.0 ** (-np.arange(hn, dtype=np.float32) / hn)).astype(np.float32)
    d = np.arange(128)
    pos = np.where(d[:, None] < 64, (t // 64)[None, :], (t % 64)[None, :]).astype(np.float32)
    ang = pos * inv[d % 32][:, None]
    return np.stack([np.cos(ang), np.sin(ang)]).astype(np.float32)


def make_rperm():
    r = np.zeros((128, 128), np.float32)
    for m in range(128):
        if (m % 64) < 32:
            r[m + 32, m] = -1.0
        else:
            r[m - 32, m] = 1.0
    return r


class Sem:
    def __init__(self, h, name):
        self.h = h
        self.name = name


class Buf:
    def __init__(self, name):
        self.name = name
        self.w = {}
        self.r = {}
        self.dsem = None
        self.dcount = 0


def _merge(dst, src):
    for s, v in src.items():
        if dst.get(s, 0) < v:
            dst[s] = v


class Builder:
    ENG = ("pe", "act", "dve", "pool", "sp")

    def __init__(self, nc, stack):
        self.nc = nc
        self.stack = stack
        self.q = {e: [] for e in self.ENG}
        self.esem = {}
        self.ecount = {}
        for e in ("pe", "act", "dve", "pool"):
            self.esem[e] = self.new_sem("es_" + e)
            self.ecount[e] = 0
        self.waited = {e: {} for e in self.ENG}
        self.dsems = {}

    def new_sem(self, name):
        return Sem(self.stack.enter_context(self.nc.semaphore(name)), name)

    def _emit_waits(self, eng, deps):
        wd = self.waited[eng]
        for s, v in deps.items():
            if wd.get(s, 0) < v:
                wd[s] = v
                self.q[eng].append(("wait", s.h, v))

    def _deps(self, reads, writes, accum):
        deps = {}
        for b in reads:
            _merge(deps, b.w)
        for b in writes:
            _merge(deps, b.r)
            _merge(deps, b.w)
        for b in accum:
            _merge(deps, b.r)
        return deps

    def _commit(self, tok, reads, writes, accum):
        for b in reads:
            _merge(b.r, tok)
        for b in writes:
            b.w = dict(tok)
            b.r = {}
        for b in accum:
            _merge(b.w, tok)
            b.r = {}

    def op(self, eng, fns, reads=(), writes=(), accum=()):
        if callable(fns):
            fns = [fns]
        deps = self._deps(reads, writes, accum)
        self._emit_waits(eng, deps)
        for f in fns[:-1]:
            self.q[eng].append(("ins", f, None, 0))
        s = self.esem[eng]
        self.ecount[eng] += 1
        self.q[eng].append(("ins", fns[-1], s.h, 1))
        tok = {s: self.ecount[eng]}
        self.waited[eng][s] = max(self.waited[eng].get(s, 0), 0)
        self._commit(tok, reads, writes, accum)

    def dma(self, eng, fn, sbuf, reads=(), writes=(), accum=()):
        if sbuf.name not in self.dsems:
            self.dsems[sbuf.name] = [self.new_sem("d_" + sbuf.name), 0]
        ent = self.dsems[sbuf.name]
        deps = self._deps(reads, writes, accum)
        self._emit_waits(eng, deps)
        ent[1] += 16
        self.q[eng].append(("ins", fn, ent[0].h, 16))
        tok = {ent[0]: ent[1]}
        self._commit(tok, reads, writes, accum)

    def barrier(self):
        toks = {self.esem[e]: self.ecount[e] for e in self.esem if self.ecount[e] > 0}
        for name, (sem, cnt) in self.dsems.items():
            if cnt > 0:
                toks[sem] = cnt
        for e in self.ENG:
            self._emit_waits(e, toks)

    def final_wait(self, eng, bufs):
        deps = {}
        for b in bufs:
            _merge(deps, b.w)
            _merge(deps, b.r)
        self._emit_waits(eng, deps)

    def flush(self, block):
        nc = self.nc

        def run(e, items):
            for it in items:
                if it[0] == "wait":
                    e.wait_ge(it[1], it[2])
                else:
                    ins = it[1](e)
                    if it[2] is not None:
                        ins.then_inc(it[2], it[3])

        q = self.q

        @block.tensor
        def _(e):
            run(e, q["pe"])

        @block.scalar
        def _(e):
            run(e, q["act"])

        @block.vector
        def _(e):
            run(e, q["dve"])

        @block.gpsimd
        def _(e):
            run(e, q["pool"])

        @block.sync
        def _(e):
            run(e, q["sp"])


def _segments(layer_is_even):
    segs = []
    if layer_is_even:
        for base, gh in ((0, 0), (4096, 8)):
            segs += [("q", gh + i) for i in range(8)]
            segs += [("k", gh + i) for i in range(8)]
            segs += [("v", gh + i) for i in range(8)]
            segs += [("g", gh + i) for i in range(8)]
    else:
        segs += [("q", i) for i in range(8)]
        segs += [("k", i) for i in range(2)]
        segs += [("v", i) for i in range(2)]
        segs += [("g", i) for i in range(8)]
        segs += [("qd", 8 + i) for i in range(8)]
        segs += [("kd", 2 + i) for i in range(2)]
        segs += [("v", 2 + i) for i in range(2)]
        segs += [("g", 8 + i) for i in range(8)]
    return segs


def _plan_a():
    plan = []
    for qt in range(NTT):
        l = []
        for kb in range(NTB):
            e = qt * 512 - kb * 128
            if -1535 <= e <= 1151:
                l.append((kb, [(0, 512, 0, e + OFF_A)]))
        plan.append(l)
    return plan


def _plan_c():
    plan = []
    for qt in range(NTT):
        l = []
        for kb in range(max(0, 4 * qt - 1), min(15, 4 * qt + 4) + 1):
            e = qt * 512 - kb * 128
            l.append((kb, [(0, 512, 0, e + OFF_C)]))
        plan.append(l)
    return plan


def _plan_b():
    plan = []
    for qt in range(NTT):
        l = []
        for j in range(max(0, 4 * qt - 2), min(15, 4 * qt + 5) + 1):
            c0 = (8 * qt - 2 * j - REL_MIN) * 64
            if qt == 0:
                which = 1 if (2 * j + 1) <= 7 else 0
                segs = [(0, 256, which, c0), (256, 512, 0, c0 + 256)]
            elif qt == NTT - 1:
                which = 1 if (2 * j) >= 24 else 0
                segs = [(0, 320, 0, c0), (320, 512, which, c0 + 320)]
            else:
                segs = [(0, 512, 0, c0)]
            l.append((j, segs))
        plan.append(l)
    return plan


def _plan_d():
    return [[(kb, None) for kb in range(NTB)] for _ in range(NTT)]


def build_program(n_layers=DEPTH, layers=None):
    nc = bass.Bass("TRN2", target_bir_lowering=False)

    def din(name, shape, dt=F32):
        return nc.dram_tensor(name, list(shape), dt, kind="ExternalInput").ap()

    x = din("x", [SEQ, D_MODEL])
    gam = din("gam", [DEPTH + 1, 128, D_MODEL])
    w_in_even = din("w_in_even", [2, D_MODEL, 8192])
    w_out_even = din("w_out_even", [2, D_MODEL, D_MODEL])
    w_in_odd = din("w_in_odd", [2, D_MODEL, 5120])
    w_out_odd = din("w_out_odd", [2, D_MODEL, D_MODEL])
    sinks = din("sinks", [2, 128, 128])
    qn = din("qn", [2, 128, 128])
    kn = din("kn", [2, 128, 128])
    bias_a = din("bias_a", [8, 128, W_A])
    bias_c = din("bias_c", [8, 128, W_C])
    strip_b = din("strip_b", [2, 8, 2, 128, W_B])
    rope = din("rope", [2, 128, SEQ])
    rperm_d = din("rperm", [128, 128])
    ident_d = din("ident", [128, 128])
    out = nc.dram_tensor("out", [SEQ, D_MODEL], F32, kind="ExternalOutput").ap()

    hbuf = nc.dram_tensor("hbuf", [SEQ, D_MODEL], F32).ap()
    qs = nc.dram_tensor("qs", [16, 128, SEQ], BF16).ap()
    ks = nc.dram_tensor("ks", [16, 128, SEQ], BF16).ap()
    gs = nc.dram_tensor("gs", [16, 128, SEQ], BF16).ap()
    vs = nc.dram_tensor("vs", [SEQ, 16 * 128], BF16).ap()

    with ExitStack() as stack:
        B = Builder(nc, stack)

        uid = [0]

        def uname(name):
            uid[0] += 1
            return f"{name}_u{uid[0]}"

        def sb(name, shape, dt):
            return stack.enter_context(nc.sbuf_tensor(uname(name), list(shape), dt))

        def ps(name, shape, dt):
            return stack.enter_context(nc.psum_tensor(uname(name), list(shape), dt))

        ybuf = sb("ybuf", [128, NCH, SEQ], BF16)
        ybufs = [Buf(f"ybuf{t}") for t in range(NTT)]
        NW = 2
        GW = 8
        wbuf = [sb(f"wbuf{i}", [128, NCH, GW * 128], BF16) for i in range(NW)]
        wbufs = [Buf(f"wbuf{i}") for i in range(NW)]
        ones_bf = sb("ones_bf", [128, 128], BF16)
        ident = sb("ident", [128, 128], BF16)
        rperm = sb("rperm_s", [128, 128], BF16)
        cbuf = Buf("consts")
        eps_t = sb("eps_t", [128, 1], F32)

        NPS = 8
        psf = [ps(f"psf{i}", [128, 512], F32) for i in range(NPS)]
        psfs = [Buf(f"psf{i}") for i in range(NPS)]
        pst = [psf[6].bitcast(BF16), psf[7].bitcast(BF16)]
        psts = [psfs[6], psfs[7]]

        hsrc_b = Buf("hres")
        qs_b = [Buf(f"qs{i}") for i in range(16)]
        ks_b = [Buf(f"ks{i}") for i in range(16)]
        gs_b = [Buf(f"gs{i}") for i in range(16)]
        vs_b = [Buf(f"vs{i}") for i in range(16)]
        out_b = Buf("out")

        B.op("dve", [lambda e: e.memset(ones_bf[:], 1.0),
                     lambda e: e.memset(eps_t[:], EPS)], writes=[cbuf])
        rp_b = Buf("rperm")
        B.dma("pool", lambda e: e.dma_start(out=rperm[:], in_=rperm_d), rp_b, writes=[rp_b])

        state = {"ps": 0}

        def next_ps():
            i = state["ps"] % NPS
            state["ps"] += 1
            return i

        def phase_scope():
            return ExitStack()

        def rmsnorm_phase(src, src_b, gidx, final):
            with phase_scope() as sc:
                def sbp(name, shape, dt):
                    return sc.enter_context(nc.sbuf_tensor(uname(name), list(shape), dt))
                NHB = 3
                hb = [sbp(f"hb{i}", [128, D_MODEL], F32) for i in range(NHB)]
                hbs = [Buf(f"hb{i}") for i in range(NHB)]
                sq = sbp("sq", [128, D_MODEL], F32)
                sqs = Buf("sq")
                gm = sbp("gm", [128, D_MODEL], F32)
                gms = Buf("gm")
                st = [sbp(f"st{i}", [128, 4], F32) for i in range(NHB)]
                sts = [Buf(f"st{i}") for i in range(NHB)]
                if final:
                    yb = [sbp(f"yo{i}", [128, D_MODEL], F32) for i in range(2)]
                else:
                    yb = [sbp(f"yb{i}", [128, D_MODEL], BF16) for i in range(2)]
                ybs = [Buf(f"yb{i}") for i in range(2)]
                B.dma("sp", lambda e: e.dma_start(out=gm[:], in_=gam[gidx]), gms, writes=[gms])
                def stage_a1(tb):
                    i = tb % NHB
                    B.dma("sp", lambda e, i=i, tb=tb: e.dma_start(out=hb[i][:], in_=src[tb * 128:(tb + 1) * 128, :]),
                          hbs[i], reads=[src_b], writes=[hbs[i]])
                    B.op("act", lambda e, i=i: e.activation(out=sq[:], in_=hb[i][:], func=AF.Square),
                         reads=[hbs[i]], writes=[sqs])
                    B.op("dve", lambda e, i=i: e.reduce_sum(out=st[i][:, 0:1], in_=sq[:], axis=AX.X),
                         reads=[sqs], writes=[sts[i]])

                def stage_a2(tb):
                    i = tb % NHB
                    j = tb % 2
                    B.op("act", lambda e, i=i: e.activation(out=st[i][:, 1:2], in_=st[i][:, 0:1], func=AF.Sqrt,
                                                            scale=1.0 / D_MODEL, bias=eps_t[:, 0:1]),
                         reads=[cbuf], writes=[sts[i]])
                    B.op("dve", lambda e, i=i: e.reciprocal(out=st[i][:, 2:3], in_=st[i][:, 1:2]), writes=[sts[i]])
                    B.op("dve", lambda e, i=i, j=j: e.scalar_tensor_tensor(out=yb[j][:], in0=hb[i][:], scalar=st[i][:, 2:3],
                                                                           in1=gm[:], op0=ALU.mult, op1=ALU.mult),
                         reads=[hbs[i], sts[i], gms], writes=[ybs[j]])

                def stage_b(tb):
                    i = tb % 2
                    if final:
                        B.dma("pool", lambda e, i=i, tb=tb: e.dma_start(out=out[tb * 128:(tb + 1) * 128, :], in_=yb[i][:]),
                              ybs[i], reads=[ybs[i]], accum=[out_b])
                    else:
                        tt = tb // 4
                        for c4 in range(4):
                            pi = (tb * 4 + c4) % 2
                            fns = []
                            for k in range(4):
                                cc = c4 * 4 + k
                                fns.append(lambda e, pi=pi, k=k, cc=cc, i=i: e.transpose(
                                    pst[pi][:, k * 128:(k + 1) * 128], yb[i][:, cc * 128:(cc + 1) * 128], ident[:]))
                            B.op("pe", fns, reads=[ybs[i], idb_], writes=[psts[pi]])
                            dst = ybuf[:, c4 * 4:(c4 + 1) * 4, tb * 128:(tb + 1) * 128]
                            srcp = pst[pi][:, 0:512].rearrange("p (k t) -> p k t", k=4)
                            B.op("act", lambda e, dst=dst, srcp=srcp: e.activation(out=dst, in_=srcp, func=AF.Copy),
                                 reads=[psts[pi]], accum=[ybufs[tt]])

                for tb in range(NTB + 1):
                    if tb < NTB:
                        stage_a1(tb)
                    if tb >= 1:
                        stage_b(tb - 1)
                    if tb < NTB:
                        stage_a2(tb)

        def inproj_phase(w, is_even, li):
            segs = _segments(is_even)
            ngroups = len(segs) // GW
            with phase_scope() as sc:
                def sbp(name, shape, dt):
                    return sc.enter_context(nc.sbuf_tensor(uname(name), list(shape), dt))
                NO = 4
                ot = [sbp(f"ot{i}", [128, 512], BF16) for i in range(NO)]
                ots = [Buf(f"ot{i}") for i in range(NO)]
                oi = [0]
                if not is_even:
                    cs_t = sbp("cs_t", [128, 2, SEQ], F32)
                    cs_b = Buf("cs_t")
                    B.dma("sp", lambda e: e.dma_start(out=cs_t[:], in_=rope.rearrange("a p t -> p a t")), cs_b, writes=[cs_b])
                    gq = sbp("gq", [128, 2, 128], F32)
                    gq_b = Buf("gq")
                    B.dma("sp", lambda e: e.dma_start(out=gq[:, 0, :], in_=qn[li]), gq_b, accum=[gq_b])
                    B.dma("sp", lambda e: e.dma_start(out=gq[:, 1, :], in_=kn[li]), gq_b, accum=[gq_b])
                    tmp = [sbp(f"tmp{i}", [128, 512], F32) for i in range(5)]
                    tmps = [Buf(f"tmp{i}") for i in range(5)]
                    sqh = sbp("sqh", [128, 512], BF16)
                    xhi = sbp("xhi", [128, 512], BF16)
                    xlo = sbp("xlo", [128, 512], BF16)
                    xhib, xlob, sqhb = Buf("xhi"), Buf("xlo"), Buf("sqh")

                NSW = 12
                stg = [sbp(f"stg{i}", [128, NCH - NSW, GW * 128], F32) for i in range(2)]
                stgb = [Buf(f"stg{i}") for i in range(2)]

                def prefetch(g):
                    wi = g % NW
                    si_ = g % 2
                    wv = w[:, g * GW * 128:(g + 1) * GW * 128].rearrange("(cc p) f -> p cc f", p=128)
                    B.dma("pool", lambda e: e.dma_start(out=wbuf[wi][:, 0:NSW, :], in_=wv[:, 0:NSW, :]),
                          wbufs[wi], writes=[wbufs[wi]])
                    B.dma("sp", lambda e: e.dma_start(out=stg[si_][:], in_=wv[:, NSW:NCH, :]),
                          stgb[si_], writes=[stgb[si_]])
                    B.op("pool", lambda e: e.tensor_copy(out=wbuf[wi][:, NSW:NCH, :], in_=stg[si_][:]),
                         reads=[stgb[si_]], accum=[wbufs[wi]])

                prefetch(0)
                for g in range(ngroups):
                    wi = g % NW
                    if g + 1 < ngroups:
                        prefetch(g + 1)
                    gsegs = segs[g * GW:(g + 1) * GW]
                    k = 0
                    while k < GW:
                        typ, idx = gsegs[k]
                        if 'noqd' in DBG and typ in ('qd', 'kd'):
                            typ = typ[0]
                        if typ == "v":
                            n = 1
                            while n < 4 and k + n < GW and gsegs[k + n][0] == "v" and gsegs[k + n][1] == idx + n:
                                n += 1
                            ncol = 128 * n
                            for tb in range(NTB):
                                p = next_ps()
                                fns = [lambda e, p=p, cc=cc, tb=tb, k=k, ncol=ncol, wi=wi: e.matmul(
                                    psf[p][:, 0:ncol], lhsT=ybuf[:, cc, tb * 128:(tb + 1) * 128],
                                    rhs=wbuf[wi][:, cc, k * 128:k * 128 + ncol], start=(cc == 0), stop=(cc == NCH - 1))
                                    for cc in range(NCH)]
                                B.op("pe", fns, reads=[ybufs[tb // 4], wbufs[wi]], writes=[psfs[p]])
                                o = oi[0] % NO
                                oi[0] += 1
                                B.op("dve", lambda e, o=o, p=p, ncol=ncol: e.tensor_copy(out=ot[o][:, 0:ncol], in_=psf[p][:, 0:ncol]),
                                     reads=[psfs[p]], writes=[ots[o]])
                                B.dma("sp", lambda e, o=o, tb=tb, idx=idx, ncol=ncol: e.dma_start(
                                    out=vs[tb * 128:(tb + 1) * 128, idx * 128:idx * 128 + ncol], in_=ot[o][:, 0:ncol]),
                                    ots[o], reads=[ots[o]], accum=[vs_b[idx + j] for j in range(n)])
                            k += n
                            continue
                        for tt in range(NTT):
                            p = next_ps()
                            fns = [lambda e, p=p, cc=cc, tt=tt, k=k, wi=wi: e.matmul(
                                psf[p][:], lhsT=wbuf[wi][:, cc, k * 128:(k + 1) * 128],
                                rhs=ybuf[:, cc, tt * 512:(tt + 1) * 512], start=(cc == 0), stop=(cc == NCH - 1))
                                for cc in range(NCH)]
                            B.op("pe", fns, reads=[ybufs[tt], wbufs[wi]], writes=[psfs[p]])
                            o = oi[0] % NO
                            oi[0] += 1
                            if typ == "q":
                                B.op("dve", lambda e, o=o, p=p: e.tensor_scalar(out=ot[o][:], in0=psf[p][:], scalar1=SCALE,
                                                                               scalar2=None, op0=ALU.mult),
                                     reads=[psfs[p]], writes=[ots[o]])
                                dst, dstb = qs, qs_b
                            elif typ == "k":
                                B.op("act", lambda e, o=o, p=p: e.activation(out=ot[o][:], in_=psf[p][:], func=AF.Copy),
                                     reads=[psfs[p]], writes=[ots[o]])
                                dst, dstb = ks, ks_b
                            elif typ == "g":
                                B.op("act", lambda e, o=o, p=p: e.activation(out=ot[o][:], in_=psf[p][:], func=AF.Silu),
                                     reads=[psfs[p]], writes=[ots[o]])
                                dst, dstb = gs, gs_b
                            else:
                                isq = typ == "qd"
                                gcol = 0 if isq else 1
                                osc = SCALE if isq else 1.0
                                xg, sqx, t1, t2, rs = tmp
                                xgb, sqxb, t1b, t2b, rsb = tmps
                                sqb = sqhb
                                B.op("dve", lambda e, p=p, gcol=gcol: e.tensor_scalar(
                                    out=xg[:], in0=psf[p][:], scalar1=gq[:, gcol, 0:1], scalar2=None, op0=ALU.mult),
                                    reads=[psfs[p], gq_b], writes=[xgb])
                                B.op("act", lambda e, p=p: e.activation(out=sqh[:], in_=psf[p][:], func=AF.Square),
                                     reads=[psfs[p], xgb], writes=[sqb])
                                B.op("dve", lambda e: e.tensor_copy(out=xhi[:], in_=xg[:]), reads=[xgb], writes=[xhib])
                                B.op("dve", lambda e: e.tensor_tensor(out=xlo[:], in0=xg[:], in1=xhi[:], op=ALU.subtract),
                                     reads=[xgb, xhib], writes=[xlob])
                                p2 = next_ps()
                                B.op("pe", lambda e, p2=p2: e.matmul(psf[p2][:], lhsT=ones_bf[:], rhs=sqh[:], start=True, stop=True),
                                     reads=[sqb, cbuf], writes=[psfs[p2]])
                                p3 = next_ps()
                                B.op("pe", [lambda e, p3=p3: e.matmul(psf[p3][:], lhsT=rperm[:], rhs=xhi[:], start=True, stop=False),
                                            lambda e, p3=p3: e.matmul(psf[p3][:], lhsT=rperm[:], rhs=xlo[:], start=False, stop=True)],
                                     reads=[xhib, xlob, rp_b], writes=[psfs[p3]])
                                B.op("dve", lambda e, p2=p2: e.tensor_scalar(out=t2[:], in0=psf[p2][:], scalar1=1.0 / HD,
                                                                             scalar2=EPS, op0=ALU.mult, op1=ALU.add),
                                     reads=[psfs[p2]], writes=[t2b])
                                B.op("act", lambda e: e.activation(out=rs[:], in_=t2[:], func=AF.Ln),
                                     reads=[t2b], writes=[rsb])
                                B.op("act", lambda e: e.activation(out=sqx[:], in_=rs[:], func=AF.Exp, scale=-0.5),
                                     reads=[rsb], writes=[sqxb])
                                B.op("dve", lambda e, tt=tt: e.tensor_tensor(out=t1[:], in0=xg[:], in1=cs_t[:, 0, tt * 512:(tt + 1) * 512],
                                                                            op=ALU.mult),
                                     reads=[xgb, cs_b], writes=[t1b])
                                B.op("dve", lambda e, tt=tt, p3=p3: e.tensor_tensor(out=t2[:], in0=psf[p3][:],
                                                                                   in1=cs_t[:, 1, tt * 512:(tt + 1) * 512], op=ALU.mult),
                                     reads=[psfs[p3], cs_b], writes=[t2b])
                                B.op("dve", lambda e: e.tensor_tensor(out=rs[:], in0=t1[:], in1=t2[:], op=ALU.add),
                                     reads=[t1b, t2b], writes=[rsb])
                                B.op("dve", lambda e, o=o, osc=osc: e.scalar_tensor_tensor(
                                    out=ot[o][:], in0=rs[:], scalar=osc, in1=sqx[:], op0=ALU.mult, op1=ALU.mult),
                                    reads=[rsb, sqxb], writes=[ots[o]])
                                dst, dstb = (qs, qs_b) if isq else (ks, ks_b)
                            B.dma("sp", lambda e, o=o, dst=dst, idx=idx, tt=tt: e.dma_start(
                                out=dst[idx, :, tt * 512:(tt + 1) * 512], in_=ot[o][:]),
                                ots[o], reads=[ots[o]], accum=[dstb[idx]])
                        k += 1

        def attention_phase(is_even, li):
            with phase_scope() as sc:
                def sbp(name, shape, dt):
                    return sc.enter_context(nc.sbuf_tensor(uname(name), list(shape), dt))
                qt_ = [sbp(f"qh{i}", [128, SEQ], BF16) for i in range(2)]
                qtb = [Buf(f"qh{i}") for i in range(2)]
                gt_ = [sbp(f"gh{i}", [128, SEQ], BF16) for i in range(2)]
                gtb = [Buf(f"gh{i}") for i in range(2)]
                kt_ = [sbp(f"kh{i}", [128, SEQ], BF16) for i in range(2)]
                ktb = [Buf(f"kh{i}") for i in range(2)]
                vt_ = [sbp(f"vh{i}", [128, NTB, 128], BF16) for i in range(2)]
                vtb = [Buf(f"vh{i}") for i in range(2)]
                WS = W_A if is_even else W_C
                sS1 = sbp("sS0", [128, WS], F32)
                sS = [sS1, sS1]
                sSb1 = Buf("sS0")
                sSb = [sSb1, sSb1]
                eS = [sbp(f"eS{i}", [128, WS], BF16) for i in range(2)]
                sAb = [Buf(f"eS{i}") for i in range(2)]
                sBb = sAb
                sA = [t[:].rearrange("p (a w) -> p a w", a=1) for t in eS]
                if is_even:
                    sBst = [t[:, 0:2 * W_B].rearrange("p (a w) -> p a w", a=2) for t in sS]
                    sB = [t[:, 0:2 * W_B].rearrange("p (a w) -> p a w", a=2) for t in eS]
                LOOK = 3
                NT = 4
                tt_ = [sbp(f"tT{i}", [128, 512], BF16) for i in range(NT)]
                ttb = [Buf(f"tT{i}") for i in range(NT)]
                NP = LOOK + 2
                pt_ = [sbp(f"pT{i}", [128, 512], BF16) for i in range(NP)]
                ptb = [Buf(f"pT{i}") for i in range(NP)]
                rd_ = [sbp(f"rd{i}", [128, 512], F32) for i in range(2)]
                rdb = [Buf(f"rd{i}") for i in range(2)]
                if not is_even:
                    of_ = [sbp(f"of{i}", [128, 512], F32) for i in range(2)]
                    ofb = [Buf(f"of{i}") for i in range(2)]
                gp_ = [sbp(f"gp{i}", [128, 512], F32) for i in range(2)]
                gpb = [Buf(f"gp{i}") for i in range(2)]
                lt1 = sbp("lt0", [128, 512], F32)
                lt_ = [lt1, lt1]
                ltb1 = Buf("lt0")
                ltb = [ltb1, ltb1]
                if not is_even:
                    sk0 = sbp("sk0", [128, 128], F32)
                    sk = sbp("sk", [128, 128], F32)
                    sk0b = Buf("sk0")
                    skb = Buf("sk")
                    B.dma("sp", lambda e: e.dma_start(out=sk0[:], in_=sinks[li]), sk0b, writes=[sk0b])
                    B.op("act", lambda e: e.activation(out=sk[:], in_=sk0[:], func=AF.Exp), reads=[sk0b], writes=[skb])

                S_IDX = [0, 1, 2, 3]
                O_IDX = [4, 5]
                D_IDX = [6, 7]

                tiles = []
                for gh in range(16):
                    if is_even:
                        kind = "A" if gh < 8 else "B"
                        kvh = gh
                    else:
                        kind = "C" if gh < 8 else "D"
                        kvh = gh // 4 if gh < 8 else 2 + (gh - 8) // 4
                    plan = {"A": _plan_a, "B": _plan_b, "C": _plan_c, "D": _plan_d}[kind]()
                    for qt in range(NTT):
                        l = plan[qt]
                        for n, (kb, segs) in enumerate(l):
                            tiles.append((gh, kind, kvh, qt, kb, segs, n == 0, n == len(l) - 1))

                hinfo = []
                ksl_, prev_kvh = -1, -1
                for gh in range(16):
                    if is_even:
                        kind = "A" if gh < 8 else "B"
                        kvh = gh
                    else:
                        kind = "C" if gh < 8 else "D"
                        kvh = gh // 4 if gh < 8 else 2 + (gh - 8) // 4
                    newkv = kvh != prev_kvh
                    if newkv:
                        ksl_ = (ksl_ + 1) % 2
                        prev_kvh = kvh
                    hinfo.append((kind, kvh, ksl_, newkv))
                hcount = [0] * 16
                for t in tiles:
                    hcount[t[0]] += 1

                def issue_loads(gh):
                    kind, kvh, ksl, newkv = hinfo[gh]
                    hs = gh % 2
                    B.dma("sp", lambda e: e.dma_start(out=qt_[hs][:], in_=qs[gh]), qtb[hs],
                          reads=[qs_b[gh]], writes=[qtb[hs]])
                    if newkv:
                        B.dma("sp", lambda e: e.dma_start(out=kt_[ksl][:], in_=ks[kvh]), ktb[ksl],
                              reads=[ks_b[kvh]], writes=[ktb[ksl]])
                        B.dma("sp", lambda e: e.dma_start(
                            out=vt_[ksl][:], in_=vs[:, kvh * 128:(kvh + 1) * 128].rearrange("(tb p) d -> p tb d", p=128)),
                            vtb[ksl], reads=[vs_b[kvh]], writes=[vtb[ksl]])
                    if kind in ("A", "C"):
                        src_ = bias_a if kind == "A" else bias_c
                        B.dma("sp", lambda e: e.dma_start(out=sS[hs][:], in_=src_[gh % 8]),
                              sSb[hs], writes=[sSb[hs]])
                    elif kind == "B":
                        B.dma("sp", lambda e: e.dma_start(
                            out=sBst[hs], in_=strip_b[li, gh - 8].rearrange("a p w -> p a w")),
                            sSb[hs], writes=[sSb[hs]])
                    B.dma("sp", lambda e: e.dma_start(out=gt_[hs][:], in_=gs[gh]), gtb[hs],
                          reads=[gs_b[gh]], writes=[gtb[hs]])

                def issue_strip_exp(gh):
                    kind = hinfo[gh][0]
                    hs = gh % 2
                    if kind in ("A", "C"):
                        B.op("act", lambda e: e.activation(out=eS[hs][:], in_=sS[hs][:], func=AF.Exp),
                             reads=[sSb[hs]], writes=[sAb[hs]])
                    elif kind == "B":
                        B.op("act", lambda e: e.activation(out=eS[hs][:, 0:2 * W_B], in_=sS[hs][:, 0:2 * W_B], func=AF.Exp),
                             reads=[sSb[hs]], writes=[sAb[hs]])

                pend = []
                deferred = []
                DEFER = 3
                ngen = [0]

                def issue_pv(item):
                    (gh, kind, kvh, qt, kb, segs, first, last, pslot, hslot, kslot, qn_) = item
                    oi_ = O_IDX[qn_ % 2]
                    di_ = D_IDX[qn_ % 2]
                    B.op("pe", [lambda e: e.matmul(psf[oi_][:], lhsT=vt_[kslot][:, kb, :], rhs=pt_[pslot][:],
                                                   start=first, stop=last),
                                lambda e: e.matmul(psf[di_][:], lhsT=ones_bf[:], rhs=pt_[pslot][:],
                                                   start=first, stop=last)],
                         reads=[vtb[kslot], ptb[pslot], cbuf],
                         writes=([psfs[oi_], psfs[di_]] if first else []),
                         accum=([] if first else [psfs[oi_], psfs[di_]]))
                    if last:
                        r = qn_ % 2

                        def epi1():
                            if kind == "C":
                                B.op("dve", lambda e: e.tensor_scalar(out=of_[r][:], in0=psf[di_][:], scalar1=sk[:, gh:gh + 1],
                                                                      scalar2=None, op0=ALU.add),
                                     reads=[psfs[di_], skb], writes=[ofb[r]])
                                B.op("act", lambda e: e.activation(out=lt_[r][:], in_=of_[r][:], func=AF.Ln),
                                     reads=[ofb[r]], writes=[ltb[r]])
                            else:
                                B.op("act", lambda e: e.activation(out=lt_[r][:], in_=psf[di_][:], func=AF.Ln),
                                     reads=[psfs[di_]], writes=[ltb[r]])
                            B.op("act", lambda e: e.activation(out=rd_[r][:], in_=lt_[r][:], func=AF.Exp, scale=-1.0),
                                 reads=[ltb[r]], writes=[rdb[r]])
                            B.op("pool", lambda e: e.tensor_tensor(out=gp_[r][:], in0=gt_[hslot][:, qt * 512:(qt + 1) * 512],
                                                                   in1=rd_[r][:], op=ALU.mult),
                                 reads=[gtb[hslot], rdb[r]], writes=[gpb[r]])

                        def epi2():
                            B.op("dve", lambda e: e.tensor_tensor(out=ybuf[:, gh, qt * 512:(qt + 1) * 512], in0=psf[oi_][:],
                                                                  in1=gp_[r][:], op=ALU.mult),
                                 reads=[psfs[oi_], gpb[r]], accum=[ybufs[qt]])
                        deferred.append((ngen[0] + 1, epi1))
                        deferred.append((ngen[0] + 4, epi2))
                        deferred.sort(key=lambda t: t[0])

                issue_loads(0)
                issue_strip_exp(0)
                qtn = -1
                tih = 0
                prev_gh = -1
                for ti, (gh, kind, kvh, qt, kb, segs, first, last) in enumerate(tiles):
                    if gh != prev_gh:
                        prev_gh = gh
                        tih = 0
                    if tih == LOOK + 6 and gh + 1 < 16:
                        issue_loads(gh + 1)
                    if tih == hcount[gh] - 1 and gh + 1 < 16:
                        issue_strip_exp(gh + 1)
                    tih += 1
                    if first:
                        qtn += 1
                    hs, ksl, qn_ = gh % 2, hinfo[gh][2], qtn
                    si = S_IDX[ti % 4]
                    B.op("pe", lambda e, si=si, ksl=ksl, kb=kb, hs=hs, qt=qt: e.matmul(
                        psf[si][:], lhsT=kt_[ksl][:, kb * 128:(kb + 1) * 128], rhs=qt_[hs][:, qt * 512:(qt + 1) * 512],
                        start=True, stop=True),
                        reads=[ktb[ksl], qtb[hs]], writes=[psfs[si]])
                    pslot = ti % NP
                    if segs is None:
                        B.op("act", lambda e, pslot=pslot, si=si: e.activation(out=pt_[pslot][:], in_=psf[si][:], func=AF.Exp),
                             reads=[psfs[si]], writes=[ptb[pslot]])
                    else:
                        tsl = ti % NT
                        B.op("act", lambda e, tsl=tsl, si=si: e.activation(out=tt_[tsl][:], in_=psf[si][:], func=AF.Exp),
                             reads=[psfs[si]], writes=[ttb[tsl]])
                        fns = []
                        for (c0, c1, which, s0) in segs:
                            if kind == "B":
                                bap = sB[hs][:, which, s0:s0 + (c1 - c0)]
                            else:
                                bap = sA[hs][:, 0, s0:s0 + (c1 - c0)]
                            fns.append(lambda e, tsl=tsl, pslot=pslot, c0=c0, c1=c1, bap=bap: e.tensor_tensor(
                                out=pt_[pslot][:, c0:c1], in0=tt_[tsl][:, c0:c1], in1=bap, op=ALU.mult))
                        B.op("dve", fns, reads=[ttb[tsl], sAb[hs]], writes=[ptb[pslot]])
                    ngen[0] += 1
                    pend.append((gh, kind, kvh, qt, kb, segs, first, last, pslot, hs, ksl, qn_))
                    if len(pend) > LOOK:
                        issue_pv(pend.pop(0))
                    while deferred and deferred[0][0] <= ngen[0]:
                        deferred.pop(0)[1]()
                while pend:
                    issue_pv(pend.pop(0))
                while deferred:
                    deferred.pop(0)[1]()

        def wout_load(w, g2):
            wi = g2 % NW
            B.dma("pool", lambda e: e.dma_start(
                out=wbuf[wi][:], in_=w[:, g2 * GW * 128:(g2 + 1) * GW * 128].rearrange("(cc p) f -> p cc f", p=128)),
                wbufs[wi], writes=[wbufs[wi]])

        def outproj_phase(w, src, src_b):
            with phase_scope() as sc:
                def sbp(name, shape, dt):
                    return sc.enter_context(nc.sbuf_tensor(uname(name), list(shape), dt))
                NH = 3
                hi = [sbp(f"hi{i}", [128, 512], F32) for i in range(NH)]
                hib = [Buf(f"hi{i}") for i in range(NH)]
                ho = [sbp(f"ho{i}", [128, 512], F32) for i in range(NH)]
                hob = [Buf(f"ho{i}") for i in range(NH)]
                new_b = Buf("hres_new")
                n = 0
                for og in range(4):
                    wi = (og // 2) % NW
                    co = (og % 2) * 512
                    for tb in range(NTB):
                        s = n % NH
                        n += 1
                        B.dma("sp", lambda e, s=s, tb=tb, og=og: e.dma_start(
                            out=hi[s][:], in_=src[tb * 128:(tb + 1) * 128, og * 512:(og + 1) * 512]),
                            hib[s], reads=[src_b], writes=[hib[s]])
                        p = next_ps()
                        fns = [lambda e, p=p, fc=fc, tb=tb, wi=wi, co=co: e.matmul(
                            psf[p][:], lhsT=ybuf[:, fc, tb * 128:(tb + 1) * 128], rhs=wbuf[wi][:, fc, co:co + 512],
                            start=(fc == 0), stop=(fc == NCH - 1)) for fc in range(NCH)]
                        B.op("pe", fns, reads=[ybufs[tb // 4], wbufs[wi]], writes=[psfs[p]])
                        B.op("dve", lambda e, s=s, p=p: e.tensor_tensor(out=ho[s][:], in0=psf[p][:], in1=hi[s][:], op=ALU.add),
                             reads=[psfs[p], hib[s]], writes=[hob[s]])
                        B.dma("pool", lambda e, s=s, tb=tb, og=og: e.dma_start(
                            out=hbuf[tb * 128:(tb + 1) * 128, og * 512:(og + 1) * 512], in_=ho[s][:]),
                            hob[s], reads=[hob[s]], accum=[new_b])
                return new_b

        idb_ = Buf("ident")
        B.dma("pool", lambda e: e.dma_start(out=ident[:], in_=ident_d), idb_, writes=[idb_])
        B.barrier()

        src, src_b = x, Buf("x")
        for layer in (layers if layers is not None else range(n_layers)):
            is_even = layer % 2 == 0
            li = layer // 2
            w_in = (w_in_even if is_even else w_in_odd)[li]
            w_out = (w_out_even if is_even else w_out_odd)[li]
            rmsnorm_phase(src, src_b, layer, final=False)
            B.barrier()
            inproj_phase(w_in, is_even, li)
            B.barrier()
            for g2 in range(2):
                wout_load(w_out, g2)
            if 'noattn' not in DBG:
                attention_phase(is_even, li)
            B.barrier()
            new_b = outproj_phase(w_out, src, src_b)
            B.barrier()
            src, src_b = hbuf, new_b
        rmsnorm_phase(src, src_b, DEPTH, final=True)
        B.final_wait("sp", [out_b])

        with nc.Block() as block:
            B.flush(block)
    return nc


_CACHE = {}


def _prep_inputs(x, norm_g, final_g, w_in_even, w_out_even, rpb_b, w_in_odd, w_out_odd,
                 sinks_c, qnorm_d, knorm_d):
    f = lambda a: np.ascontiguousarray(np.asarray(a, dtype=np.float32))
    gam = np.concatenate([f(norm_g), f(final_g)[None, :]], 0)
    gam = np.ascontiguousarray(np.broadcast_to(gam[:, None, :], (DEPTH + 1, 128, D_MODEL)))
    common = {
        "gam": gam,
        "w_in_even": f(w_in_even), "w_out_even": f(w_out_even),
        "w_in_odd": f(w_in_odd), "w_out_odd": f(w_out_odd),
        "sinks": np.ascontiguousarray(np.broadcast_to(np.tile(f(sinks_c), (1, 16))[:, None, :], (2, 128, 128))),
        "qn": np.ascontiguousarray(np.broadcast_to(f(qnorm_d)[:, :, None], (2, 128, 128))),
        "kn": np.ascontiguousarray(np.broadcast_to(f(knorm_d)[:, :, None], (2, 128, 128))),
        "bias_a": make_bias_a(), "bias_c": make_bias_c(),
        "strip_b": make_strip_b(f(rpb_b)),
        "rope": make_rope(), "rperm": make_rperm(), "ident": np.eye(128, dtype=np.float32),
    }
    return common


def kernel(x, norm_g, final_g, w_in_even, w_out_even, rpb_b, w_in_odd, w_out_odd,
           sinks_c, qnorm_d, knorm_d):
    x = np.asarray(x, dtype=np.float32)
    common = _prep_inputs(x, norm_g, final_g, w_in_even, w_out_even, rpb_b, w_in_odd, w_out_odd,
                          sinks_c, qnorm_d, knorm_d)
    if "nc" not in _CACHE:
        _CACHE["nc"] = build_program(DEPTH)
    nc = _CACHE["nc"]
    in_maps = [dict(common, x=np.ascontiguousarray(x[b])) for b in range(BATCH)]
    res = run_bass_kernel_spmd(nc, in_maps, core_ids=list(range(BATCH)))
    return np.stack([np.asarray(res.results[b]["out"], dtype=np.float32) for b in range(BATCH)], 0)
```

```python
import math
import os
DBG = os.environ.get('KDBG', '')
from contextlib import ExitStack

import numpy as np
import concourse.bass as bass
import concourse.mybir as mybir
from concourse.bass_utils import run_bass_kernel_spmd

F32 = mybir.dt.float32
BF16 = mybir.dt.bfloat16
AF = mybir.ActivationFunctionType
ALU = mybir.AluOpType
AX = mybir.AxisListType

D_MODEL = 2048
SEQ = 2048
BATCH = 4
DEPTH = 4
HD = 128
NCH = D_MODEL // 128
NTB = SEQ // 128
NTT = SEQ // 512
EPS = 1e-6
SCALE = HD ** -0.5
NEG = -30000.0

OFF_A, W_A = 1408, 2944
OFF_C, W_C = 512, 1152
REL_MIN, W_B = -10, 22 * 64
W_STRIP = W_A


def _alibi_slopes(n):
    return (2.0 ** (-8.0 * np.arange(1, n + 1, dtype=np.float32) / n)).astype(np.float32)


def _toeplitz_strip(f, off, width):
    k = np.arange(128)[:, None]
    j = np.arange(width)[None, :]
    return f(k - j + off)


def make_bias_a():
    slopes = _alibi_slopes(8)
    out = np.empty((8, 128, W_A), np.float32)
    for h in range(8):
        def f(d, h=h):
            ad = np.abs(d)
            mult = (ad <= 64).astype(np.float32)
            mult += ((d % 4 == 0) & (ad <= 256)).astype(np.float32)
            mult += ((d % 16 == 0) & (ad <= 1024)).astype(np.float32)
            val = -slopes[h] * ad.astype(np.float32) + np.log(np.maximum(mult, 1.0)).astype(np.float32)
            return np.where(mult > 0, val, NEG).astype(np.float32)
        out[h] = _toeplitz_strip(f, OFF_A, W_A)
    return out


def make_bias_c():
    slopes = _alibi_slopes(8)
    out = np.empty((8, 128, W_C), np.float32)
    for h in range(8):
        def f(d, h=h):
            ad = np.abs(d)
            val = -slopes[h] * ad.astype(np.float32)
            return np.where(ad <= 128, val, NEG).astype(np.float32)
        out[h] = _toeplitz_strip(f, OFF_C, W_C)
    return out


def make_strip_b(rpb):
    n = rpb.shape[0]
    p = np.arange(128)
    b = (p // 64)[:, None]
    cp = (p % 64)[:, None]
    col = np.arange(W_B)[None, :]
    rel = col // 64 + REL_MIN
    c = col % 64
    cs = np.clip(c - 8, 0, 48)
    colvalid = (cp >= cs) & (cp < cs + 16)
    dcol = np.clip(cp - c + 15, 0, 30)
    dr = b - rel + 7
    drc = np.clip(dr, 0, 14)
    rowvalid_i = ((b - rel) >= -4) & ((b - rel) <= 3)
    rowvalid_u = (dr >= 0) & (dr <= 14)
    out = np.empty((n, 8, 2, 128, W_B), np.float32)
    for i in range(n):
        for h in range(8):
            g = rpb[i, h][drc, dcol]
            out[i, h, 0] = np.where(colvalid & rowvalid_i, g, NEG)
            out[i, h, 1] = np.where(colvalid & rowvalid_u, g, NEG)
    return out


def make_rope():
    t = np.arange(SEQ)
    hn = 32
    inv = (10000.0 ** (-np.arange(hn, dtype=np.float32) / hn)).astype(np.float32)
    d = np.arange(128)
    pos = np.where(d[:, None] < 64, (t // 64)[None, :], (t % 64)[None, :]).astype(np.float32)
    ang = pos * inv[d % 32][:, None]
    return np.stack([np.cos(ang), np.sin(ang)]).astype(np.float32)


def make_rperm():
    r = np.zeros((128, 128), np.float32)
    for m in range(128):
        if (m % 64) < 32:
            r[m + 32, m] = -1.0
        else:
            r[m - 32, m] = 1.0
    return r


class Sem:
    def __init__(self, h, name):
        self.h = h
        self.name = name


class Buf:
    def __init__(self, name):
        self.name = name
        self.w = {}
        self.r = {}
        self.dsem = None
        self.dcount = 0


def _merge(dst, src):
    for s, v in src.items():
        if dst.get(s, 0) < v:
            dst[s] = v


class Builder:
    ENG = ("pe", "act", "dve", "pool", "sp")

    def __init__(self, nc, stack):
        self.nc = nc
        self.stack = stack
        self.q = {e: [] for e in self.ENG}
        self.esem = {}
        self.ecount = {}
        for e in ("pe", "act", "dve", "pool"):
            self.esem[e] = self.new_sem("es_" + e)
            self.ecount[e] = 0
        self.waited = {e: {} for e in self.ENG}
        self.dsems = {}

    def new_sem(self, name):
        return Sem(self.stack.enter_context(self.nc.semaphore(name)), name)

    def _emit_waits(self, eng, deps):
        wd = self.waited[eng]
        for s, v in deps.items():
            if wd.get(s, 0) < v:
                wd[s] = v
                self.q[eng].append(("wait", s.h, v))

    def _deps(self, reads, writes, accum):
        deps = {}
        for b in reads:
            _merge(deps, b.w)
        for b in writes:
            _merge(deps, b.r)
            _merge(deps, b.w)
        for b in accum:
            _merge(deps, b.r)
        return deps

    def _commit(self, tok, reads, writes, accum):
        for b in reads:
            _merge(b.r, tok)
        for b in writes:
            b.w = dict(tok)
            b.r = {}
        for b in accum:
            _merge(b.w, tok)
            b.r = {}

    def op(self, eng, fns, reads=(), writes=(), accum=()):
        if callable(fns):
            fns = [fns]
        deps = self._deps(reads, writes, accum)
        self._emit_waits(eng, deps)
        for f in fns[:-1]:
            self.q[eng].append(("ins", f, None, 0))
        s = self.esem[eng]
        self.ecount[eng] += 1
        self.q[eng].append(("ins", fns[-1], s.h, 1))
        tok = {s: self.ecount[eng]}
        self.waited[eng][s] = max(self.waited[eng].get(s, 0), 0)
        self._commit(tok, reads, writes, accum)

    def dma(self, eng, fn, sbuf, reads=(), writes=(), accum=()):
        if sbuf.name not in self.dsems:
            self.dsems[sbuf.name] = [self.new_sem("d_" + sbuf.name), 0]
        ent = self.dsems[sbuf.name]
        deps = self._deps(reads, writes, accum)
        self._emit_waits(eng, deps)
        ent[1] += 16
        self.q[eng].append(("ins", fn, ent[0].h, 16))
        tok = {ent[0]: ent[1]}
        self._commit(tok, reads, writes, accum)

    def barrier(self):
        toks = {self.esem[e]: self.ecount[e] for e in self.esem if self.ecount[e] > 0}
        for name, (sem, cnt) in self.dsems.items():
            if cnt > 0:
                toks[sem] = cnt
        for e in self.ENG:
            self._emit_waits(e, toks)

    def final_wait(self, eng, bufs):
        deps = {}
        for b in bufs:
            _merge(deps, b.w)
            _merge(deps, b.r)
        self._emit_waits(eng, deps)

    def flush(self, block):
        nc = self.nc

        def run(e, items):
            for it in items:
                if it[0] == "wait":
                    e.wait_ge(it[1], it[2])
                else:
                    ins = it[1](e)
                    if it[2] is not None:
                        ins.then_inc(it[2], it[3])

        q = self.q

        @block.tensor
        def _(e):
            run(e, q["pe"])

        @block.scalar
        def _(e):
            run(e, q["act"])

        @block.vector
        def _(e):
            run(e, q["dve"])

        @block.gpsimd
        def _(e):
            run(e, q["pool"])

        @block.sync
        def _(e):
            run(e, q["sp"])


def _segments(layer_is_even):
    segs = []
    if layer_is_even:
        for base, gh in ((0, 0), (4096, 8)):
            segs += [("q", gh + i) for i in range(8)]
            segs += [("k", gh + i) for i in range(8)]
            segs += [("v", gh + i) for i in range(8)]
            segs += [("g", gh + i) for i in range(8)]
    else:
        segs += [("q", i) for i in range(8)]
        segs += [("k", i) for i in range(2)]
        segs += [("v", i) for i in range(2)]
        segs += [("g", i) for i in range(8)]
        segs += [("qd", 8 + i) for i in range(8)]
        segs += [("kd", 2 + i) for i in range(2)]
        segs += [("v", 2 + i) for i in range(2)]
        segs += [("g", 8 + i) for i in range(8)]
    return segs


def _plan_a():
    plan = []
    for qt in range(NTT):
        l = []
        for kb in range(NTB):
            e = qt * 512 - kb * 128
            if -1535 <= e <= 1151:
                l.append((kb, [(0, 512, 0, e + OFF_A)]))
        plan.append(l)
    return plan


def _plan_c():
    plan = []
    for qt in range(NTT):
        l = []
        for kb in range(max(0, 4 * qt - 1), min(15, 4 * qt + 4) + 1):
            e = qt * 512 - kb * 128
            l.append((kb, [(0, 512, 0, e + OFF_C)]))
        plan.append(l)
    return plan


def _plan_b():
    plan = []
    for qt in range(NTT):
        l = []
        for j in range(max(0, 4 * qt - 2), min(15, 4 * qt + 5) + 1):
            c0 = (8 * qt - 2 * j - REL_MIN) * 64
            if qt == 0:
                which = 1 if (2 * j + 1) <= 7 else 0
                segs = [(0, 256, which, c0), (256, 512, 0, c0 + 256)]
            elif qt == NTT - 1:
                which = 1 if (2 * j) >= 24 else 0
                segs = [(0, 320, 0, c0), (320, 512, which, c0 + 320)]
            else:
                segs = [(0, 512, 0, c0)]
            l.append((j, segs))
        plan.append(l)
    return plan


def _plan_d():
    return [[(kb, None) for kb in range(NTB)] for _ in range(NTT)]


def build_program(n_layers=DEPTH, layers=None):
    nc = bass.Bass("TRN2", target_bir_lowering=False)

    def din(name, shape, dt=F32):
        return nc.dram_tensor(name, list(shape), dt, kind="ExternalInput").ap()

    x = din("x", [SEQ, D_MODEL])
    gam = din("gam", [DEPTH + 1, 128, D_MODEL])
    w_in_even = din("w_in_even", [2, D_MODEL, 8192])
    w_out_even = din("w_out_even", [2, D_MODEL, D_MODEL])
    w_in_odd = din("w_in_odd", [2, D_MODEL, 5120])
    w_out_odd = din("w_out_odd", [2, D_MODEL, D_MODEL])
    sinks = din("sinks", [2, 128, 128])
    qn = din("qn", [2, 128, 128])
    kn = din("kn", [2, 128, 128])
    bias_a = din("bias_a", [8, 128, W_A])
    bias_c = din("bias_c", [8, 128, W_C])
    strip_b = din("strip_b", [2, 8, 2, 128, W_B])
    rope = din("rope", [2, 128, SEQ])
    rperm_d = din("rperm", [128, 128])
    ident_d = din("ident", [128, 128])
    out = nc.dram_tensor("out", [SEQ, D_MODEL], F32, kind="ExternalOutput").ap()

    hbuf = nc.dram_tensor("hbuf", [SEQ, D_MODEL], F32).ap()
    qs = nc.dram_tensor("qs", [16, 128, SEQ], BF16).ap()
    ks = nc.dram_tensor("ks", [16, 128, SEQ], BF16).ap()
    gs = nc.dram_tensor("gs", [16, 128, SEQ], BF16).ap()
    vs = nc.dram_tensor("vs", [SEQ, 16 * 128], BF16).ap()

    with ExitStack() as stack:
        B = Builder(nc, stack)

        uid = [0]

        def uname(name):
            uid[0] += 1
            return f"{name}_u{uid[0]}"

        def sb(name, shape, dt):
            return stack.enter_context(nc.sbuf_tensor(uname(name), list(shape), dt))

        def ps(name, shape, dt):
            return stack.enter_context(nc.psum_tensor(uname(name), list(shape), dt))

        ybuf = sb("ybuf", [128, NCH, SEQ], BF16)
        ybufs = [Buf(f"ybuf{t}") for t in range(NTT)]
        NW = 2
        GW = 8
        wbuf = [sb(f"wbuf{i}", [128, NCH, GW * 128], BF16) for i in range(NW)]
        wbufs = [Buf(f"wbuf{i}") for i in range(NW)]
        ones_bf = sb("ones_bf", [128, 128], BF16)
        ident = sb("ident", [128, 128], BF16)
        rperm = sb("rperm_s", [128, 128], BF16)
        cbuf = Buf("consts")
        eps_t = sb("eps_t", [128, 1], F32)

        NPS = 8
        psf = [ps(f"psf{i}", [128, 512], F32) for i in range(NPS)]
        psfs = [Buf(f"psf{i}") for i in range(NPS)]
        pst = [psf[6].bitcast(BF16), psf[7].bitcast(BF16)]
        psts = [psfs[6], psfs[7]]

        hsrc_b = Buf("hres")
        qs_b = [Buf(f"qs{i}") for i in range(16)]
        ks_b = [Buf(f"ks{i}") for i in range(16)]
        gs_b = [Buf(f"gs{i}") for i in range(16)]
        vs_b = [Buf(f"vs{i}") for i in range(16)]
        out_b = Buf("out")

        B.op("dve", [lambda e: e.memset(ones_bf[:], 1.0),
                     lambda e: e.memset(eps_t[:], EPS)], writes=[cbuf])
        rp_b = Buf("rperm")
        B.dma("pool", lambda e: e.dma_start(out=rperm[:], in_=rperm_d), rp_b, writes=[rp_b])

        state = {"ps": 0}

        def next_ps():
            i = state["ps"] % NPS
            state["ps"] += 1
            return i

        def phase_scope():
            return ExitStack()

        def rmsnorm_phase(src, src_b, gidx, final):
            with phase_scope() as sc:
                def sbp(name, shape, dt):
                    return sc.enter_context(nc.sbuf_tensor(uname(name), list(shape), dt))
                NHB = 3
                hb = [sbp(f"hb{i}", [128, D_MODEL], F32) for i in range(NHB)]
                hbs = [Buf(f"hb{i}") for i in range(NHB)]
                sq = sbp("sq", [128, D_MODEL], F32)
                sqs = Buf("sq")
                gm = sbp("gm", [128, D_MODEL], F32)
                gms = Buf("gm")
                st = [sbp(f"st{i}", [128, 4], F32) for i in range(NHB)]
                sts = [Buf(f"st{i}") for i in range(NHB)]
                if final:
                    yb = [sbp(f"yo{i}", [128, D_MODEL], F32) for i in range(2)]
                else:
                    yb = [sbp(f"yb{i}", [128, D_MODEL], BF16) for i in range(2)]
                ybs = [Buf(f"yb{i}") for i in range(2)]
                B.dma("sp", lambda e: e.dma_start(out=gm[:], in_=gam[gidx]), gms, writes=[gms])
                def stage_a1(tb):
                    i = tb % NHB
                    B.dma("sp", lambda e, i=i, tb=tb: e.dma_start(out=hb[i][:], in_=src[tb * 128:(tb + 1) * 128, :]),
                          hbs[i], reads=[src_b], writes=[hbs[i]])
                    B.op("act", lambda e, i=i: e.activation(out=sq[:], in_=hb[i][:], func=AF.Square),
                         reads=[hbs[i]], writes=[sqs])
                    B.op("dve", lambda e, i=i: e.reduce_sum(out=st[i][:, 0:1], in_=sq[:], axis=AX.X),
                         reads=[sqs], writes=[sts[i]])

                def stage_a2(tb):
                    i = tb % NHB
                    j = tb % 2
                    B.op("act", lambda e, i=i: e.activation(out=st[i][:, 1:2], in_=st[i][:, 0:1], func=AF.Sqrt,
                                                            scale=1.0 / D_MODEL, bias=eps_t[:, 0:1]),
                         reads=[cbuf], writes=[sts[i]])
                    B.op("dve", lambda e, i=i: e.reciprocal(out=st[i][:, 2:3], in_=st[i][:, 1:2]), writes=[sts[i]])
                    B.op("dve", lambda e, i=i, j=j: e.scalar_tensor_tensor(out=yb[j][:], in0=hb[i][:], scalar=st[i][:, 2:3],
                                                                           in1=gm[:], op0=ALU.mult, op1=ALU.mult),
                         reads=[hbs[i], sts[i], gms], writes=[ybs[j]])

                def stage_b(tb):
                    i = tb % 2
                    if final:
                        B.dma("pool", lambda e, i=i, tb=tb: e.dma_start(out=out[tb * 128:(tb + 1) * 128, :], in_=yb[i][:]),
                              ybs[i], reads=[ybs[i]], accum=[out_b])
                    else:
                        tt = tb // 4
                        for c4 in range(4):
                            pi = (tb * 4 + c4) % 2
                            fns = []
                            for k in range(4):
                                cc = c4 * 4 + k
                                fns.append(lambda e, pi=pi, k=k, cc=cc, i=i: e.transpose(
                                    pst[pi][:, k * 128:(k + 1) * 128], yb[i][:, cc * 128:(cc + 1) * 128], ident[:]))
                            B.op("pe", fns, reads=[ybs[i], idb_], writes=[psts[pi]])
                            dst = ybuf[:, c4 * 4:(c4 + 1) * 4, tb * 128:(tb + 1) * 128]
                            srcp = pst[pi][:, 0:512].rearrange("p (k t) -> p k t", k=4)
                            B.op("act", lambda e, dst=dst, srcp=srcp: e.activation(out=dst, in_=srcp, func=AF.Copy),
                                 reads=[psts[pi]], accum=[ybufs[tt]])

                for tb in range(NTB + 1):
                    if tb < NTB:
                        stage_a1(tb)
                    if tb >= 1:
                        stage_b(tb - 1)
                    if tb < NTB:
                        stage_a2(tb)

        def inproj_phase(w, is_even, li):
            segs = _segments(is_even)
            ngroups = len(segs) // GW
            with phase_scope() as sc:
                def sbp(name, shape, dt):
                    return sc.enter_context(nc.sbuf_tensor(uname(name), list(shape), dt))
                NO = 4
                ot = [sbp(f"ot{i}", [128, 512], BF16) for i in range(NO)]
                ots = [Buf(f"ot{i}") for i in range(NO)]
                oi = [0]
                if not is_even:
                    cs_t = sbp("cs_t", [128, 2, SEQ], F32)
                    cs_b = Buf("cs_t")
                    B.dma("sp", lambda e: e.dma_start(out=cs_t[:], in_=rope.rearrange("a p t -> p a t")), cs_b, writes=[cs_b])
                    gq = sbp("gq", [128, 2, 128], F32)
                    gq_b = Buf("gq")
                    B.dma("sp", lambda e: e.dma_start(out=gq[:, 0, :], in_=qn[li]), gq_b, accum=[gq_b])
                    B.dma("sp", lambda e: e.dma_start(out=gq[:, 1, :], in_=kn[li]), gq_b, accum=[gq_b])
                    tmp = [sbp(f"tmp{i}", [128, 512], F32) for i in range(5)]
                    tmps = [Buf(f"tmp{i}") for i in range(5)]
                    sqh = sbp("sqh", [128, 512], BF16)
                    xhi = sbp("xhi", [128, 512], BF16)
                    xlo = sbp("xlo", [128, 512], BF16)
                    xhib, xlob, sqhb = Buf("xhi"), Buf("xlo"), Buf("sqh")

                NSW = 12
                stg = [sbp(f"stg{i}", [128, NCH - NSW, GW * 128], F32) for i in range(2)]
                stgb = [Buf(f"stg{i}") for i in range(2)]

                def prefetch(g):
                    wi = g % NW
                    si_ = g % 2
                    wv = w[:, g * GW * 128:(g + 1) * GW * 128].rearrange("(cc p) f -> p cc f", p=128)
                    B.dma("pool", lambda e: e.dma_start(out=wbuf[wi][:, 0:NSW, :], in_=wv[:, 0:NSW, :]),
                          wbufs[wi], writes=[wbufs[wi]])
                    B.dma("sp", lambda e: e.dma_start(out=stg[si_][:], in_=wv[:, NSW:NCH, :]),
                          stgb[si_], writes=[stgb[si_]])
                    B.op("pool", lambda e: e.tensor_copy(out=wbuf[wi][:, NSW:NCH, :], in_=stg[si_][:]),
                         reads=[stgb[si_]], accum=[wbufs[wi]])

                prefetch(0)
                for g in range(ngroups):
                    wi = g % NW
                    if g + 1 < ngroups:
                        prefetch(g + 1)
                    gsegs = segs[g * GW:(g + 1) * GW]
                    k = 0
                    while k < GW:
                        typ, idx = gsegs[k]
                        if 'noqd' in DBG and typ in ('qd', 'kd'):
                            typ = typ[0]
                        if typ == "v":
                            n = 1
                            while n < 4 and k + n < GW and gsegs[k + n][0] == "v" and gsegs[k + n][1] == idx + n:
                                n += 1
                            ncol = 128 * n
                            for tb in range(NTB):
                                p = next_ps()
                                fns = [lambda e, p=p, cc=cc, tb=tb, k=k, ncol=ncol, wi=wi: e.matmul(
                                    psf[p][:, 0:ncol], lhsT=ybuf[:, cc, tb * 128:(tb + 1) * 128],
                                    rhs=wbuf[wi][:, cc, k * 128:k * 128 + ncol], start=(cc == 0), stop=(cc == NCH - 1))
                                    for cc in range(NCH)]
                                B.op("pe", fns, reads=[ybufs[tb // 4], wbufs[wi]], writes=[psfs[p]])
                                o = oi[0] % NO
                                oi[0] += 1
                                B.op("dve", lambda e, o=o, p=p, ncol=ncol: e.tensor_copy(out=ot[o][:, 0:ncol], in_=psf[p][:, 0:ncol]),
                                     reads=[psfs[p]], writes=[ots[o]])
                                B.dma("sp", lambda e, o=o, tb=tb, idx=idx, ncol=ncol: e.dma_start(
                                    out=vs[tb * 128:(tb + 1) * 128, idx * 128:idx * 128 + ncol], in_=ot[o][:, 0:ncol]),
                                    ots[o], reads=[ots[o]], accum=[vs_b[idx + j] for j in range(n)])
                            k += n
                            continue
                        for tt in range(NTT):
                            p = next_ps()
                            fns = [lambda e, p=p, cc=cc, tt=tt, k=k, wi=wi: e.matmul(
                                psf[p][:], lhsT=wbuf[wi][:, cc, k * 128:(k + 1) * 128],
                                rhs=ybuf[:, cc, tt * 512:(tt + 1) * 512], start=(cc == 0), stop=(cc == NCH - 1))
                                for cc in range(NCH)]
                            B.op("pe", fns, reads=[ybufs[tt], wbufs[wi]], writes=[psfs[p]])
                            o = oi[0] % NO
                            oi[0] += 1
                            if typ == "q":
                                B.op("dve", lambda e, o=o, p=p: e.tensor_scalar(out=ot[o][:], in0=psf[p][:], scalar1=SCALE,
                                                                               scalar2=None, op0=ALU.mult),
                                     reads=[psfs[p]], writes=[ots[o]])
                                dst, dstb = qs, qs_b
                            elif typ == "k":
                                B.op("act", lambda e, o=o, p=p: e.activation(out=ot[o][:], in_=psf[p][:], func=AF.Copy),
                                     reads=[psfs[p]], writes=[ots[o]])
                                dst, dstb = ks, ks_b
                            elif typ == "g":
                                B.op("act", lambda e, o=o, p=p: e.activation(out=ot[o][:], in_=psf[p][:], func=AF.Silu),
                                     reads=[psfs[p]], writes=[ots[o]])
                                dst, dstb = gs, gs_b
                            else:
                                isq = typ == "qd"
                                gcol = 0 if isq else 1
                                osc = SCALE if isq else 1.0
                                xg, sqx, t1, t2, rs = tmp
                                xgb, sqxb, t1b, t2b, rsb = tmps
                                sqb = sqhb
                                B.op("dve", lambda e, p=p, gcol=gcol: e.tensor_scalar(
                                    out=xg[:], in0=psf[p][:], scalar1=gq[:, gcol, 0:1], scalar2=None, op0=ALU.mult),
                                    reads=[psfs[p], gq_b], writes=[xgb])
                                B.op("act", lambda e, p=p: e.activation(out=sqh[:], in_=psf[p][:], func=AF.Square),
                                     reads=[psfs[p], xgb], writes=[sqb])
                                B.op("dve", lambda e: e.tensor_copy(out=xhi[:], in_=xg[:]), reads=[xgb], writes=[xhib])
                                B.op("dve", lambda e: e.tensor_tensor(out=xlo[:], in0=xg[:], in1=xhi[:], op=ALU.subtract),
                                     reads=[xgb, xhib], writes=[xlob])
                                p2 = next_ps()
                                B.op("pe", lambda e, p2=p2: e.matmul(psf[p2][:], lhsT=ones_bf[:], rhs=sqh[:], start=True, stop=True),
                                     reads=[sqb, cbuf], writes=[psfs[p2]])
                                p3 = next_ps()
                                B.op("pe", [lambda e, p3=p3: e.matmul(psf[p3][:], lhsT=rperm[:], rhs=xhi[:], start=True, stop=False),
                                            lambda e, p3=p3: e.matmul(psf[p3][:], lhsT=rperm[:], rhs=xlo[:], start=False, stop=True)],
                                     reads=[xhib, xlob, rp_b], writes=[psfs[p3]])
                                B.op("dve", lambda e, p2=p2: e.tensor_scalar(out=t2[:], in0=psf[p2][:], scalar1=1.0 / HD,
                                                                             scalar2=EPS, op0=ALU.mult, op1=ALU.add),
                                     reads=[psfs[p2]], writes=[t2b])
                                B.op("act", lambda e: e.activation(out=rs[:], in_=t2[:], func=AF.Ln),
                                     reads=[t2b], writes=[rsb])
                                B.op("act", lambda e: e.activation(out=sqx[:], in_=rs[:], func=AF.Exp, scale=-0.5),
                                     reads=[rsb], writes=[sqxb])
                                B.op("dve", lambda e, tt=tt: e.tensor_tensor(out=t1[:], in0=xg[:], in1=cs_t[:, 0, tt * 512:(tt + 1) * 512],
                                                                            op=ALU.mult),
                                     reads=[xgb, cs_b], writes=[t1b])
                                B.op("dve", lambda e, tt=tt, p3=p3: e.tensor_tensor(out=t2[:], in0=psf[p3][:],
                                                                                   in1=cs_t[:, 1, tt * 512:(tt + 1) * 512], op=ALU.mult),
                                     reads=[psfs[p3], cs_b], writes=[t2b])
                                B.op("dve", lambda e: e.tensor_tensor(out=rs[:], in0=t1[:], in1=t2[:], op=ALU.add),
                                     reads=[t1b, t2b], writes=[rsb])
                                B.op("dve", lambda e, o=o, osc=osc: e.scalar_tensor_tensor(
                                    out=ot[o][:], in0=rs[:], scalar=osc, in1=sqx[:], op0=ALU.mult, op1=ALU.mult),
                                    reads=[rsb, sqxb], writes=[ots[o]])
                                dst, dstb = (qs, qs_b) if isq else (ks, ks_b)
                            B.dma("sp", lambda e, o=o, dst=dst, idx=idx, tt=tt: e.dma_start(
                                out=dst[idx, :, tt * 512:(tt + 1) * 512], in_=ot[o][:]),
                                ots[o], reads=[ots[o]], accum=[dstb[idx]])
                        k += 1

        def attention_phase(is_even, li):
            with phase_scope() as sc:
                def sbp(name, shape, dt):
                    return sc.enter_context(nc.sbuf_tensor(uname(name), list(shape), dt))
                qt_ = [sbp(f"qh{i}", [128, SEQ], BF16) for i in range(2)]
                qtb = [Buf(f"qh{i}") for i in range(2)]
                gt_ = [sbp(f"gh{i}", [128, SEQ], BF16) for i in range(2)]
                gtb = [Buf(f"gh{i}") for i in range(2)]
                kt_ = [sbp(f"kh{i}", [128, SEQ], BF16) for i in range(2)]
                ktb = [Buf(f"kh{i}") for i in range(2)]
                vt_ = [sbp(f"vh{i}", [128, NTB, 128], BF16) for i in range(2)]
                vtb = [Buf(f"vh{i}") for i in range(2)]
                WS = W_A if is_even else W_C
                sS1 = sbp("sS0", [128, WS], F32)
                sS = [sS1, sS1]
                sSb1 = Buf("sS0")
                sSb = [sSb1, sSb1]
                eS = [sbp(f"eS{i}", [128, WS], BF16) for i in range(2)]
                sAb = [Buf(f"eS{i}") for i in range(2)]
                sBb = sAb
                sA = [t[:].rearrange("p (a w) -> p a w", a=1) for t in eS]
                if is_even:
                    sBst = [t[:, 0:2 * W_B].rearrange("p (a w) -> p a w", a=2) for t in sS]
                    sB = [t[:, 0:2 * W_B].rearrange("p (a w) -> p a w", a=2) for t in eS]
                LOOK = 3
                NT = 4
                tt_ = [sbp(f"tT{i}", [128, 512], BF16) for i in range(NT)]
                ttb = [Buf(f"tT{i}") for i in range(NT)]
                NP = LOOK + 2
                pt_ = [sbp(f"pT{i}", [128, 512], BF16) for i in range(NP)]
                ptb = [Buf(f"pT{i}") for i in range(NP)]
                rd_ = [sbp(f"rd{i}", [128, 512], F32) for i in range(2)]
                rdb = [Buf(f"rd{i}") for i in range(2)]
                if not is_even:
                    of_ = [sbp(f"of{i}", [128, 512], F32) for i in range(2)]
                    ofb = [Buf(f"of{i}") for i in range(2)]
                gp_ = [sbp(f"gp{i}", [128, 512], F32) for i in range(2)]
                gpb = [Buf(f"gp{i}") for i in range(2)]
                lt1 = sbp("lt0", [128, 512], F32)
                lt_ = [lt1, lt1]
                ltb1 = Buf("lt0")
                ltb = [ltb1, ltb1]
                if not is_even:
                    sk0 = sbp("sk0", [128, 128], F32)
                    sk = sbp("sk", [128, 128], F32)
                    sk0b = Buf("sk0")
                    skb = Buf("sk")
                    B.dma("sp", lambda e: e.dma_start(out=sk0[:], in_=sinks[li]), sk0b, writes=[sk0b])
                    B.op("act", lambda e: e.activation(out=sk[:], in_=sk0[:], func=AF.Exp), reads=[sk0b], writes=[skb])

                S_IDX = [0, 1, 2, 3]
                O_IDX = [4, 5]
                D_IDX = [6, 7]

                tiles = []
                for gh in range(16):
                    if is_even:
                        kind = "A" if gh < 8 else "B"
                        kvh = gh
                    else:
                        kind = "C" if gh < 8 else "D"
                        kvh = gh // 4 if gh < 8 else 2 + (gh - 8) // 4
                    plan = {"A": _plan_a, "B": _plan_b, "C": _plan_c, "D": _plan_d}[kind]()
                    for qt in range(NTT):
                        l = plan[qt]
                        for n, (kb, segs) in enumerate(l):
                            tiles.append((gh, kind, kvh, qt, kb, segs, n == 0, n == len(l) - 1))

                hinfo = []
                ksl_, prev_kvh = -1, -1
                for gh in range(16):
                    if is_even:
                        kind = "A" if gh < 8 else "B"
                        kvh = gh
                    else:
                        kind = "C" if gh < 8 else "D"
                        kvh = gh // 4 if gh < 8 else 2 + (gh - 8) // 4
                    newkv = kvh != prev_kvh
                    if newkv:
                        ksl_ = (ksl_ + 1) % 2
                        prev_kvh = kvh
                    hinfo.append((kind, kvh, ksl_, newkv))
                hcount = [0] * 16
                for t in tiles:
                    hcount[t[0]] += 1

                def issue_loads(gh):
                    kind, kvh, ksl, newkv = hinfo[gh]
                    hs = gh % 2
                    B.dma("sp", lambda e: e.dma_start(out=qt_[hs][:], in_=qs[gh]), qtb[hs],
                          reads=[qs_b[gh]], writes=[qtb[hs]])
                    if newkv:
                        B.dma("sp", lambda e: e.dma_start(out=kt_[ksl][:], in_=ks[kvh]), ktb[ksl],
                              reads=[ks_b[kvh]], writes=[ktb[ksl]])
                        B.dma("sp", lambda e: e.dma_start(
                            out=vt_[ksl][:], in_=vs[:, kvh * 128:(kvh + 1) * 128].rearrange("(tb p) d -> p tb d", p=128)),
                            vtb[ksl], reads=[vs_b[kvh]], writes=[vtb[ksl]])
                    if kind in ("A", "C"):
                        src_ = bias_a if kind == "A" else bias_c
                        B.dma("sp", lambda e: e.dma_start(out=sS[hs][:], in_=src_[gh % 8]),
                              sSb[hs], writes=[sSb[hs]])
                    elif kind == "B":
                        B.dma("sp", lambda e: e.dma_start(
                            out=sBst[hs], in_=strip_b[li, gh - 8].rearrange("a p w -> p a w")),
                            sSb[hs], writes=[sSb[hs]])
                    B.dma("sp", lambda e: e.dma_start(out=gt_[hs][:], in_=gs[gh]), gtb[hs],
                          reads=[gs_b[gh]], writes=[gtb[hs]])

                def issue_strip_exp(gh):
                    kind = hinfo[gh][0]
                    hs = gh % 2
                    if kind in ("A", "C"):
                        B.op("act", lambda e: e.activation(out=eS[hs][:], in_=sS[hs][:], func=AF.Exp),
                             reads=[sSb[hs]], writes=[sAb[hs]])
                    elif kind == "B":
                        B.op("act", lambda e: e.activation(out=eS[hs][:, 0:2 * W_B], in_=sS[hs][:, 0:2 * W_B], func=AF.Exp),
                             reads=[sSb[hs]], writes=[sAb[hs]])

                pend = []
                deferred = []
                DEFER = 3
                ngen = [0]

                def issue_pv(item):
                    (gh, kind, kvh, qt, kb, segs, first, last, pslot, hslot, kslot, qn_) = item
                    oi_ = O_IDX[qn_ % 2]
                    di_ = D_IDX[qn_ % 2]
                    B.op("pe", [lambda e: e.matmul(psf[oi_][:], lhsT=vt_[kslot][:, kb, :], rhs=pt_[pslot][:],
                                                   start=first, stop=last),
                                lambda e: e.matmul(psf[di_][:], lhsT=ones_bf[:], rhs=pt_[pslot][:],
                                                   start=first, stop=last)],
                         reads=[vtb[kslot], ptb[pslot], cbuf],
                         writes=([psfs[oi_], psfs[di_]] if first else []),
                         accum=([] if first else [psfs[oi_], psfs[di_]]))
                    if last:
                        r = qn_ % 2

                        def epi1():
                            if kind == "C":
                                B.op("dve", lambda e: e.tensor_scalar(out=of_[r][:], in0=psf[di_][:], scalar1=sk[:, gh:gh + 1],
                                                                      scalar2=None, op0=ALU.add),
                                     reads=[psfs[di_], skb], writes=[ofb[r]])
                                B.op("act", lambda e: e.activation(out=lt_[r][:], in_=of_[r][:], func=AF.Ln),
                                     reads=[ofb[r]], writes=[ltb[r]])
                            else:
                                B.op("act", lambda e: e.activation(out=lt_[r][:], in_=psf[di_][:], func=AF.Ln),
                                     reads=[psfs[di_]], writes=[ltb[r]])
                            B.op("act", lambda e: e.activation(out=rd_[r][:], in_=lt_[r][:], func=AF.Exp, scale=-1.0),
                                 reads=[ltb[r]], writes=[rdb[r]])
                            B.op("pool", lambda e: e.tensor_tensor(out=gp_[r][:], in0=gt_[hslot][:, qt * 512:(qt + 1) * 512],
                                                                   in1=rd_[r][:], op=ALU.mult),
                                 reads=[gtb[hslot], rdb[r]], writes=[gpb[r]])

                        def epi2():
                            B.op("dve", lambda e: e.tensor_tensor(out=ybuf[:, gh, qt * 512:(qt + 1) * 512], in0=psf[oi_][:],
                                                                  in1=gp_[r][:], op=ALU.mult),
                                 reads=[psfs[oi_], gpb[r]], accum=[ybufs[qt]])
                        deferred.append((ngen[0] + 1, epi1))
                        deferred.append((ngen[0] + 4, epi2))
                        deferred.sort(key=lambda t: t[0])

                issue_loads(0)
                issue_strip_exp(0)
                qtn = -1
                tih = 0
                prev_gh = -1
                for ti, (gh, kind, kvh, qt, kb, segs, first, last) in enumerate(tiles):
                    if gh != prev_gh:
                        prev_gh = gh
                        tih = 0
                    if tih == LOOK + 6 and gh + 1 < 16:
                        issue_loads(gh + 1)
                    if tih == hcount[gh] - 1 and gh + 1 < 16:
                        issue_strip_exp(gh + 1)
                    tih += 1
                    if first:
                        qtn += 1
                    hs, ksl, qn_ = gh % 2, hinfo[gh][2], qtn
                    si = S_IDX[ti % 4]
                    B.op("pe", lambda e, si=si, ksl=ksl, kb=kb, hs=hs, qt=qt: e.matmul(
                        psf[si][:], lhsT=kt_[ksl][:, kb * 128:(kb + 1) * 128], rhs=qt_[hs][:, qt * 512:(qt + 1) * 512],
                        start=True, stop=True),
                        reads=[ktb[ksl], qtb[hs]], writes=[psfs[si]])
                    pslot = ti % NP
                    if segs is None:
                        B.op("act", lambda e, pslot=pslot, si=si: e.activation(out=pt_[pslot][:], in_=psf[si][:], func=AF.Exp),
                             reads=[psfs[si]], writes=[ptb[pslot]])
                    else:
                        tsl = ti % NT
                        B.op("act", lambda e, tsl=tsl, si=si: e.activation(out=tt_[tsl][:], in_=psf[si][:], func=AF.Exp),
                             reads=[psfs[si]], writes=[ttb[tsl]])
                        fns = []
                        for (c0, c1, which, s0) in segs:
                            if kind == "B":
                                bap = sB[hs][:, which, s0:s0 + (c1 - c0)]
                            else:
                                bap = sA[hs][:, 0, s0:s0 + (c1 - c0)]
                            fns.append(lambda e, tsl=tsl, pslot=pslot, c0=c0, c1=c1, bap=bap: e.tensor_tensor(
                                out=pt_[pslot][:, c0:c1], in0=tt_[tsl][:, c0:c1], in1=bap, op=ALU.mult))
                        B.op("dve", fns, reads=[ttb[tsl], sAb[hs]], writes=[ptb[pslot]])
                    ngen[0] += 1
                    pend.append((gh, kind, kvh, qt, kb, segs, first, last, pslot, hs, ksl, qn_))
                    if len(pend) > LOOK:
                        issue_pv(pend.pop(0))
                    while deferred and deferred[0][0] <= ngen[0]:
                        deferred.pop(0)[1]()
                while pend:
                    issue_pv(pend.pop(0))
                while deferred:
                    deferred.pop(0)[1]()

        def wout_load(w, g2):
            wi = g2 % NW
            B.dma("pool", lambda e: e.dma_start(
                out=wbuf[wi][:], in_=w[:, g2 * GW * 128:(g2 + 1) * GW * 128].rearrange("(cc p) f -> p cc f", p=128)),
                wbufs[wi], writes=[wbufs[wi]])

        def outproj_phase(w, src, src_b):
            with phase_scope() as sc:
                def sbp(name, shape, dt):
                    return sc.enter_context(nc.sbuf_tensor(uname(name), list(shape), dt))
                NH = 3
                hi = [sbp(f"hi{i}", [128, 512], F32) for i in range(NH)]
                hib = [Buf(f"hi{i}") for i in range(NH)]
                ho = [sbp(f"ho{i}", [128, 512], F32) for i in range(NH)]
                hob = [Buf(f"ho{i}") for i in range(NH)]
                new_b = Buf("hres_new")
                n = 0
                for og in range(4):
                    wi = (og // 2) % NW
                    co = (og % 2) * 512
                    for tb in range(NTB):
                        s = n % NH
                        n += 1
                        B.dma("sp", lambda e, s=s, tb=tb, og=og: e.dma_start(
                            out=hi[s][:], in_=src[tb * 128:(tb + 1) * 128, og * 512:(og + 1) * 512]),
                            hib[s], reads=[src_b], writes=[hib[s]])
                        p = next_ps()
                        fns = [lambda e, p=p, fc=fc, tb=tb, wi=wi, co=co: e.matmul(
                            psf[p][:], lhsT=ybuf[:, fc, tb * 128:(tb + 1) * 128], rhs=wbuf[wi][:, fc, co:co + 512],
                            start=(fc == 0), stop=(fc == NCH - 1)) for fc in range(NCH)]
                        B.op("pe", fns, reads=[ybufs[tb // 4], wbufs[wi]], writes=[psfs[p]])
                        B.op("dve", lambda e, s=s, p=p: e.tensor_tensor(out=ho[s][:], in0=psf[p][:], in1=hi[s][:], op=ALU.add),
                             reads=[psfs[p], hib[s]], writes=[hob[s]])
                        B.dma("pool", lambda e, s=s, tb=tb, og=og: e.dma_start(
                            out=hbuf[tb * 128:(tb + 1) * 128, og * 512:(og + 1) * 512], in_=ho[s][:]),
                            hob[s], reads=[hob[s]], accum=[new_b])
                return new_b

        def outproj_norm_phase(src, src_b, gidx, final):
            with phase_scope() as sc:
                def sbp(name, shape, dt):
                    return sc.enter_context(nc.sbuf_tensor(uname(name), list(shape), dt))
                hi = [sbp(f"fhi{i}", [128, D_MODEL], F32) for i in range(2)]
                hib = [Buf(f"fhi{i}") for i in range(2)]
                hn = [sbp(f"fhn{i}", [128, D_MODEL], F32) for i in range(2)]
                hnb = [Buf(f"fhn{i}") for i in range(2)]
                sq = sbp("fsq", [128, D_MODEL], F32)
                sqs = Buf("fsq")
                gm = sbp("fgm", [128, D_MODEL], F32)
                gms = Buf("fgm")
                st = [sbp(f"fst{i}", [128, 4], F32) for i in range(2)]
                sts = [Buf(f"fst{i}") for i in range(2)]
                yb = [sbp(f"fyb{i}", [128, D_MODEL], F32 if final else BF16) for i in range(2)]
                ybs = [Buf(f"fyb{i}") for i in range(2)]
                ytb = [Buf(f"ybtb{i}") for i in range(NTB)]
                new_b = Buf("hres_new")
                B.dma("sp", lambda e: e.dma_start(out=gm[:], in_=gam[gidx]), gms, writes=[gms])
                pcount = [0]

                def main(tb):
                    b = tb % 2
                    B.dma("sp", lambda e: e.dma_start(out=hi[b][:], in_=src[tb * 128:(tb + 1) * 128, :]),
                          hib[b], reads=[src_b], writes=[hib[b]])
                    for og in range(4):
                        wi = (og // 2) % NW
                        co = (og % 2) * 512
                        p = pcount[0] % 6
                        pcount[0] += 1
                        fns = [lambda e, p=p, fc=fc, wi=wi, co=co: e.matmul(
                            psf[p][:], lhsT=ybuf[:, fc, tb * 128:(tb + 1) * 128], rhs=wbuf[wi][:, fc, co:co + 512],
                            start=(fc == 0), stop=(fc == NCH - 1)) for fc in range(NCH)]
                        B.op("pe", fns, reads=[ytb[tb], wbufs[wi]], writes=[psfs[p]])
                        B.op("dve", lambda e, p=p, og=og: e.tensor_tensor(
                            out=hn[b][:, og * 512:(og + 1) * 512], in0=psf[p][:], in1=hi[b][:, og * 512:(og + 1) * 512], op=ALU.add),
                            reads=[psfs[p], hib[b]], writes=([hnb[b]] if og == 0 else []), accum=([] if og == 0 else [hnb[b]]))
                    if not final:
                        B.dma("sp", lambda e: e.dma_start(out=hbuf[tb * 128:(tb + 1) * 128, :], in_=hn[b][:]),
                              hnb[b], reads=[hnb[b]], accum=[new_b])
                    B.op("act", lambda e: e.activation(out=sq[:], in_=hn[b][:], func=AF.Square),
                         reads=[hnb[b]], writes=[sqs])
                    B.op("dve", lambda e: e.reduce_sum(out=st[b][:, 0:1], in_=sq[:], axis=AX.X),
                         reads=[sqs], writes=[sts[b]])
                    B.op("act", lambda e: e.activation(out=st[b][:, 1:2], in_=st[b][:, 0:1], func=AF.Sqrt,
                                                       scale=1.0 / D_MODEL, bias=eps_t[:, 0:1]),
                         reads=[cbuf], writes=[sts[b]])
                    B.op("dve", lambda e: e.reciprocal(out=st[b][:, 2:3], in_=st[b][:, 1:2]), writes=[sts[b]])
                    B.op("dve", lambda e: e.scalar_tensor_tensor(out=yb[b][:], in0=hn[b][:], scalar=st[b][:, 2:3],
                                                                 in1=gm[:], op0=ALU.mult, op1=ALU.mult),
                         reads=[hnb[b], sts[b], gms], writes=[ybs[b]])

                def tail(tb):
                    b = tb % 2
                    if final:
                        B.dma("sp", lambda e: e.dma_start(out=out[tb * 128:(tb + 1) * 128, :], in_=yb[b][:]),
                              ybs[b], reads=[ybs[b]], accum=[out_b])
                        return
                    for c4 in range(4):
                        pi = (tb * 4 + c4) % 2
                        fns = []
                        for k in range(4):
                            cc = c4 * 4 + k
                            fns.append(lambda e, pi=pi, k=k, cc=cc: e.transpose(
                                pst[pi][:, k * 128:(k + 1) * 128], yb[b][:, cc * 128:(cc + 1) * 128], ident[:]))
                        B.op("pe", fns, reads=[ybs[b], idb_], writes=[psts[pi]])
                        dst = ybuf[:, c4 * 4:(c4 + 1) * 4, tb * 128:(tb + 1) * 128]
                        srcp = pst[pi][:, 0:512].rearrange("p (k t) -> p k t", k=4)
                        B.op("act", lambda e, dst=dst, srcp=srcp: e.activation(out=dst, in_=srcp, func=AF.Copy),
                             reads=[psts[pi]], writes=([ytb[tb]] if c4 == 0 else []), accum=([] if c4 == 0 else [ytb[tb]]))

                for tb in range(NTB + 1):
                    if tb < NTB:
                        main(tb)
                    if tb >= 1:
                        tail(tb - 1)
                return new_b

        idb_ = Buf("ident")
        B.dma("pool", lambda e: e.dma_start(out=ident[:], in_=ident_d), idb_, writes=[idb_])
        B.barrier()

        src, src_b = x, Buf("x")
        layer_list = list(layers if layers is not None else range(n_layers))
        first_layer = True
        for pos, layer in enumerate(layer_list):
            is_even = layer % 2 == 0
            li = layer // 2
            w_in = (w_in_even if is_even else w_in_odd)[li]
            w_out = (w_out_even if is_even else w_out_odd)[li]
            if first_layer:
                rmsnorm_phase(src, src_b, layer, final=False)
                B.barrier()
                first_layer = False
            inproj_phase(w_in, is_even, li)
            B.barrier()
            for g2 in range(2):
                wout_load(w_out, g2)
            if 'noattn' not in DBG:
                attention_phase(is_even, li)
            B.barrier()
            is_last = pos == len(layer_list) - 1
            new_b = outproj_norm_phase(src, src_b, DEPTH if is_last else layer_list[pos + 1], final=is_last)
            B.barrier()
            src, src_b = hbuf, new_b
        B.final_wait("sp", [out_b])

        with nc.Block() as block:
            B.flush(block)
    return nc


_CACHE = {}


def _prep_inputs(x, norm_g, final_g, w_in_even, w_out_even, rpb_b, w_in_odd, w_out_odd,
                 sinks_c, qnorm_d, knorm_d):
    f = lambda a: np.ascontiguousarray(np.asarray(a, dtype=np.float32))
    gam = np.concatenate([f(norm_g), f(final_g)[None, :]], 0)
    gam = np.ascontiguousarray(np.broadcast_to(gam[:, None, :], (DEPTH + 1, 128, D_MODEL)))
    common = {
        "gam": gam,
        "w_in_even": f(w_in_even), "w_out_even": f(w_out_even),
        "w_in_odd": f(w_in_odd), "w_out_odd": f(w_out_odd),
        "sinks": np.ascontiguousarray(np.broadcast_to(np.tile(f(sinks_c), (1, 16))[:, None, :], (2, 128, 128))),
        "qn": np.ascontiguousarray(np.broadcast_to(f(qnorm_d)[:, :, None], (2, 128, 128))),
        "kn": np.ascontiguousarray(np.broadcast_to(f(knorm_d)[:, :, None], (2, 128, 128))),
        "bias_a": make_bias_a(), "bias_c": make_bias_c(),
        "strip_b": make_strip_b(f(rpb_b)),
        "rope": make_rope(), "rperm": make_rperm(), "ident": np.eye(128, dtype=np.float32),
    }
    return common


def kernel(x, norm_g, final_g, w_in_even, w_out_even, rpb_b, w_in_odd, w_out_odd,
           sinks_c, qnorm_d, knorm_d):
    x = np.asarray(x, dtype=np.float32)
    common = _prep_inputs(x, norm_g, final_g, w_in_even, w_out_even, rpb_b, w_in_odd, w_out_odd,
                          sinks_c, qnorm_d, knorm_d)
    if "nc" not in _CACHE:
        _CACHE["nc"] = build_program(DEPTH)
    nc = _CACHE["nc"]
    in_maps = [dict(common, x=np.ascontiguousarray(x[b])) for b in range(BATCH)]
    res = run_bass_kernel_spmd(nc, in_maps, core_ids=list(range(BATCH)))
    return np.stack([np.asarray(res.results[b]["out"], dtype=np.float32) for b in range(BATCH)], 0)
```

```python
import math
import os
DBG = os.environ.get('KDBG', '')
from contextlib import ExitStack

import numpy as np
import concourse.bass as bass
import concourse.mybir as mybir
from concourse.bass_utils import run_bass_kernel_spmd

F32 = mybir.dt.float32
BF16 = mybir.dt.bfloat16
AF = mybir.ActivationFunctionType
ALU = mybir.AluOpType
AX = mybir.AxisListType

D_MODEL = 2048
SEQ = 2048
BATCH = 4
DEPTH = 4
HD = 128
NCH = D_MODEL // 128
NTB = SEQ // 128
NTT = SEQ // 512
EPS = 1e-6
SCALE = HD ** -0.5
NEG = -30000.0

OFF_A, W_A = 1408, 2944
OFF_C, W_C = 512, 1152
REL_MIN, W_B = -10, 22 * 64
W_STRIP = W_A


def _alibi_slopes(n):
    return (2.0 ** (-8.0 * np.arange(1, n + 1, dtype=np.float32) / n)).astype(np.float32)


def _toeplitz_strip(f, off, width):
    k = np.arange(128)[:, None]
    j = np.arange(width)[None, :]
    return f(k - j + off)


def make_bias_a():
    slopes = _alibi_slopes(8)
    out = np.empty((8, 128, W_A), np.float32)
    for h in range(8):
        def f(d, h=h):
            ad = np.abs(d)
            mult = (ad <= 64).astype(np.float32)
            mult += ((d % 4 == 0) & (ad <= 256)).astype(np.float32)
            mult += ((d % 16 == 0) & (ad <= 1024)).astype(np.float32)
            val = -slopes[h] * ad.astype(np.float32) + np.log(np.maximum(mult, 1.0)).astype(np.float32)
            return np.where(mult > 0, val, NEG).astype(np.float32)
        out[h] = _toeplitz_strip(f, OFF_A, W_A)
    return out


def make_bias_c():
    slopes = _alibi_slopes(8)
    out = np.empty((8, 128, W_C), np.float32)
    for h in range(8):
        def f(d, h=h):
            ad = np.abs(d)
            val = -slopes[h] * ad.astype(np.float32)
            return np.where(ad <= 128, val, NEG).astype(np.float32)
        out[h] = _toeplitz_strip(f, OFF_C, W_C)
    return out


def make_strip_b(rpb):
    n = rpb.shape[0]
    p = np.arange(128)
    b = (p // 64)[:, None]
    cp = (p % 64)[:, None]
    col = np.arange(W_B)[None, :]
    rel = col // 64 + REL_MIN
    c = col % 64
    cs = np.clip(c - 8, 0, 48)
    colvalid = (cp >= cs) & (cp < cs + 16)
    dcol = np.clip(cp - c + 15, 0, 30)
    dr = b - rel + 7
    drc = np.clip(dr, 0, 14)
    rowvalid_i = ((b - rel) >= -4) & ((b - rel) <= 3)
    rowvalid_u = (dr >= 0) & (dr <= 14)
    out = np.empty((n, 8, 2, 128, W_B), np.float32)
    for i in range(n):
        for h in range(8):
            g = rpb[i, h][drc, dcol]
            out[i, h, 0] = np.where(colvalid & rowvalid_i, g, NEG)
            out[i, h, 1] = np.where(colvalid & rowvalid_u, g, NEG)
    return out


def make_rope():
    t = np.arange(SEQ)
    hn = 32
    inv = (10000.0 ** (-np.arange(hn, dtype=np.float32) / hn)).astype(np.float32)
    d = np.arange(128)
    pos = np.where(d[:, None] < 64, (t // 64)[None, :], (t % 64)[None, :]).astype(np.float32)
    ang = pos * inv[d % 32][:, None]
    return np.stack([np.cos(ang), np.sin(ang)]).astype(np.float32)


def make_rperm():
    r = np.zeros((128, 128), np.float32)
    for m in range(128):
        if (m % 64) < 32:
            r[m + 32, m] = -1.0
        else:
            r[m - 32, m] = 1.0
    return r


class Sem:
    def __init__(self, h, name):
        self.h = h
        self.name = name


class Buf:
    def __init__(self, name):
        self.name = name
        self.w = {}
        self.r = {}
        self.dsem = None
        self.dcount = 0


def _merge(dst, src):
    for s, v in src.items():
        if dst.get(s, 0) < v:
            dst[s] = v


class Builder:
    ENG = ("pe", "act", "dve", "pool", "sp")

    def __init__(self, nc, stack):
        self.nc = nc
        self.stack = stack
        self.q = {e: [] for e in self.ENG}
        self.esem = {}
        self.ecount = {}
        for e in ("pe", "act", "dve", "pool"):
            self.esem[e] = self.new_sem("es_" + e)
            self.ecount[e] = 0
        self.waited = {e: {} for e in self.ENG}
        self.dsems = {}

    def new_sem(self, name):
        return Sem(self.stack.enter_context(self.nc.semaphore(name)), name)

    def _emit_waits(self, eng, deps):
        wd = self.waited[eng]
        for s, v in deps.items():
            if wd.get(s, 0) < v:
                wd[s] = v
                self.q[eng].append(("wait", s.h, v))

    def _deps(self, reads, writes, accum):
        deps = {}
        for b in reads:
            _merge(deps, b.w)
        for b in writes:
            _merge(deps, b.r)
            _merge(deps, b.w)
        for b in accum:
            _merge(deps, b.r)
        return deps

    def _commit(self, tok, reads, writes, accum):
        for b in reads:
            _merge(b.r, tok)
        for b in writes:
            b.w = dict(tok)
            b.r = {}
        for b in accum:
            _merge(b.w, tok)
            b.r = {}

    def op(self, eng, fns, reads=(), writes=(), accum=()):
        if callable(fns):
            fns = [fns]
        deps = self._deps(reads, writes, accum)
        self._emit_waits(eng, deps)
        for f in fns[:-1]:
            self.q[eng].append(("ins", f, None, 0))
        s = self.esem[eng]
        self.ecount[eng] += 1
        self.q[eng].append(("ins", fns[-1], s.h, 1))
        tok = {s: self.ecount[eng]}
        self.waited[eng][s] = max(self.waited[eng].get(s, 0), 0)
        self._commit(tok, reads, writes, accum)

    def dma(self, eng, fn, sbuf, reads=(), writes=(), accum=()):
        if sbuf.name not in self.dsems:
            self.dsems[sbuf.name] = [self.new_sem("d_" + sbuf.name), 0]
        ent = self.dsems[sbuf.name]
        deps = self._deps(reads, writes, accum)
        self._emit_waits(eng, deps)
        ent[1] += 16
        self.q[eng].append(("ins", fn, ent[0].h, 16))
        tok = {ent[0]: ent[1]}
        self._commit(tok, reads, writes, accum)

    def barrier(self):
        toks = {self.esem[e]: self.ecount[e] for e in self.esem if self.ecount[e] > 0}
        for name, (sem, cnt) in self.dsems.items():
            if cnt > 0:
                toks[sem] = cnt
        for e in self.ENG:
            self._emit_waits(e, toks)

    def final_wait(self, eng, bufs):
        deps = {}
        for b in bufs:
            _merge(deps, b.w)
            _merge(deps, b.r)
        self._emit_waits(eng, deps)

    def flush(self, block):
        nc = self.nc

        def run(e, items):
            for it in items:
                if it[0] == "wait":
                    e.wait_ge(it[1], it[2])
                else:
                    ins = it[1](e)
                    if it[2] is not None:
                        ins.then_inc(it[2], it[3])

        q = self.q

        @block.tensor
        def _(e):
            run(e, q["pe"])

        @block.scalar
        def _(e):
            run(e, q["act"])

        @block.vector
        def _(e):
            run(e, q["dve"])

        @block.gpsimd
        def _(e):
            run(e, q["pool"])

        @block.sync
        def _(e):
            run(e, q["sp"])


def _segments(layer_is_even):
    segs = []
    if layer_is_even:
        for base, gh in ((0, 0), (4096, 8)):
            segs += [("q", gh + i) for i in range(8)]
            segs += [("k", gh + i) for i in range(8)]
            segs += [("v", gh + i) for i in range(8)]
            segs += [("g", gh + i) for i in range(8)]
    else:
        segs += [("q", i) for i in range(8)]
        segs += [("k", i) for i in range(2)]
        segs += [("v", i) for i in range(2)]
        segs += [("g", i) for i in range(8)]
        segs += [("qd", 8 + i) for i in range(8)]
        segs += [("kd", 2 + i) for i in range(2)]
        segs += [("v", 2 + i) for i in range(2)]
        segs += [("g", 8 + i) for i in range(8)]
    return segs


def _plan_a():
    plan = []
    for qt in range(NTT):
        l = []
        for kb in range(NTB):
            e = qt * 512 - kb * 128
            if -1535 <= e <= 1151:
                l.append((kb, [(0, 512, 0, e + OFF_A)]))
        plan.append(l)
    return plan


def _plan_c():
    plan = []
    for qt in range(NTT):
        l = []
        for kb in range(max(0, 4 * qt - 1), min(15, 4 * qt + 4) + 1):
            e = qt * 512 - kb * 128
            l.append((kb, [(0, 512, 0, e + OFF_C)]))
        plan.append(l)
    return plan


def _plan_b():
    plan = []
    for qt in range(NTT):
        l = []
        for j in range(max(0, 4 * qt - 2), min(15, 4 * qt + 5) + 1):
            c0 = (8 * qt - 2 * j - REL_MIN) * 64
            if qt == 0:
                which = 1 if (2 * j + 1) <= 7 else 0
                segs = [(0, 256, which, c0), (256, 512, 0, c0 + 256)]
            elif qt == NTT - 1:
                which = 1 if (2 * j) >= 24 else 0
                segs = [(0, 320, 0, c0), (320, 512, which, c0 + 320)]
            else:
                segs = [(0, 512, 0, c0)]
            l.append((j, segs))
        plan.append(l)
    return plan


def _plan_d():
    return [[(kb, None) for kb in range(NTB)] for _ in range(NTT)]


def build_program(n_layers=DEPTH, layers=None):
    nc = bass.Bass("TRN2", target_bir_lowering=False)

    def din(name, shape, dt=F32):
        return nc.dram_tensor(name, list(shape), dt, kind="ExternalInput").ap()

    x = din("x", [SEQ, D_MODEL])
    gam = din("gam", [DEPTH + 1, 128, D_MODEL])
    w_in_even = din("w_in_even", [2, D_MODEL, 8192])
    w_out_even = din("w_out_even", [2, D_MODEL, D_MODEL])
    w_in_odd = din("w_in_odd", [2, D_MODEL, 5120])
    w_out_odd = din("w_out_odd", [2, D_MODEL, D_MODEL])
    sinks = din("sinks", [2, 128, 128])
    qn = din("qn", [2, 128, 128])
    kn = din("kn", [2, 128, 128])
    bias_a = din("bias_a", [8, 128, W_A])
    bias_c = din("bias_c", [8, 128, W_C])
    strip_b = din("strip_b", [2, 8, 2, 128, W_B])
    rope = din("rope", [2, 128, SEQ])
    rperm_d = din("rperm", [128, 128])
    ident_d = din("ident", [128, 128])
    out = nc.dram_tensor("out", [SEQ, D_MODEL], F32, kind="ExternalOutput").ap()

    hbuf = nc.dram_tensor("hbuf", [SEQ, D_MODEL], F32).ap()
    qs = nc.dram_tensor("qs", [16, 128, SEQ], BF16).ap()
    ks = nc.dram_tensor("ks", [16, 128, SEQ], BF16).ap()
    gs = nc.dram_tensor("gs", [16, 128, SEQ], BF16).ap()
    vs = nc.dram_tensor("vs", [SEQ, 16 * 128], BF16).ap()

    with ExitStack() as stack:
        B = Builder(nc, stack)

        uid = [0]

        def uname(name):
            uid[0] += 1
            return f"{name}_u{uid[0]}"

        def sb(name, shape, dt):
            return stack.enter_context(nc.sbuf_tensor(uname(name), list(shape), dt))

        def ps(name, shape, dt):
            return stack.enter_context(nc.psum_tensor(uname(name), list(shape), dt))

        ybuf = sb("ybuf", [128, NCH, SEQ], BF16)
        ybufs = [Buf(f"ybuf{t}") for t in range(NTT)]
        NW = 2
        GW = 8
        wbuf = [sb(f"wbuf{i}", [128, NCH, GW * 128], BF16) for i in range(NW)]
        wbufs = [Buf(f"wbuf{i}") for i in range(NW)]
        ones_bf = sb("ones_bf", [128, 128], BF16)
        ident = sb("ident", [128, 128], BF16)
        rperm = sb("rperm_s", [128, 128], BF16)
        cbuf = Buf("consts")
        eps_t = sb("eps_t", [128, 1], F32)

        NPS = 8
        psf = [ps(f"psf{i}", [128, 512], F32) for i in range(NPS)]
        psfs = [Buf(f"psf{i}") for i in range(NPS)]
        pst = [psf[6].bitcast(BF16), psf[7].bitcast(BF16)]
        psts = [psfs[6], psfs[7]]

        hsrc_b = Buf("hres")
        qs_b = [Buf(f"qs{i}") for i in range(16)]
        ks_b = [Buf(f"ks{i}") for i in range(16)]
        gs_b = [Buf(f"gs{i}") for i in range(16)]
        vs_b = [Buf(f"vs{i}") for i in range(16)]
        out_b = Buf("out")

        B.op("dve", [lambda e: e.memset(ones_bf[:], 1.0),
                     lambda e: e.memset(eps_t[:], EPS)], writes=[cbuf])
        rp_b = Buf("rperm")
        B.dma("pool", lambda e: e.dma_start(out=rperm[:], in_=rperm_d), rp_b, writes=[rp_b])

        state = {"ps": 0}

        def next_ps():
            i = state["ps"] % NPS
            state["ps"] += 1
            return i

        def phase_scope():
            return ExitStack()

        def rmsnorm_phase(src, src_b, gidx, final):
            with phase_scope() as sc:
                def sbp(name, shape, dt):
                    return sc.enter_context(nc.sbuf_tensor(uname(name), list(shape), dt))
                NHB = 3
                hb = [sbp(f"hb{i}", [128, D_MODEL], F32) for i in range(NHB)]
                hbs = [Buf(f"hb{i}") for i in range(NHB)]
                sq = sbp("sq", [128, D_MODEL], F32)
                sqs = Buf("sq")
                gm = sbp("gm", [128, D_MODEL], F32)
                gms = Buf("gm")
                st = [sbp(f"st{i}", [128, 4], F32) for i in range(NHB)]
                sts = [Buf(f"st{i}") for i in range(NHB)]
                if final:
                    yb = [sbp(f"yo{i}", [128, D_MODEL], F32) for i in range(2)]
                else:
                    yb = [sbp(f"yb{i}", [128, D_MODEL], BF16) for i in range(2)]
                ybs = [Buf(f"yb{i}") for i in range(2)]
                B.dma("sp", lambda e: e.dma_start(out=gm[:], in_=gam[gidx]), gms, writes=[gms])
                def stage_a1(tb):
                    i = tb % NHB
                    B.dma("sp", lambda e, i=i, tb=tb: e.dma_start(out=hb[i][:], in_=src[tb * 128:(tb + 1) * 128, :]),
                          hbs[i], reads=[src_b], writes=[hbs[i]])
                    B.op("act", lambda e, i=i: e.activation(out=sq[:], in_=hb[i][:], func=AF.Square),
                         reads=[hbs[i]], writes=[sqs])
                    B.op("dve", lambda e, i=i: e.reduce_sum(out=st[i][:, 0:1], in_=sq[:], axis=AX.X),
                         reads=[sqs], writes=[sts[i]])

                def stage_a2(tb):
                    i = tb % NHB
                    j = tb % 2
                    B.op("act", lambda e, i=i: e.activation(out=st[i][:, 1:2], in_=st[i][:, 0:1], func=AF.Sqrt,
                                                            scale=1.0 / D_MODEL, bias=eps_t[:, 0:1]),
                         reads=[cbuf], writes=[sts[i]])
                    B.op("dve", lambda e, i=i: e.reciprocal(out=st[i][:, 2:3], in_=st[i][:, 1:2]), writes=[sts[i]])
                    B.op("dve", lambda e, i=i, j=j: e.scalar_tensor_tensor(out=yb[j][:], in0=hb[i][:], scalar=st[i][:, 2:3],
                                                                           in1=gm[:], op0=ALU.mult, op1=ALU.mult),
                         reads=[hbs[i], sts[i], gms], writes=[ybs[j]])

                def stage_b(tb):
                    i = tb % 2
                    if final:
                        B.dma("pool", lambda e, i=i, tb=tb: e.dma_start(out=out[tb * 128:(tb + 1) * 128, :], in_=yb[i][:]),
                              ybs[i], reads=[ybs[i]], accum=[out_b])
                    else:
                        tt = tb // 4
                        for c4 in range(4):
                            pi = (tb * 4 + c4) % 2
                            fns = []
                            for k in range(4):
                                cc = c4 * 4 + k
                                fns.append(lambda e, pi=pi, k=k, cc=cc, i=i: e.transpose(
                                    pst[pi][:, k * 128:(k + 1) * 128], yb[i][:, cc * 128:(cc + 1) * 128], ident[:]))
                            B.op("pe", fns, reads=[ybs[i], idb_], writes=[psts[pi]])
                            dst = ybuf[:, c4 * 4:(c4 + 1) * 4, tb * 128:(tb + 1) * 128]
                            srcp = pst[pi][:, 0:512].rearrange("p (k t) -> p k t", k=4)
                            B.op("act", lambda e, dst=dst, srcp=srcp: e.activation(out=dst, in_=srcp, func=AF.Copy),
                                 reads=[psts[pi]], accum=[ybufs[tt]])

                for tb in range(NTB + 1):
                    if tb < NTB:
                        stage_a1(tb)
                    if tb >= 1:
                        stage_b(tb - 1)
                    if tb < NTB:
                        stage_a2(tb)

        def inproj_phase(w, is_even, li):
            segs = _segments(is_even)
            ngroups = len(segs) // GW
            with phase_scope() as sc:
                def sbp(name, shape, dt):
                    return sc.enter_context(nc.sbuf_tensor(uname(name), list(shape), dt))
                NO = 4
                ot = [sbp(f"ot{i}", [128, 512], BF16) for i in range(NO)]
                ots = [Buf(f"ot{i}") for i in range(NO)]
                oi = [0]
                if not is_even:
                    cs_t = sbp("cs_t", [128, 2, SEQ], F32)
                    cs_b = Buf("cs_t")
                    B.dma("sp", lambda e: e.dma_start(out=cs_t[:], in_=rope.rearrange("a p t -> p a t")), cs_b, writes=[cs_b])
                    gq = sbp("gq", [128, 2, 128], F32)
                    gq_b = Buf("gq")
                    B.dma("sp", lambda e: e.dma_start(out=gq[:, 0, :], in_=qn[li]), gq_b, accum=[gq_b])
                    B.dma("sp", lambda e: e.dma_start(out=gq[:, 1, :], in_=kn[li]), gq_b, accum=[gq_b])
                    tmp = [sbp(f"tmp{i}", [128, 512], F32) for i in range(5)]
                    tmps = [Buf(f"tmp{i}") for i in range(5)]
                    sqh = sbp("sqh", [128, 512], BF16)
                    xhi = sbp("xhi", [128, 512], BF16)
                    xlo = sbp("xlo", [128, 512], BF16)
                    xhib, xlob, sqhb = Buf("xhi"), Buf("xlo"), Buf("sqh")

                NSW = 12
                stg = [sbp(f"stg{i}", [128, NCH - NSW, GW * 128], F32) for i in range(2)]
                stgb = [Buf(f"stg{i}") for i in range(2)]

                def prefetch(g):
                    wi = g % NW
                    si_ = g % 2
                    wv = w[:, g * GW * 128:(g + 1) * GW * 128].rearrange("(cc p) f -> p cc f", p=128)
                    B.dma("pool", lambda e: e.dma_start(out=wbuf[wi][:, 0:NSW, :], in_=wv[:, 0:NSW, :]),
                          wbufs[wi], writes=[wbufs[wi]])
                    B.dma("sp", lambda e: e.dma_start(out=stg[si_][:], in_=wv[:, NSW:NCH, :]),
                          stgb[si_], writes=[stgb[si_]])
                    B.op("pool", lambda e: e.tensor_copy(out=wbuf[wi][:, NSW:NCH, :], in_=stg[si_][:]),
                         reads=[stgb[si_]], accum=[wbufs[wi]])

                prefetch(0)
                for g in range(ngroups):
                    wi = g % NW
                    if g + 1 < ngroups:
                        prefetch(g + 1)
                    gsegs = segs[g * GW:(g + 1) * GW]
                    k = 0
                    while k < GW:
                        typ, idx = gsegs[k]
                        if 'noqd' in DBG and typ in ('qd', 'kd'):
                            typ = typ[0]
                        if typ == "v":
                            n = 1
                            while n < 4 and k + n < GW and gsegs[k + n][0] == "v" and gsegs[k + n][1] == idx + n:
                                n += 1
                            ncol = 128 * n
                            for tb in range(NTB):
                                p = next_ps()
                                fns = [lambda e, p=p, cc=cc, tb=tb, k=k, ncol=ncol, wi=wi: e.matmul(
                                    psf[p][:, 0:ncol], lhsT=ybuf[:, cc, tb * 128:(tb + 1) * 128],
                                    rhs=wbuf[wi][:, cc, k * 128:k * 128 + ncol], start=(cc == 0), stop=(cc == NCH - 1))
                                    for cc in range(NCH)]
                                B.op("pe", fns, reads=[ybufs[tb // 4], wbufs[wi]], writes=[psfs[p]])
                                o = oi[0] % NO
                                oi[0] += 1
                                B.op("dve", lambda e, o=o, p=p, ncol=ncol: e.tensor_copy(out=ot[o][:, 0:ncol], in_=psf[p][:, 0:ncol]),
                                     reads=[psfs[p]], writes=[ots[o]])
                                B.dma("sp", lambda e, o=o, tb=tb, idx=idx, ncol=ncol: e.dma_start(
                                    out=vs[tb * 128:(tb + 1) * 128, idx * 128:idx * 128 + ncol], in_=ot[o][:, 0:ncol]),
                                    ots[o], reads=[ots[o]], accum=[vs_b[idx + j] for j in range(n)])
                            k += n
                            continue
                        for tt in range(NTT):
                            p = next_ps()
                            fns = [lambda e, p=p, cc=cc, tt=tt, k=k, wi=wi: e.matmul(
                                psf[p][:], lhsT=wbuf[wi][:, cc, k * 128:(k + 1) * 128],
                                rhs=ybuf[:, cc, tt * 512:(tt + 1) * 512], start=(cc == 0), stop=(cc == NCH - 1))
                                for cc in range(NCH)]
                            B.op("pe", fns, reads=[ybufs[tt], wbufs[wi]], writes=[psfs[p]])
                            o = oi[0] % NO
                            oi[0] += 1
                            if typ == "q":
                                B.op("dve", lambda e, o=o, p=p: e.tensor_scalar(out=ot[o][:], in0=psf[p][:], scalar1=SCALE,
                                                                               scalar2=None, op0=ALU.mult),
                                     reads=[psfs[p]], writes=[ots[o]])
                                dst, dstb = qs, qs_b
                            elif typ == "k":
                                B.op("act", lambda e, o=o, p=p: e.activation(out=ot[o][:], in_=psf[p][:], func=AF.Copy),
                                     reads=[psfs[p]], writes=[ots[o]])
                                dst, dstb = ks, ks_b
                            elif typ == "g":
                                B.op("act", lambda e, o=o, p=p: e.activation(out=ot[o][:], in_=psf[p][:], func=AF.Silu),
                                     reads=[psfs[p]], writes=[ots[o]])
                                dst, dstb = gs, gs_b
                            else:
                                isq = typ == "qd"
                                gcol = 0 if isq else 1
                                osc = SCALE if isq else 1.0
                                xg, sqx, t1, t2, rs = tmp
                                xgb, sqxb, t1b, t2b, rsb = tmps
                                sqb = sqhb
                                B.op("dve", lambda e, p=p, gcol=gcol: e.tensor_scalar(
                                    out=xg[:], in0=psf[p][:], scalar1=gq[:, gcol, 0:1], scalar2=None, op0=ALU.mult),
                                    reads=[psfs[p], gq_b], writes=[xgb])
                                B.op("act", lambda e, p=p: e.activation(out=sqh[:], in_=psf[p][:], func=AF.Square),
                                     reads=[psfs[p], xgb], writes=[sqb])
                                B.op("dve", lambda e: e.tensor_copy(out=xhi[:], in_=xg[:]), reads=[xgb], writes=[xhib])
                                B.op("dve", lambda e: e.tensor_tensor(out=xlo[:], in0=xg[:], in1=xhi[:], op=ALU.subtract),
                                     reads=[xgb, xhib], writes=[xlob])
                                p2 = next_ps()
                                B.op("pe", lambda e, p2=p2: e.matmul(psf[p2][:], lhsT=ones_bf[:], rhs=sqh[:], start=True, stop=True),
                                     reads=[sqb, cbuf], writes=[psfs[p2]])
                                p3 = next_ps()
                                B.op("pe", [lambda e, p3=p3: e.matmul(psf[p3][:], lhsT=rperm[:], rhs=xhi[:], start=True, stop=False),
                                            lambda e, p3=p3: e.matmul(psf[p3][:], lhsT=rperm[:], rhs=xlo[:], start=False, stop=True)],
                                     reads=[xhib, xlob, rp_b], writes=[psfs[p3]])
                                B.op("dve", lambda e, p2=p2: e.tensor_scalar(out=t2[:], in0=psf[p2][:], scalar1=1.0 / HD,
                                                                             scalar2=EPS, op0=ALU.mult, op1=ALU.add),
                                     reads=[psfs[p2]], writes=[t2b])
                                B.op("act", lambda e: e.activation(out=rs[:], in_=t2[:], func=AF.Ln),
                                     reads=[t2b], writes=[rsb])
                                B.op("act", lambda e: e.activation(out=sqx[:], in_=rs[:], func=AF.Exp, scale=-0.5),
                                     reads=[rsb], writes=[sqxb])
                                B.op("dve", lambda e, tt=tt: e.tensor_tensor(out=t1[:], in0=xg[:], in1=cs_t[:, 0, tt * 512:(tt + 1) * 512],
                                                                            op=ALU.mult),
                                     reads=[xgb, cs_b], writes=[t1b])
                                B.op("dve", lambda e, tt=tt, p3=p3: e.tensor_tensor(out=t2[:], in0=psf[p3][:],
                                                                                   in1=cs_t[:, 1, tt * 512:(tt + 1) * 512], op=ALU.mult),
                                     reads=[psfs[p3], cs_b], writes=[t2b])
                                B.op("dve", lambda e: e.tensor_tensor(out=rs[:], in0=t1[:], in1=t2[:], op=ALU.add),
                                     reads=[t1b, t2b], writes=[rsb])
                                B.op("dve", lambda e, o=o, osc=osc: e.scalar_tensor_tensor(
                                    out=ot[o][:], in0=rs[:], scalar=osc, in1=sqx[:], op0=ALU.mult, op1=ALU.mult),
                                    reads=[rsb, sqxb], writes=[ots[o]])
                                dst, dstb = (qs, qs_b) if isq else (ks, ks_b)
                            B.dma("sp", lambda e, o=o, dst=dst, idx=idx, tt=tt: e.dma_start(
                                out=dst[idx, :, tt * 512:(tt + 1) * 512], in_=ot[o][:]),
                                ots[o], reads=[ots[o]], accum=[dstb[idx]])
                        k += 1

        def attention_phase(is_even, li):
            with phase_scope() as sc:
                def sbp(name, shape, dt):
                    return sc.enter_context(nc.sbuf_tensor(uname(name), list(shape), dt))
                qt_ = [sbp(f"qh{i}", [128, SEQ], BF16) for i in range(2)]
                qtb = [Buf(f"qh{i}") for i in range(2)]
                gt_ = [sbp(f"gh{i}", [128, SEQ], BF16) for i in range(2)]
                gtb = [Buf(f"gh{i}") for i in range(2)]
                kt_ = [sbp(f"kh{i}", [128, SEQ], BF16) for i in range(2)]
                ktb = [Buf(f"kh{i}") for i in range(2)]
                vt_ = [sbp(f"vh{i}", [128, NTB, 128], BF16) for i in range(2)]
                vtb = [Buf(f"vh{i}") for i in range(2)]
                WS = W_A if is_even else W_C
                sS1 = sbp("sS0", [128, WS], F32)
                sS = [sS1, sS1]
                sSb1 = Buf("sS0")
                sSb = [sSb1, sSb1]
                eS = [sbp(f"eS{i}", [128, WS], BF16) for i in range(2)]
                sAb = [Buf(f"eS{i}") for i in range(2)]
                sBb = sAb
                sA = [t[:].rearrange("p (a w) -> p a w", a=1) for t in eS]
                if is_even:
                    sBst = [t[:, 0:2 * W_B].rearrange("p (a w) -> p a w", a=2) for t in sS]
                    sB = [t[:, 0:2 * W_B].rearrange("p (a w) -> p a w", a=2) for t in eS]
                LOOK = 3
                NT = 4
                tt_ = [sbp(f"tT{i}", [128, 512], BF16) for i in range(NT)]
                ttb = [Buf(f"tT{i}") for i in range(NT)]
                NP = LOOK + 2
                pt_ = [sbp(f"pT{i}", [128, 512], BF16) for i in range(NP)]
                ptb = [Buf(f"pT{i}") for i in range(NP)]
                rd_ = [sbp(f"rd{i}", [128, 512], F32) for i in range(2)]
                rdb = [Buf(f"rd{i}") for i in range(2)]
                if not is_even:
                    of_ = [sbp(f"of{i}", [128, 512], F32) for i in range(2)]
                    ofb = [Buf(f"of{i}") for i in range(2)]
                gp_ = [sbp(f"gp{i}", [128, 512], F32) for i in range(2)]
                gpb = [Buf(f"gp{i}") for i in range(2)]
                lt1 = sbp("lt0", [128, 512], F32)
                lt_ = [lt1, lt1]
                ltb1 = Buf("lt0")
                ltb = [ltb1, ltb1]
                if not is_even:
                    sk0 = sbp("sk0", [128, 128], F32)
                    sk = sbp("sk", [128, 128], F32)
                    sk0b = Buf("sk0")
                    skb = Buf("sk")
                    B.dma("sp", lambda e: e.dma_start(out=sk0[:], in_=sinks[li]), sk0b, writes=[sk0b])
                    B.op("act", lambda e: e.activation(out=sk[:], in_=sk0[:], func=AF.Exp), reads=[sk0b], writes=[skb])

                S_IDX = [0, 1, 2, 3]
                O_IDX = [4, 5]
                D_IDX = [6, 7]

                tiles = []
                for gh in range(16):
                    if is_even:
                        kind = "A" if gh < 8 else "B"
                        kvh = gh
                    else:
                        kind = "C" if gh < 8 else "D"
                        kvh = gh // 4 if gh < 8 else 2 + (gh - 8) // 4
                    plan = {"A": _plan_a, "B": _plan_b, "C": _plan_c, "D": _plan_d}[kind]()
                    for qt in range(NTT):
                        l = plan[qt]
                        for n, (kb, segs) in enumerate(l):
                            tiles.append((gh, kind, kvh, qt, kb, segs, n == 0, n == len(l) - 1))

                hinfo = []
                ksl_, prev_kvh = -1, -1
                for gh in range(16):
                    if is_even:
                        kind = "A" if gh < 8 else "B"
                        kvh = gh
                    else:
                        kind = "C" if gh < 8 else "D"
                        kvh = gh // 4 if gh < 8 else 2 + (gh - 8) // 4
                    newkv = kvh != prev_kvh
                    if newkv:
                        ksl_ = (ksl_ + 1) % 2
                        prev_kvh = kvh
                    hinfo.append((kind, kvh, ksl_, newkv))
                hcount = [0] * 16
                for t in tiles:
                    hcount[t[0]] += 1

                def issue_loads(gh):
                    kind, kvh, ksl, newkv = hinfo[gh]
                    hs = gh % 2
                    B.dma("sp", lambda e: e.dma_start(out=qt_[hs][:], in_=qs[gh]), qtb[hs],
                          reads=[qs_b[gh]], writes=[qtb[hs]])
                    if newkv:
                        B.dma("sp", lambda e: e.dma_start(out=kt_[ksl][:], in_=ks[kvh]), ktb[ksl],
                              reads=[ks_b[kvh]], writes=[ktb[ksl]])
                        B.dma("sp", lambda e: e.dma_start(
                            out=vt_[ksl][:], in_=vs[:, kvh * 128:(kvh + 1) * 128].rearrange("(tb p) d -> p tb d", p=128)),
                            vtb[ksl], reads=[vs_b[kvh]], writes=[vtb[ksl]])
                    if kind in ("A", "C"):
                        src_ = bias_a if kind == "A" else bias_c
                        B.dma("sp", lambda e: e.dma_start(out=sS[hs][:], in_=src_[gh % 8]),
                              sSb[hs], writes=[sSb[hs]])
                    elif kind == "B":
                        B.dma("sp", lambda e: e.dma_start(
                            out=sBst[hs], in_=strip_b[li, gh - 8].rearrange("a p w -> p a w")),
                            sSb[hs], writes=[sSb[hs]])
                    B.dma("sp", lambda e: e.dma_start(out=gt_[hs][:], in_=gs[gh]), gtb[hs],
                          reads=[gs_b[gh]], writes=[gtb[hs]])

                def issue_strip_exp(gh):
                    kind = hinfo[gh][0]
                    hs = gh % 2
                    if kind in ("A", "C"):
                        B.op("act", lambda e: e.activation(out=eS[hs][:], in_=sS[hs][:], func=AF.Exp),
                             reads=[sSb[hs]], writes=[sAb[hs]])
                    elif kind == "B":
                        B.op("act", lambda e: e.activation(out=eS[hs][:, 0:2 * W_B], in_=sS[hs][:, 0:2 * W_B], func=AF.Exp),
                             reads=[sSb[hs]], writes=[sAb[hs]])

                pend = []
                deferred = []
                DEFER = 3
                ngen = [0]

                def issue_pv(item):
                    (gh, kind, kvh, qt, kb, segs, first, last, pslot, hslot, kslot, qn_) = item
                    oi_ = O_IDX[qn_ % 2]
                    di_ = D_IDX[qn_ % 2]
                    B.op("pe", [lambda e: e.matmul(psf[oi_][:], lhsT=vt_[kslot][:, kb, :], rhs=pt_[pslot][:],
                                                   start=first, stop=last),
                                lambda e: e.matmul(psf[di_][:], lhsT=ones_bf[:], rhs=pt_[pslot][:],
                                                   start=first, stop=last)],
                         reads=[vtb[kslot], ptb[pslot], cbuf],
                         writes=([psfs[oi_], psfs[di_]] if first else []),
                         accum=([] if first else [psfs[oi_], psfs[di_]]))
                    if last:
                        r = qn_ % 2

                        def epi1():
                            if kind == "C":
                                B.op("dve", lambda e: e.tensor_scalar(out=of_[r][:], in0=psf[di_][:], scalar1=sk[:, gh:gh + 1],
                                                                      scalar2=None, op0=ALU.add),
                                     reads=[psfs[di_], skb], writes=[ofb[r]])
                                B.op("act", lambda e: e.activation(out=lt_[r][:], in_=of_[r][:], func=AF.Ln),
                                     reads=[ofb[r]], writes=[ltb[r]])
                            else:
                                B.op("act", lambda e: e.activation(out=lt_[r][:], in_=psf[di_][:], func=AF.Ln),
                                     reads=[psfs[di_]], writes=[ltb[r]])
                            B.op("act", lambda e: e.activation(out=rd_[r][:], in_=lt_[r][:], func=AF.Exp, scale=-1.0),
                                 reads=[ltb[r]], writes=[rdb[r]])
                            B.op("pool", lambda e: e.tensor_tensor(out=gp_[r][:], in0=gt_[hslot][:, qt * 512:(qt + 1) * 512],
                                                                   in1=rd_[r][:], op=ALU.mult),
                                 reads=[gtb[hslot], rdb[r]], writes=[gpb[r]])

                        def epi2():
                            B.op("dve", lambda e: e.tensor_tensor(out=ybuf[:, gh, qt * 512:(qt + 1) * 512], in0=psf[oi_][:],
                                                                  in1=gp_[r][:], op=ALU.mult),
                                 reads=[psfs[oi_], gpb[r]], accum=[ybufs[qt]])
                        deferred.append((ngen[0] + 1, epi1))
                        deferred.append((ngen[0] + 4, epi2))
                        deferred.sort(key=lambda t: t[0])

                issue_loads(0)
                issue_strip_exp(0)
                qtn = -1
                tih = 0
                prev_gh = -1
                for ti, (gh, kind, kvh, qt, kb, segs, first, last) in enumerate(tiles):
                    if gh != prev_gh:
                        prev_gh = gh
                        tih = 0
                    if tih == LOOK + 6 and gh + 1 < 16:
                        issue_loads(gh + 1)
                    if tih == hcount[gh] - 1 and gh + 1 < 16:
                        issue_strip_exp(gh + 1)
                    tih += 1
                    if first:
                        qtn += 1
                    hs, ksl, qn_ = gh % 2, hinfo[gh][2], qtn
                    si = S_IDX[ti % 4]
                    B.op("pe", lambda e, si=si, ksl=ksl, kb=kb, hs=hs, qt=qt: e.matmul(
                        psf[si][:], lhsT=kt_[ksl][:, kb * 128:(kb + 1) * 128], rhs=qt_[hs][:, qt * 512:(qt + 1) * 512],
                        start=True, stop=True),
                        reads=[ktb[ksl], qtb[hs]], writes=[psfs[si]])
                    pslot = ti % NP
                    if segs is None:
                        B.op("act", lambda e, pslot=pslot, si=si: e.activation(out=pt_[pslot][:], in_=psf[si][:], func=AF.Exp),
                             reads=[psfs[si]], writes=[ptb[pslot]])
                    else:
                        tsl = ti % NT
                        B.op("act", lambda e, tsl=tsl, si=si: e.activation(out=tt_[tsl][:], in_=psf[si][:], func=AF.Exp),
                             reads=[psfs[si]], writes=[ttb[tsl]])
                        fns = []
                        for (c0, c1, which, s0) in segs:
                            if kind == "B":
                                bap = sB[hs][:, which, s0:s0 + (c1 - c0)]
                            else:
                                bap = sA[hs][:, 0, s0:s0 + (c1 - c0)]
                            fns.append(lambda e, tsl=tsl, pslot=pslot, c0=c0, c1=c1, bap=bap: e.tensor_tensor(
                                out=pt_[pslot][:, c0:c1], in0=tt_[tsl][:, c0:c1], in1=bap, op=ALU.mult))
                        B.op("dve", fns, reads=[ttb[tsl], sAb[hs]], writes=[ptb[pslot]])
                    ngen[0] += 1
                    pend.append((gh, kind, kvh, qt, kb, segs, first, last, pslot, hs, ksl, qn_))
                    if len(pend) > LOOK:
                        issue_pv(pend.pop(0))
                    while deferred and deferred[0][0] <= ngen[0]:
                        deferred.pop(0)[1]()
                while pend:
                    issue_pv(pend.pop(0))
                while deferred:
                    deferred.pop(0)[1]()

        def wout_load(w, g2):
            wi = g2 % NW
            B.dma("pool", lambda e: e.dma_start(
                out=wbuf[wi][:], in_=w[:, g2 * GW * 128:(g2 + 1) * GW * 128].rearrange("(cc p) f -> p cc f", p=128)),
                wbufs[wi], writes=[wbufs[wi]])

        def outproj_phase(w, src, src_b):
            with phase_scope() as sc:
                def sbp(name, shape, dt):
                    return sc.enter_context(nc.sbuf_tensor(uname(name), list(shape), dt))
                NH = 3
                hi = [sbp(f"hi{i}", [128, 512], F32) for i in range(NH)]
                hib = [Buf(f"hi{i}") for i in range(NH)]
                ho = [sbp(f"ho{i}", [128, 512], F32) for i in range(NH)]
                hob = [Buf(f"ho{i}") for i in range(NH)]
                new_b = Buf("hres_new")
                n = 0
                for og in range(4):
                    wi = (og // 2) % NW
                    co = (og % 2) * 512
                    for tb in range(NTB):
                        s = n % NH
                        n += 1
                        B.dma("sp", lambda e, s=s, tb=tb, og=og: e.dma_start(
                            out=hi[s][:], in_=src[tb * 128:(tb + 1) * 128, og * 512:(og + 1) * 512]),
                            hib[s], reads=[src_b], writes=[hib[s]])
                        p = next_ps()
                        fns = [lambda e, p=p, fc=fc, tb=tb, wi=wi, co=co: e.matmul(
                            psf[p][:], lhsT=ybuf[:, fc, tb * 128:(tb + 1) * 128], rhs=wbuf[wi][:, fc, co:co + 512],
                            start=(fc == 0), stop=(fc == NCH - 1)) for fc in range(NCH)]
                        B.op("pe", fns, reads=[ybufs[tb // 4], wbufs[wi]], writes=[psfs[p]])
                        B.op("dve", lambda e, s=s, p=p: e.tensor_tensor(out=ho[s][:], in0=psf[p][:], in1=hi[s][:], op=ALU.add),
                             reads=[psfs[p], hib[s]], writes=[hob[s]])
                        B.dma("pool", lambda e, s=s, tb=tb, og=og: e.dma_start(
                            out=hbuf[tb * 128:(tb + 1) * 128, og * 512:(og + 1) * 512], in_=ho[s][:]),
                            hob[s], reads=[hob[s]], accum=[new_b])
                return new_b

        def outproj_norm_phase(src, src_b, gidx, final):
            with phase_scope() as sc:
                def sbp(name, shape, dt):
                    return sc.enter_context(nc.sbuf_tensor(uname(name), list(shape), dt))
                hi = [sbp(f"fhi{i}", [128, D_MODEL], F32) for i in range(2)]
                hib = [Buf(f"fhi{i}") for i in range(2)]
                hn = [sbp(f"fhn{i}", [128, D_MODEL], F32) for i in range(2)]
                hnb = [Buf(f"fhn{i}") for i in range(2)]
                sq = sbp("fsq", [128, D_MODEL], F32)
                sqs = Buf("fsq")
                gm = sbp("fgm", [128, D_MODEL], F32)
                gms = Buf("fgm")
                st = [sbp(f"fst{i}", [128, 4], F32) for i in range(2)]
                sts = [Buf(f"fst{i}") for i in range(2)]
                yb = [sbp(f"fyb{i}", [128, D_MODEL], F32 if final else BF16) for i in range(2)]
                ybs = [Buf(f"fyb{i}") for i in range(2)]
                ytb = [Buf(f"ybtb{i}") for i in range(NTB)]
                new_b = Buf("hres_new")
                B.dma("sp", lambda e: e.dma_start(out=gm[:], in_=gam[gidx]), gms, writes=[gms])
                pcount = [0]

                def main(tb):
                    b = tb % 2
                    B.dma("sp", lambda e: e.dma_start(out=hi[b][:], in_=src[tb * 128:(tb + 1) * 128, :]),
                          hib[b], reads=[src_b], writes=[hib[b]])
                    for og in range(4):
                        wi = (og // 2) % NW
                        co = (og % 2) * 512
                        p = pcount[0] % 6
                        pcount[0] += 1
                        fns = [lambda e, p=p, fc=fc, wi=wi, co=co: e.matmul(
                            psf[p][:], lhsT=ybuf[:, fc, tb * 128:(tb + 1) * 128], rhs=wbuf[wi][:, fc, co:co + 512],
                            start=(fc == 0), stop=(fc == NCH - 1)) for fc in range(NCH)]
                        B.op("pe", fns, reads=[ytb[tb], wbufs[wi]], writes=[psfs[p]])
                        B.op("dve", lambda e, p=p, og=og: e.tensor_tensor(
                            out=hn[b][:, og * 512:(og + 1) * 512], in0=psf[p][:], in1=hi[b][:, og * 512:(og + 1) * 512], op=ALU.add),
                            reads=[psfs[p], hib[b]], writes=([hnb[b]] if og == 0 else []), accum=([] if og == 0 else [hnb[b]]))
                    if not final:
                        B.dma("sp", lambda e: e.dma_start(out=hbuf[tb * 128:(tb + 1) * 128, :], in_=hn[b][:]),
                              hnb[b], reads=[hnb[b]], accum=[new_b])

                def main_b(tb):
                    b = tb % 2
                    B.op("act", lambda e: e.activation(out=sq[:], in_=hn[b][:], func=AF.Square),
                         reads=[hnb[b]], writes=[sqs])
                    B.op("dve", lambda e: e.reduce_sum(out=st[b][:, 0:1], in_=sq[:], axis=AX.X),
                         reads=[sqs], writes=[sts[b]])
                    B.op("act", lambda e: e.activation(out=st[b][:, 1:2], in_=st[b][:, 0:1], func=AF.Sqrt,
                                                       scale=1.0 / D_MODEL, bias=eps_t[:, 0:1]),
                         reads=[cbuf], writes=[sts[b]])
                    B.op("dve", lambda e: e.reciprocal(out=st[b][:, 2:3], in_=st[b][:, 1:2]), writes=[sts[b]])
                    B.op("dve", lambda e: e.scalar_tensor_tensor(out=yb[b][:], in0=hn[b][:], scalar=st[b][:, 2:3],
                                                                 in1=gm[:], op0=ALU.mult, op1=ALU.mult),
                         reads=[hnb[b], sts[b], gms], writes=[ybs[b]])

                def tail(tb):
                    b = tb % 2
                    if final:
                        B.dma("sp", lambda e: e.dma_start(out=out[tb * 128:(tb + 1) * 128, :], in_=yb[b][:]),
                              ybs[b], reads=[ybs[b]], accum=[out_b])
                        return
                    for c4 in range(4):
                        pi = (tb * 4 + c4) % 2
                        fns = []
                        for k in range(4):
                            cc = c4 * 4 + k
                            fns.append(lambda e, pi=pi, k=k, cc=cc: e.transpose(
                                pst[pi][:, k * 128:(k + 1) * 128], yb[b][:, cc * 128:(cc + 1) * 128], ident[:]))
                        B.op("pe", fns, reads=[ybs[b], idb_], writes=[psts[pi]])
                        dst = ybuf[:, c4 * 4:(c4 + 1) * 4, tb * 128:(tb + 1) * 128]
                        srcp = pst[pi][:, 0:512].rearrange("p (k t) -> p k t", k=4)
                        B.op("act", lambda e, dst=dst, srcp=srcp: e.activation(out=dst, in_=srcp, func=AF.Copy),
                             reads=[psts[pi]], writes=([ytb[tb]] if c4 == 0 else []), accum=([] if c4 == 0 else [ytb[tb]]))

                for tb in range(NTB + 1):
                    if tb < NTB:
                        main(tb)
                    if tb >= 1:
                        tail(tb - 1)
                    if tb < NTB:
                        main_b(tb)
                return new_b

        idb_ = Buf("ident")
        B.dma("pool", lambda e: e.dma_start(out=ident[:], in_=ident_d), idb_, writes=[idb_])
        B.barrier()

        src, src_b = x, Buf("x")
        layer_list = list(layers if layers is not None else range(n_layers))
        first_layer = True
        for pos, layer in enumerate(layer_list):
            is_even = layer % 2 == 0
            li = layer // 2
            w_in = (w_in_even if is_even else w_in_odd)[li]
            w_out = (w_out_even if is_even else w_out_odd)[li]
            if first_layer:
                rmsnorm_phase(src, src_b, layer, final=False)
                B.barrier()
                first_layer = False
            inproj_phase(w_in, is_even, li)
            B.barrier()
            for g2 in range(2):
                wout_load(w_out, g2)
            if 'noattn' not in DBG:
                attention_phase(is_even, li)
            B.barrier()
            is_last = pos == len(layer_list) - 1
            new_b = outproj_norm_phase(src, src_b, DEPTH if is_last else layer_list[pos + 1], final=is_last)
            B.barrier()
            src, src_b = hbuf, new_b
        B.final_wait("sp", [out_b])

        with nc.Block() as block:
            B.flush(block)
    return nc


_CACHE = {}


def _prep_inputs(x, norm_g, final_g, w_in_even, w_out_even, rpb_b, w_in_odd, w_out_odd,
                 sinks_c, qnorm_d, knorm_d):
    f = lambda a: np.ascontiguousarray(np.asarray(a, dtype=np.float32))
    gam = np.concatenate([f(norm_g), f(final_g)[None, :]], 0)
    gam = np.ascontiguousarray(np.broadcast_to(gam[:, None, :], (DEPTH + 1, 128, D_MODEL)))
    common = {
        "gam": gam,
        "w_in_even": f(w_in_even), "w_out_even": f(w_out_even),
        "w_in_odd": f(w_in_odd), "w_out_odd": f(w_out_odd),
        "sinks": np.ascontiguousarray(np.broadcast_to(np.tile(f(sinks_c), (1, 16))[:, None, :], (2, 128, 128))),
        "qn": np.ascontiguousarray(np.broadcast_to(f(qnorm_d)[:, :, None], (2, 128, 128))),
        "kn": np.ascontiguousarray(np.broadcast_to(f(knorm_d)[:, :, None], (2, 128, 128))),
        "bias_a": make_bias_a(), "bias_c": make_bias_c(),
        "strip_b": make_strip_b(f(rpb_b)),
        "rope": make_rope(), "rperm": make_rperm(), "ident": np.eye(128, dtype=np.float32),
    }
    return common


def kernel(x, norm_g, final_g, w_in_even, w_out_even, rpb_b, w_in_odd, w_out_odd,
           sinks_c, qnorm_d, knorm_d):
    x = np.asarray(x, dtype=np.float32)
    common = _prep_inputs(x, norm_g, final_g, w_in_even, w_out_even, rpb_b, w_in_odd, w_out_odd,
                          sinks_c, qnorm_d, knorm_d)
    if "nc" not in _CACHE:
        _CACHE["nc"] = build_program(DEPTH)
    nc = _CACHE["nc"]
    in_maps = [dict(common, x=np.ascontiguousarray(x[b])) for b in range(BATCH)]
    res = run_bass_kernel_spmd(nc, in_maps, core_ids=list(range(BATCH)))
    return np.stack([np.asarray(res.results[b]["out"], dtype=np.float32) for b in range(BATCH)], 0)
```
